# Optimizing a Trainium2 kernel written in Bass

```python
import math
import jax
import jax.numpy as jnp
from jax import lax
import numpy as np

D_MODEL = 1024
BATCH = 32
SEQ = 2048
DEPTH = 2

HEAD_DIM = 64
SSD_WIDTH = 3 * D_MODEL // 8
LRU_WIDTH = D_MODEL // 4
FOX_WIDTH = D_MODEL - SSD_WIDTH - LRU_WIDTH
SSD_HEADS = SSD_WIDTH // HEAD_DIM
SSD_GROUPS = 2
SSD_STATE = 128
SSD_CONV = 4
SSD_CONV_DIM = SSD_WIDTH + 2 * SSD_GROUPS * SSD_STATE
SSD_CHUNK = 128
LRU_BLOCKS = 4
LRU_BLOCK = LRU_WIDTH // LRU_BLOCKS
LRU_CONV = 4
LRU_C = 8.0
FOX_HEADS = FOX_WIDTH // HEAD_DIM
Q_BLOCK = 128
D_FF = ((8 * D_MODEL + 3 * 256 - 1) // (3 * 256)) * 256
PLE_DIM = 256
IN_SPLITS = (SSD_WIDTH, SSD_CONV_DIM, SSD_HEADS, LRU_WIDTH, LRU_WIDTH,
             FOX_WIDTH, FOX_WIDTH, FOX_WIDTH, FOX_HEADS)
IN_COLS = sum(IN_SPLITS)
EPS = 1e-6

kernel_name = "hymba_style_ssd_rglru_fox_trunk"


def _rmsnorm(x, g):
    xf = x.astype(jnp.float32)
    y = xf * lax.rsqrt(jnp.mean(xf * xf, axis=-1, keepdims=True) + EPS)
    return (y * g.astype(jnp.float32)).astype(x.dtype)


def _causal_dwconv(x, w, b):
    k, c = w.shape
    y = lax.conv_general_dilated(
        x, w[:, None, :].astype(x.dtype), window_strides=(1,),
        padding=[(k - 1, 0)], dimension_numbers=("NWC", "WIO", "NWC"),
        feature_group_count=c)
    return y + b.astype(x.dtype)


def _ssd(xs, dt, a, bm, cm, d_skip):
    b, s, h, p = xs.shape
    g, n = bm.shape[2], bm.shape[3]
    nc = s // SSD_CHUNK
    rep = h // g
    bh = jnp.repeat(bm, rep, axis=2)
    chh = jnp.repeat(cm, rep, axis=2)
    xdt = xs * dt[..., None]
    adt = a * dt
    chunk = lambda t: t.reshape((b, nc, SSD_CHUNK) + t.shape[2:])
    xc, bc, cc, ac = chunk(xdt), chunk(bh), chunk(chh), chunk(adt)
    acs = jnp.cumsum(ac, axis=2)
    seg = acs[:, :, :, None, :] - acs[:, :, None, :, :]
    causal = jnp.tril(jnp.ones((SSD_CHUNK, SSD_CHUNK), bool))
    lmat = jnp.exp(jnp.where(causal[None, None, :, :, None], seg, -jnp.inf))
    scores = jnp.einsum("bclhn,bcshn->bclsh", cc, bc) * lmat
    y_diag = jnp.einsum("bclsh,bcshp->bclhp", scores, xc)
    decay_s = jnp.exp(acs[:, :, -1:, :] - acs)
    states = jnp.einsum("bclhn,bclh,bclhp->bchpn", bc, decay_s, xc)
    chunk_decay = jnp.exp(acs[:, :, -1, :])

    def step(carry, inp):
        st, dec = inp
        return carry * dec[..., None, None] + st, carry

    init = jnp.zeros((b, h, p, n), states.dtype)
    _, prev = lax.scan(step, init, (jnp.moveaxis(states, 1, 0),
                                    jnp.moveaxis(chunk_decay, 1, 0).astype(states.dtype)))
    prev = jnp.moveaxis(prev, 0, 1)
    y_off = jnp.einsum("bclhn,bchpn,bclh->bclhp", cc, prev, jnp.exp(acs))
    y = (y_diag + y_off).reshape(b, s, h, p) + xs * d_skip[:, None]
    return y.astype(xs.dtype)


def _rglru(x, w_a, b_a, w_x, b_x, lam):
    b, s, w = x.shape
    xb = x.reshape(b, s, LRU_BLOCKS, LRU_BLOCK)
    r = jax.nn.sigmoid(jnp.einsum("bsgi,gij->bsgj", xb, w_a).reshape(b, s, w) + b_a)
    i = jax.nn.sigmoid(jnp.einsum("bsgi,gij->bsgj", xb, w_x).reshape(b, s, w) + b_x)
    log_a = -LRU_C * r.astype(jnp.float32) * jax.nn.softplus(-lam.astype(jnp.float32))
    a = jnp.exp(log_a)
    mult = jnp.sqrt(-jnp.expm1(2.0 * log_a))
    u = mult * (i * x).astype(jnp.float32)

    def comb(lhs, rhs):
        a1, b1 = lhs
        a2, b2 = rhs
        return a1 * a2, a2 * b1 + b2

    _, hseq = lax.associative_scan(comb, (a, u), axis=1)
    return hseq.astype(x.dtype)


def _forgetting_attention(q, k, v, log_f):
    s, e = q.shape[1], q.shape[3]
    cum = jnp.cumsum(log_f, axis=-1)
    scale = e ** -0.5
    outs = []
    for blk in range(s // Q_BLOCK):
        q0, q1 = blk * Q_BLOCK, (blk + 1) * Q_BLOCK
        logits = jnp.einsum("bqhe,bkhe->bhqk", q[:, q0:q1], k[:, :q1]).astype(jnp.float32) * scale
        logits = logits + cum[:, :, q0:q1, None] - cum[:, :, None, :q1]
        mask = (q0 + jnp.arange(Q_BLOCK))[:, None] >= jnp.arange(q1)[None, :]
        logits = jnp.where(mask[None, None], logits, -jnp.inf)
        probs = jax.nn.softmax(logits, axis=-1).astype(v.dtype)
        outs.append(jnp.einsum("bhqk,bkhe->bqhe", probs, v[:, :q1]))
    return jnp.concatenate(outs, axis=1)


def _mixer(u, w_in, ssd_conv_w, ssd_conv_b, ssd_dt_bias, ssd_a_log, ssd_d, ssd_norm_g,
           lru_conv_w, lru_conv_b, lru_w_a, lru_b_a, lru_w_x, lru_b_x, lru_lambda, lru_norm_g,
           fox_b_f, fox_norm_g, w_out):
    b, s, _ = u.shape
    proj = u @ w_in
    offs = [int(o) for o in np.cumsum(IN_SPLITS)[:-1]]
    (z, xbc, dt_raw, lru_x, lru_gate, fq, fk, fv, f_raw) = jnp.split(proj, offs, axis=-1)

    xbc = jax.nn.silu(_causal_dwconv(xbc, ssd_conv_w, ssd_conv_b))
    xs, bm, cm = jnp.split(xbc, [SSD_WIDTH, SSD_WIDTH + SSD_GROUPS * SSD_STATE], axis=-1)
    xs = xs.reshape(b, s, SSD_HEADS, HEAD_DIM)
    bm = bm.reshape(b, s, SSD_GROUPS, SSD_STATE)
    cm = cm.reshape(b, s, SSD_GROUPS, SSD_STATE)
    dt = jax.nn.softplus(dt_raw.astype(jnp.float32) + ssd_dt_bias.astype(jnp.float32))
    a = -jnp.exp(ssd_a_log.astype(jnp.float32))
    y_ssd = _ssd(xs, dt, a, bm, cm, ssd_d).reshape(b, s, SSD_WIDTH)
    y_ssd = _rmsnorm(y_ssd * jax.nn.silu(z), ssd_norm_g)

    xl = _causal_dwconv(lru_x, lru_conv_w, lru_conv_b)
    hl = _rglru(xl, lru_w_a, lru_b_a, lru_w_x, lru_b_x, lru_lambda)
    y_lru = _rmsnorm(hl * jax.nn.gelu(lru_gate), lru_norm_g)

    q = fq.reshape(b, s, FOX_HEADS, HEAD_DIM)
    k = fk.reshape(b, s, FOX_HEADS, HEAD_DIM)
    v = fv.reshape(b, s, FOX_HEADS, HEAD_DIM)
    log_f = jnp.transpose(jax.nn.log_sigmoid(f_raw.astype(jnp.float32) + fox_b_f.astype(jnp.float32)), (0, 2, 1))
    y_fox = _forgetting_attention(q, k, v, log_f).reshape(b, s, FOX_WIDTH)
    y_fox = _rmsnorm(y_fox, fox_norm_g)

    return jnp.concatenate([y_ssd, y_lru, y_fox], axis=-1) @ w_out


def setup_inputs(seed: int = 0) -> dict:
    key = jax.random.key(seed)
    ks = jax.random.split(key, 32)
    f32 = jnp.float32
    L = DEPTH
    nrm = lambda k, shape, scale: jax.random.normal(k, shape, f32) * scale
    gain = lambda k, shape: 1.0 + 0.02 * jax.random.normal(k, shape, f32)
    dt0 = jnp.exp(jax.random.uniform(ks[6], (L, SSD_HEADS), f32, math.log(1e-3), math.log(1e-1)))
    lam_s = jax.random.uniform(ks[16], (L, LRU_WIDTH), f32, 0.9, 0.999) ** (1.0 / LRU_C)
    return {
        "x": jax.random.normal(ks[0], (BATCH, SEQ, D_MODEL), f32),
        "p": jax.random.normal(ks[1], (L, BATCH, SEQ, PLE_DIM), f32),
        "norm1_g": gain(ks[2], (L, D_MODEL)),
        "w_in": nrm(ks[3], (L, D_MODEL, IN_COLS), D_MODEL ** -0.5),
        "ssd_conv_w": nrm(ks[4], (L, SSD_CONV, SSD_CONV_DIM), SSD_CONV ** -0.5),
        "ssd_conv_b": nrm(ks[5], (L, SSD_CONV_DIM), 0.02),
        "ssd_dt_bias": dt0 + jnp.log(-jnp.expm1(-dt0)),
        "ssd_a_log": jnp.log(jax.random.uniform(ks[7], (L, SSD_HEADS), f32, 1.0, 16.0)),
        "ssd_d": gain(ks[8], (L, SSD_HEADS)),
        "ssd_norm_g": gain(ks[9], (L, SSD_WIDTH)),
        "lru_conv_w": nrm(ks[10], (L, LRU_CONV, LRU_WIDTH), LRU_CONV ** -0.5),
        "lru_conv_b": nrm(ks[11], (L, LRU_WIDTH), 0.02),
        "lru_w_a": nrm(ks[12], (L, LRU_BLOCKS, LRU_BLOCK, LRU_BLOCK), LRU_BLOCK ** -0.5),
        "lru_b_a": nrm(ks[13], (L, LRU_WIDTH), 0.02),
        "lru_w_x": nrm(ks[14], (L, LRU_BLOCKS, LRU_BLOCK, LRU_BLOCK), LRU_BLOCK ** -0.5),
        "lru_b_x": nrm(ks[15], (L, LRU_WIDTH), 0.02),
        "lru_lambda": jnp.log(lam_s) - jnp.log1p(-lam_s),
        "lru_norm_g": gain(ks[17], (L, LRU_WIDTH)),
        "fox_b_f": 3.0 + nrm(ks[18], (L, FOX_HEADS), 0.1),
        "fox_norm_g": gain(ks[19], (L, FOX_WIDTH)),
        "w_out": nrm(ks[20], (L, D_MODEL, D_MODEL), D_MODEL ** -0.5),
        "norm2_g": gain(ks[21], (L, D_MODEL)),
        "w_gate": nrm(ks[22], (L, D_MODEL, D_FF), D_MODEL ** -0.5),
        "w_up": nrm(ks[23], (L, D_MODEL, D_FF), D_MODEL ** -0.5),
        "w_down": nrm(ks[24], (L, D_FF, D_MODEL), D_FF ** -0.5),
        "norm3_g": gain(ks[25], (L, D_MODEL)),
        "w_ple_gate": nrm(ks[26], (L, D_MODEL, D_MODEL), D_MODEL ** -0.5),
        "b_ple_gate": nrm(ks[27], (L, D_MODEL), 0.02),
        "w_ple_proj": nrm(ks[28], (L, PLE_DIM, D_MODEL), PLE_DIM ** -0.5),
        "final_norm_g": gain(ks[29], (D_MODEL,)),
    }


def reference(x, p, norm1_g, w_in, ssd_conv_w, ssd_conv_b, ssd_dt_bias, ssd_a_log, ssd_d,
              ssd_norm_g, lru_conv_w, lru_conv_b, lru_w_a, lru_b_a, lru_w_x, lru_b_x,
              lru_lambda, lru_norm_g, fox_b_f, fox_norm_g, w_out, norm2_g, w_gate, w_up,
              w_down, norm3_g, w_ple_gate, b_ple_gate, w_ple_proj, final_norm_g):
    h = x
    for i in range(DEPTH):
        u = _rmsnorm(h, norm1_g[i])
        h = h + _mixer(u, w_in[i], ssd_conv_w[i], ssd_conv_b[i], ssd_dt_bias[i], ssd_a_log[i],
                       ssd_d[i], ssd_norm_g[i], lru_conv_w[i], lru_conv_b[i], lru_w_a[i],
                       lru_b_a[i], lru_w_x[i], lru_b_x[i], lru_lambda[i], lru_norm_g[i],
                       fox_b_f[i], fox_norm_g[i], w_out[i])
        u = _rmsnorm(h, norm2_g[i])
        h = h + (jax.nn.silu(u @ w_gate[i]) * (u @ w_up[i])) @ w_down[i]
        u = _rmsnorm(h, norm3_g[i])
        gate = jax.nn.sigmoid(u @ w_ple_gate[i] + b_ple_gate[i])
        h = h + gate * (p[i] @ w_ple_proj[i])
    return _rmsnorm(h, final_norm_g)
```

```python
import contextlib
import numpy as np
import concourse.bass as bass
import concourse.mybir as mybir
from concourse.bass_utils import run_bass_kernel_spmd

F32 = mybir.dt.float32
BF16 = mybir.dt.bfloat16
AF = mybir.ActivationFunctionType
ALU = mybir.AluOpType

ENGS = ["tensor", "vector", "scalar", "gpsimd", "sync"]
SEM_CHUNK = 3000

D_MODEL = 1024
D_FF = 2816
IN_COLS = 2956
OZ, OXS, OB, OC, OLX, OLG, OQ, OK_, OV, ODT, OFR = 0, 384, 768, 1024, 1280, 1536, 1792, 2176, 2560, 2944, 2950
G1, G2, G3, SCW, SCB, DSK, SNG, LCW, LCB, LBA, LBX, LAM, LNG, FNG, BPG, DTB, ALOG, FBF, NPV = (
    0, 8, 16, 24, 52, 59, 62, 65, 73, 75, 77, 79, 81, 83, 86, 94, 100, 106, 112)
EPS = 1e-6
TB = 256


class Buf:
    __slots__ = ("name", "writer", "readers")

    def __init__(self, name=""):
        self.name = name
        self.writer = None
        self.readers = []


class Sched:
    def __init__(self, nc):
        self.nc = nc
        self.prog = {e: [] for e in ENGS}
        self.clock = {e: {} for e in ENGS}
        self.dma_count = []
        self.dma_last = []
        self.out_stamps = []
        self.pool = {}
        self.pool_i = {}

    def new_dma_sem(self):
        self.dma_count.append(0)
        self.dma_last.append(None)
        return len(self.dma_count) - 1

    def pool_sem(self, q, n=28):
        if q not in self.pool:
            self.pool[q] = [self.new_dma_sem() for _ in range(n)]
            self.pool_i[q] = 0
        k = self.pool[q][self.pool_i[q] % n]
        self.pool_i[q] += 1
        return k

    def _need(self, eng, stamp, waits):
        if stamp is None:
            return
        kind, key, val, snap = stamp
        if kind == "c" and key == eng and eng == "tensor":
            return
        ck = self.clock[eng]
        k = (kind, key)
        if ck.get(k, -1) >= val:
            return
        waits.append((kind, key, val))
        if kind == "c":
            self.prog[key][val]["inc"] = True
        for kk, vv in snap.items():
            if ck.get(kk, -1) < vv:
                ck[kk] = vv
        ck[k] = val

    enabled = True
    dead = False

    def op(self, eng, fn, reads=(), writes=(), dma_sem=None, is_out=False):
        if not self.enabled or self.dead:
            return None
        waits = []
        for b in reads:
            self._need(eng, b.writer, waits)
        for b in writes:
            self._need(eng, b.writer, waits)
            for r in b.readers:
                self._need(eng, r, waits)
        if dma_sem is not None:
            self._need(eng, self.dma_last[dma_sem], waits)
        idx = len(self.prog[eng])
        ent = {"waits": waits, "fn": fn, "inc": False, "dma": None}
        self.prog[eng].append(ent)
        snap = dict(self.clock[eng])
        if dma_sem is None:
            stamp = ("c", eng, idx, snap)
        else:
            self.dma_count[dma_sem] += 1
            val = 16 * self.dma_count[dma_sem]
            ent["dma"] = (dma_sem, val)
            stamp = ("d", dma_sem, val, snap)
            self.dma_last[dma_sem] = stamp
            if is_out:
                self.out_stamps.append(stamp)
        for b in reads:
            b.readers.append(stamp)
        for b in writes:
            b.writer = stamp
            b.readers = []
        return stamp

    def emit(self, final_eng="sync"):
        nc = self.nc
        waits = []
        for st in self.out_stamps:
            self._need(final_eng, st, waits)
        self.prog[final_eng].append({"waits": waits, "fn": None, "inc": False, "dma": None})
        with contextlib.ExitStack() as es:
            csem, rank = {}, {}
            for e in ENGS:
                r = 0
                rank[e] = {}
                for i, ent in enumerate(self.prog[e]):
                    if ent["inc"]:
                        rank[e][i] = r
                        r += 1
                nsem = (r + SEM_CHUNK - 1) // SEM_CHUNK
                csem[e] = [es.enter_context(nc.semaphore(f"c_{e}_{k}")) for k in range(nsem)]
            dsem = [es.enter_context(nc.semaphore(f"d_{k}")) for k in range(len(self.dma_count))]
            block = es.enter_context(nc.Block())

            def mk(e):
                def body(engh):
                    for i, ent in enumerate(self.prog[e]):
                        for (kind, key, val) in ent["waits"]:
                            if kind == "c":
                                r = rank[key][val]
                                engh.wait_ge(csem[key][r // SEM_CHUNK], r % SEM_CHUNK + 1)
                            else:
                                engh.wait_ge(dsem[key], val)
                        if ent["fn"] is None:
                            continue
                        ins = ent["fn"](engh)
                        if ent["dma"] is not None:
                            ins.then_inc(dsem[ent["dma"][0]], 16)
                        elif ent["inc"]:
                            r = rank[e][i]
                            ins.then_inc(csem[e][r // SEM_CHUNK], 1)
                return body

            for e in ENGS:
                if self.prog[e]:
                    getattr(block, e)(mk(e))


class Tl:
    __slots__ = ("ap", "b", "subs")

    def __init__(self, ap, b):
        self.ap = ap
        self.b = b
        self.subs = None

    def split(self, n):
        self.subs = [Buf() for _ in range(n)]
        for sb_ in self.subs:
            sb_.readers = list(self.b.readers)
        return self

    def R(self, i):
        return [self.b, self.subs[i]]


class Arena:
    def __init__(self, tensor_bf16, nbytes):
        self.t = tensor_bf16
        self.n = nbytes
        self.live = []

    def view(self, off, shape, dt, name=""):
        n = 1
        for s in shape:
            n *= s
        size = n * (4 if dt == F32 else 2)
        assert off % 4 == 0 and off + size <= self.n, (name, off, size, self.n)
        ap = self.t[:, off // 2:(off + size) // 2]
        if dt == F32:
            ap = ap.bitcast(F32)
        if len(shape) == 2:
            ap = ap.rearrange("p (a b) -> p a b", a=shape[0])
        elif len(shape) == 3:
            ap = ap.rearrange("p (a b c) -> p a b c", a=shape[0], b=shape[1])
        buf = Buf(name)
        newlive = []
        for (s, e, b) in self.live:
            if s < off + size and off < e:
                if b.writer is not None:
                    buf.readers.append(b.writer)
                buf.readers.extend(b.readers)
                if not (s >= off and e <= off + size):
                    newlive.append((s, e, b))
            else:
                newlive.append((s, e, b))
        newlive.append((off, off + size, buf))
        self.live = newlive
        return Tl(ap, buf)


class Bump:
    def __init__(self, arena, base):
        self.a = arena
        self.base = base
        self.off = base

    def reset(self):
        self.off = self.base

    def get(self, shape, dt, name=""):
        n = 1
        for s in shape:
            n *= s
        size = (n * (4 if dt == F32 else 2) + 31) // 32 * 32
        t = self.a.view(self.off, shape, dt, name)
        self.off += size
        return t


def build(NSEQ, NT, DEPTH):
    NBLK = NT // TB
    NT128 = NT // 128
    nc = bass.Bass("TRN2", target_bir_lowering=False)
    din = lambda name, shape: nc.dram_tensor(name, shape, F32, kind="ExternalInput").ap()
    x_d = din("x", [NSEQ, NT, D_MODEL])
    p_d = din("p", [DEPTH, NSEQ, NT, 256])
    win_d = din("w_in", [DEPTH, D_MODEL, IN_COLS])
    wout_d = din("w_out", [DEPTH, D_MODEL, D_MODEL])
    bd_d = din("bd", [DEPTH, 128, 4 * 128])
    wg_d = din("w_gate", [DEPTH, D_MODEL, D_FF])
    wu_d = din("w_up", [DEPTH, D_MODEL, D_FF])
    wd_d = din("w_down", [DEPTH, D_FF, D_MODEL])
    wpg_d = din("w_pg", [DEPTH, D_MODEL, D_MODEL])
    wpp_d = din("w_pp", [DEPTH, 256, D_MODEL])
    pv_d = din("pv", [128, DEPTH * NPV + 8])
    cf_d = din("cf", [128, 3 * 128])
    cb_d = din("cb", [128, 3 * 128])
    sel_d = din("sel", [6, 6 * 128])
    y_d = nc.dram_tensor("y", [NSEQ, NT, D_MODEL], F32, kind="ExternalOutput").ap()
    if KDUMP:
        dbg_d = nc.dram_tensor("dbg", [128, 8192], F32, kind="ExternalOutput").ap()

    es = contextlib.ExitStack()
    with es:
        sb = lambda name, shape, dt: es.enter_context(nc.sbuf_tensor("s_" + name, shape, dt))
        S = Sched(nc)
        GF = DEPTH * NPV

        def static(name, shape, dt):
            return Tl(sb(name, shape, dt)[:], Buf(name))

        hT = sb("hT", [128, 8, NT], F32)
        hTb = [Buf(f"hT{i}") for i in range(NBLK)]
        cf = static("cf", [128, 3, 128], F32)
        cb = static("cb", [128, 3, 128], BF16)
        pv = static("pv", [128, DEPTH * NPV + 8], F32)
        dv = static("dv", [128, DEPTH, 8], F32)
        assert TB == 256
        uT_f = static("uT", [128, 1024], F32)
        ycat_f = static("ycat", [128, 1024], F32)
        uT = Tl(uT_f.ap.bitcast(BF16).rearrange("p (c t) -> p c t", c=8), uT_f.b)
        ycat = Tl(ycat_f.ap.bitcast(BF16).rearrange("p (c t) -> p c t", c=8), ycat_f.b)
        sq = static("sq", [128, 8, TB], BF16)
        rs0 = static("rs0", [128, TB], F32)
        rs1 = static("rs1", [128, TB], F32)
        ARENA_BYTES = 125 * 1024
        arena_t = sb("arena", [128, ARENA_BYTES // 2], BF16)
        A = Arena(arena_t, ARENA_BYTES)
        ps_t = [es.enter_context(nc.psum_tensor(f"ps{i}", [128, 512], F32)) for i in range(7)]
        psb_t = es.enter_context(nc.psum_tensor("psb", [128, 1024], BF16))
        pbuf = [[Buf(f"ps{i}_{h}") for h in range(2)] for i in range(7)]
        psbb = Buf("psb")

        def PS(bank, c0, c1, p0=0, p1=128):
            bs = [pbuf[bank][0], pbuf[bank][1]]
            return ps_t[bank][p0:p1, c0:c1], bs

        ident_f = cf.ap[:, 0, :]
        tri_f = cf.ap[:, 1, :]
        ones_f = cf.ap[:, 2, :]
        ident_b = cb.ap[:, 0, :]
        ones_b = cb.ap[:, 1, :]
        mneg_b = cb.ap[:, 2, :]

        def bl(xs):
            out = []
            for x_ in xs:
                if isinstance(x_, Tl):
                    out.append(x_.b)
                elif isinstance(x_, (list, tuple)):
                    out.extend(bl(x_))
                elif x_ is not None:
                    out.append(x_)
            return out

        def mm(out, lhsT, rhs, start, stop, r, w):
            S.op("tensor", lambda e: e.matmul(out, lhsT=lhsT, rhs=rhs, start=start, stop=stop), bl(r), bl(w))

        def tr(out, in_, ident, r, w):
            S.op("tensor", lambda e: e.transpose(out, in_, ident), bl(r), bl(w))

        def act(out, in_, func, r, w, bias=None, scale=None):
            kw = {}
            if bias is not None:
                kw["bias"] = bias
            if scale is not None:
                kw["scale"] = scale
            S.op("scalar", lambda e: e.activation(out=out, in_=in_, func=func, **kw), bl(r), bl(w))

        def tt(eng, out, in0, in1, op, r, w):
            S.op(eng, lambda e: e.tensor_tensor(out=out, in0=in0, in1=in1, op=op), bl(r), bl(w))

        def ts(eng, out, in0, s1, s2, op0, op1, r, w):
            if op1 is None:
                S.op(eng, lambda e: e.tensor_scalar(out=out, in0=in0, scalar1=s1, scalar2=None, op0=op0), bl(r), bl(w))
            else:
                S.op(eng, lambda e: e.tensor_scalar(out=out, in0=in0, scalar1=s1, scalar2=s2, op0=op0, op1=op1), bl(r), bl(w))

        def stt(out, in0, scalar, in1, op0, op1, r, w):
            S.op("vector", lambda e: e.scalar_tensor_tensor(out=out, in0=in0, scalar=scalar, in1=in1, op0=op0, op1=op1), bl(r), bl(w))

        def cp(eng, out, in_, r, w):
            if eng == "scalar":
                act(out, in_, AF.Copy, r, w)
            else:
                S.op(eng, lambda e: e.tensor_copy(out=out, in_=in_), bl(r), bl(w))

        def mset(eng, ap, val, w):
            S.op(eng, lambda e: e.memset(ap, val), [], bl(w))

        def scan(out, d0, d1, init, r, w):
            S.op("vector", lambda e: e.tensor_tensor_scan(out=out, data0=d0, data1=d1, initial=init, op0=ALU.mult, op1=ALU.add),
                 bl(r), bl(w))

        def dma(q, out, in_, r, w, is_out=False, cast=False):
            kw = {"max_dma_last_dim": 8192} if cast else {}
            return S.op(q, lambda e: e.dma_start(out=out, in_=in_, **kw), bl(r), bl(w), dma_sem=S.pool_sem(q), is_out=is_out)

        dbg_state = [0]
        if KDUMP:
            dbgt = static("dbgt", [128, 8192], F32)

        def dump(name, ap, r, p0=0, p1=128):
            if not KDUMP or name in DUMPS or S.dead or not S.enabled:
                return
            n = ap.shape[-1]
            c0 = dbg_state[0]
            DUMPS[name] = (c0, n, p0, p1)
            dbg_state[0] += n
            S.op("vector", lambda e: e.tensor_copy(out=dbgt.ap[p0:p1, c0:c0 + n], in_=ap), bl(r), [dbgt.b])

        dma("sync", cf.ap, cf_d.rearrange("p (a b) -> p a b", a=3), [], [cf])
        dma("sync", pv.ap, pv_d, [], [pv])
        dma("gpsimd", cb.ap, cb_d.rearrange("p (a b) -> p a b", a=3), [], [cb], cast=True)
        for l in range(DEPTH):
            o = l * NPV
            act(dv.ap[:, l, 0:6], pv.ap[:, o + ALOG:o + ALOG + 6], AF.Exp, [pv], [dv])
            ts("vector", dv.ap[:, l, 0:6], dv.ap[:, l, 0:6], -1.0, None, ALU.mult, None, [dv], [dv])
            act(dv.ap[:, l, 6:8], pv.ap[:, o + LAM:o + LAM + 2], AF.Exp, [pv], [dv], scale=-1.0)
            act(dv.ap[:, l, 6:8], dv.ap[:, l, 6:8], AF.Ln, [dv], [dv], bias=1.0)
            ts("vector", dv.ap[:, l, 6:8], dv.ap[:, l, 6:8], -8.0, None, ALU.mult, None, [dv], [dv])

        if KSTOP <= 1:
            S.dead = True
        def rmsnorm(src_aps, src_bufs, gcol0, nch, width, out_aps, out_tl, ncols, psum_loc):
            bank, c0 = psum_loc
            pso, psb_ = PS(bank, c0, c0 + ncols)
            for c in range(nch):
                eng = "gpsimd" if c % 2 == 0 else "vector"
                tt(eng, sq.ap[:, c, 0:ncols], src_aps[c], src_aps[c], ALU.mult, src_bufs, [sq])
            for c in range(nch):
                mm(pso, ones_b, sq.ap[:, c, 0:ncols], c == 0, c == nch - 1, [cb, sq], psb_)
            act(rs0.ap[:, 0:ncols], pso, AF.Ln, psb_, [rs0], bias=EPS_AP, scale=1.0 / width)
            act(rs1.ap[:, 0:ncols], rs0.ap[:, 0:ncols], AF.Exp, [rs0], [rs1], scale=-0.5)
            for c in range(nch):
                stt(out_aps[c], src_aps[c], pv.ap[:, gcol0 + c:gcol0 + c + 1], rs1.ap[:, 0:ncols], ALU.mult, ALU.mult,
                    src_bufs + [pv, rs1], [out_tl])

        epst = static("epst", [128, 1], F32)
        mset("vector", epst.ap, EPS, [epst])
        EPS_AP = epst.ap[:, 0:1]

        MP = Bump(A, 0)

        def mixer_persist():
            MP.reset()
            d = {}
            d["win"] = MP.get([8, IN_COLS], BF16, "win")
            d["wout"] = MP.get([8, D_MODEL], BF16, "wout")
            d["bd"] = MP.get([4, 128], BF16, "bd")
            d["KT"] = MP.get([3, NT], BF16, "KT")
            d["V"] = MP.get([NT128, 384], BF16, "V")
            d["ncum"] = MP.get([NT128, 6], F32, "ncum")
            d["xraw"] = MP.get([7, TB + 3], F32, "xraw")
            d["lraw"] = MP.get([2, TB + 3], F32, "lraw")
            d["st"] = MP.get([384], F32, "st")
            d["stb"] = MP.get([384], BF16, "stb")
            d["lcar"] = MP.get([2], F32, "lcar")
            d["fcar"] = MP.get([6], F32, "fcar")
            d["fref"] = MP.get([6], F32, "fref")
            return d

        _tmp = mixer_persist()
        SCR_BASE = MP.off
        A.live = []
        SC = Bump(A, SCR_BASE)
        FFN_B = Bump(A, 0)
        PLE_BASE = SCR_BASE
        PLE_B = Bump(A, PLE_BASE)

        def load_mixer_weights(l):
            d = mixer_persist()
            d["win"].split(8)
            d["wout"].split(8)
            for c in range(8):
                dma("gpsimd", d["win"].ap[:, c, :], win_d[l, c * 128:(c + 1) * 128, :], [], [d["win"].subs[c]], cast=True)
            for c in range(8):
                dma("gpsimd", d["wout"].ap[:, c, :], wout_d[l, c * 128:(c + 1) * 128, :], [], [d["wout"].subs[c]], cast=True)
            dma("gpsimd", d["bd"].ap, bd_d[l].rearrange("p (a b) -> p a b", a=4), [], [d["bd"]], cast=True)
            return d

        for s in range(NSEQ):
            SC.reset()
            xst = [uT_f, ycat_f]
            for t in range(NT128):
                st_ = xst[t % 2]
                dma("sync", st_.ap, x_d[s, t * 128:(t + 1) * 128, :], [], [st_])
                for half in range(2):
                    bank = (2 * t + half) % 4
                    pso, pbs = PS(bank, 0, 512)
                    for j in range(4):
                        c = half * 4 + j
                        tr(ps_t[bank][:, j * 128:(j + 1) * 128], st_.ap[:, c * 128:(c + 1) * 128], ident_f, [st_, cf], pbs)
                    eng = "vector" if half == 0 else "scalar"
                    cp(eng, hT[:, half * 4:half * 4 + 4, t * 128:(t + 1) * 128],
                       pso.rearrange("p (a b) -> p a b", a=4), pbs, [hTb[(t * 128) // TB]])

            if KSTOP <= 2:
                S.dead = True
            for l in range(DEPTH):
                o = l * NPV
                W = PREFETCHED[0] if PREFETCHED[0] is not None else load_mixer_weights(l)
                PREFETCHED[0] = None
                win, wout, bdm = W["win"], W["wout"], W["bd"]
                KT, V, ncum, xraw, lraw = W["KT"], W["V"], W["ncum"], W["xraw"], W["lraw"]
                st, stb, lcar, fcar = W["st"], W["stb"], W["lcar"], W["fcar"]
                fref = W["fref"]
                mset("vector", xraw.ap[:, :, 0:3], 0.0, [xraw])
                mset("vector", lraw.ap[:, :, 0:3], 0.0, [lraw])
                mset("gpsimd", st.ap, 0.0, [st])
                mset("gpsimd", stb.ap, 0.0, [stb])
                mset("gpsimd", lcar.ap, 0.0, [lcar])
                mset("gpsimd", fcar.ap, 0.0, [fcar])

                for blk in range(NBLK):
                    t0 = blk * TB
                    hb = hTb[blk]
                    hsl = [hT[:, c, t0:t0 + TB] for c in range(8)]
                    rmsnorm(hsl, [hb], o + G1, 8, D_MODEL, [uT.ap[:, c, :] for c in range(8)], uT, TB, (6, 256))

                    slot = [0]

                    def proj(col0, ncolsM=128):
                        i = slot[0]
                        slot[0] += 1
                        bank, half = i % 2, (i // 2) % 2
                        pso, pbs = PS(bank, half * 256, half * 256 + 256)
                        for c in range(8):
                            mm(pso, win.ap[:, c, col0:col0 + ncolsM], uT.ap[:, c, :], c == 0, c == 7, win.R(c) + [uT], pbs)
                        return pso, pbs

                    SC.reset()
                    if "ssd" not in DBG:
                        for c in range(3):
                            mset("vector", ycat.ap[:, c, :], 0.0, [ycat])
                    S.enabled = "ssd" in DBG
                    xbc = SC.get([7, TB], BF16, "xbc")
                    zs = SC.get([3, TB], F32, "zs")
                    yssd = SC.get([3, TB], F32, "yssd")
                    cacc = [SC.get([TB], F32, f"cacc{i}") for i in range(2)]
                    sm = SC.get([8, 6], F32, "sm")
                    xdt = SC.get([384], BF16, "xdt")
                    xdts = SC.get([384], BF16, "xdts")
                    btok = SC.get([256], BF16, "btok")
                    Dm = [SC.get([128], F32, f"D{i}") for i in range(2)]
                    MT = [SC.get([128], BF16, f"MT{i}") for i in range(2)]
                    ebc = [SC.get([128], F32, f"ebc{i}") for i in range(2)]
                    Cs = [SC.get([128], BF16, f"Cs{i}") for i in range(2)]
                    stt_tmp = SC.get([384], F32, "sttmp")
                    smp = SC.get([24], F32, "smp")

                    for c in range(3):
                        pso, pbs = proj(OZ + c * 128)
                        act(zs.ap[:, c, :], pso, AF.Silu, pbs, [zs])
                    xbc.split(7)
                    xbc.b.readers = []
                    xraw_s = [Buf() for _ in range(7)]
                    for sb_ in xraw_s:
                        sb_.writer = xraw.b.writer
                        sb_.readers = list(xraw.b.readers)

                    def silu_c(c):
                        act(xbc.ap[:, c, :], cacc[c % 2].ap, AF.Silu, [cacc[c % 2]], [xbc.subs[c]])

                    for c in range(7):
                        pso, pbs = proj(OXS + c * 128)
                        cp("scalar", xraw.ap[:, c, 3:3 + TB], pso, pbs, [xraw_s[c]])
                        if c > 0:
                            silu_c(c - 1)
                        ca = cacc[c % 2]
                        wc = lambda k: pv.ap[:, o + SCW + c * 4 + k:o + SCW + c * 4 + k + 1]
                        ts("vector", ca.ap, xraw.ap[:, c, 0:TB], wc(0), pv.ap[:, o + SCB + c:o + SCB + c + 1], ALU.mult, ALU.add,
                           [xraw_s[c], pv], [ca])
                        for k in range(1, 4):
                            stt(ca.ap, xraw.ap[:, c, k:k + TB], wc(k), ca.ap, ALU.mult, ALU.add, [xraw_s[c], pv, ca], [ca])
                    silu_c(6)
                    cp("gpsimd", xraw.ap[:, :, 0:3], xraw.ap[:, :, TB:TB + 3], xraw_s, xraw_s + [xraw])
                    cp("vector", fref.ap, fcar.ap, [fcar], [fref])
                    for j in range(TB // 128):
                        cs = slice(j * 128, (j + 1) * 128)
                        tg = t0 + j * 128
                        pso, pbs = PS(2, 0, 396)
                        for c in range(8):
                            mm(pso, uT.ap[:, c, cs], win.ap[:, c, OV:OV + 396], c == 0, c == 7, win.R(c) + [uT], pbs)
                        kb = tg // 128
                        cp("vector", V.ap[:, kb, :], ps_t[2][:, 0:384], pbs, [V])
                        dt, adt, nacs, tmp6, decs, cdec, sdt, nlf = [sm.ap[:, i, :] for i in range(8)]
                        tt("vector", tmp6, ps_t[2][:, 384:390], pv.ap[:, o + DTB:o + DTB + 6], ALU.add, pbs + [pv], [sm])
                        tt("vector", nlf, ps_t[2][:, 390:396], pv.ap[:, o + FBF:o + FBF + 6], ALU.add, pbs + [pv], [sm])
                        act(tmp6, tmp6, AF.Exp, [sm], [sm])
                        act(nlf, nlf, AF.Exp, [sm], [sm], scale=-1.0)
                        act(dt, tmp6, AF.Ln, [sm], [sm], bias=1.0)
                        act(nlf, nlf, AF.Ln, [sm], [sm], bias=1.0)
                        tt("vector", adt, dt, dv.ap[:, l, 0:6], ALU.mult, [sm, dv], [sm])
                        p4, p4b = PS(4, 384, 408)
                        mm(ps_t[4][:, 384:390], tri_f, adt, True, True, [cf, sm], p4b)
                        mm(ps_t[4][:, 390:396], ones_f, adt, True, True, [cf, sm], p4b)
                        mm(ps_t[4][:, 396:402], tri_f, nlf, True, True, [cf, sm], p4b)
                        mm(ps_t[4][:, 402:408], ones_f, nlf, True, True, [cf, sm], p4b)
                        cp("vector", smp.ap, ps_t[4][:, 384:408], p4b, [smp])
                        ts("vector", nacs, smp.ap[:, 0:6], -1.0, None, ALU.mult, None, [smp], [sm])
                        tt("vector", tmp6, smp.ap[:, 6:12], nacs, ALU.add, [smp, sm], [sm])
                        act(decs, tmp6, AF.Exp, [sm], [sm])
                        act(cdec, smp.ap[:, 6:12], AF.Exp, [smp], [sm])
                        tt("vector", sdt, dt, decs, ALU.mult, [sm], [sm])
                        tt("vector", ncum.ap[:, kb, :], smp.ap[:, 12:18], fcar.ap, ALU.add, [smp, fcar], [ncum])
                        tt("vector", fcar.ap, smp.ap[:, 18:24], fcar.ap, ALU.add, [smp, fcar], [fcar])
                        for c in range(5):
                            tr(psb_t[:, c * 128:(c + 1) * 128], xbc.ap[:, c, cs], ident_b, xbc.R(c) + [cb], [psbb])
                        xs_tok = psb_t[:, 0:384].rearrange("p (h e) -> p h e", h=6)
                        tt("vector", xdt.ap.rearrange("p (h e) -> p h e", h=6), xs_tok, dt.unsqueeze(2).to_broadcast([128, 6, 64]),
                           ALU.mult, [psbb, sm], [xdt])
                        tt("vector", xdts.ap.rearrange("p (h e) -> p h e", h=6), xs_tok, sdt.unsqueeze(2).to_broadcast([128, 6, 64]),
                           ALU.mult, [psbb, sm], [xdts])
                        cp("vector", btok.ap, psb_t[:, 384:640], [psbb], [btok])
                        pg, pgb = PS(3, 0, 256)
                        for g in range(2):
                            mm(ps_t[3][:, g * 128:(g + 1) * 128], xbc.ap[:, 3 + g, cs], xbc.ap[:, 5 + g, cs], True, True, xbc.R(3 + g) + xbc.R(5 + g), pgb)
                        py, pyb = PS(4, 0, 384)

                        def stageA(h):
                            g = h // 3
                            i2 = h % 2
                            eb = 6 if i2 == 0 else 2
                            pE, pEb = PS(eb, 0, 256)
                            adt_b = adt[:, h:h + 1].to_broadcast([128, 128])
                            mm(ps_t[eb][:, 0:128], adt_b, tri_f, True, False, [sm, cf], pEb)
                            mm(ps_t[eb][:, 0:128], ident_b, mneg_b, False, True, [cb], pEb)
                            mm(ps_t[eb][:, 128:256], adt_b, tri_f, True, True, [sm, cf], pEb)
                            act(Dm[i2].ap, ps_t[eb][:, 0:128], AF.Exp, pEb + [sm], [Dm[i2]], bias=nacs[:, h:h + 1])
                            act(ebc[i2].ap, ps_t[eb][:, 128:256], AF.Exp, pEb, [ebc[i2]])
                            tt("vector", MT[i2].ap, ps_t[3][:, g * 128:(g + 1) * 128], Dm[i2].ap, ALU.mult, pgb + [Dm[i2]], [MT[i2]])
                            tt("gpsimd", Cs[i2].ap, xbc.ap[:, 5 + g, cs], ebc[i2].ap, ALU.mult, xbc.R(5 + g) + [ebc[i2]], [Cs[i2]])

                        def stageB(h):
                            i2 = h % 2
                            hp = (h % 2) * 64
                            yo = ps_t[4][hp:hp + 64, (h // 2) * 128:(h // 2) * 128 + 128]
                            mm(yo, xdt.ap[:, h * 64:(h + 1) * 64], MT[i2].ap, True, False, [xdt, MT[i2]], pyb)
                            mm(yo, stb.ap[:, h * 64:(h + 1) * 64], Cs[i2].ap, False, True, [stb, Cs[i2]], pyb)

                        stageA(0)
                        for h in range(6):
                            if h + 1 < 6:
                                stageA(h + 1)
                            stageB(h)
                        for c in range(3):
                            stt(yssd.ap[:, c, cs], xbc.ap[:, c, cs], pv.ap[:, o + DSK + c:o + DSK + c + 1],
                                ps_t[4][:, c * 128:(c + 1) * 128], ALU.mult, ALU.add, xbc.R(c) + [pv] + pyb, [yssd])
                        pst, pstb = PS(5, 0, 384)
                        for g in range(2):
                            mm(ps_t[5][:, g * 192:(g + 1) * 192], btok.ap[:, g * 128:(g + 1) * 128], xdts.ap[:, g * 192:(g + 1) * 192],
                               True, True, [btok, xdts], pstb)
                        tt("vector", stt_tmp.ap.rearrange("p (h e) -> p h e", h=6), st.ap.rearrange("p (h e) -> p h e", h=6),
                           cdec.unsqueeze(2).to_broadcast([128, 6, 64]), ALU.mult, [st, sm], [stt_tmp])
                        tt("vector", st.ap, stt_tmp.ap, pst, ALU.add, [stt_tmp] + pstb, [st])
                        cp("gpsimd", stb.ap, st.ap, [st], [stb])
                    dump("yssd1_pre", yssd.ap[:, 1, :], [yssd])
                    dump("zs1", zs.ap[:, 1, :], [zs])
                    for c in range(3):
                        tt("gpsimd", yssd.ap[:, c, :], yssd.ap[:, c, :], zs.ap[:, c, :], ALU.mult, [yssd, zs], [yssd])
                    rmsnorm([yssd.ap[:, c, :] for c in range(3)], [yssd], o + SNG, 3, 384,
                            [ycat.ap[:, c, :] for c in range(3)], ycat, TB, (6, 256))

                    dump("ycat1_early", ycat.ap[:, 1, :], [ycat])
                    dump("yssd1", yssd.ap[:, 1, :], [yssd])
                    S.enabled = True
                    SC.reset()
                    if "lru" not in DBG:
                        for c in range(2):
                            mset("vector", ycat.ap[:, 3 + c, :], 0.0, [ycat])
                    S.enabled = "lru" in DBG
                    LT = [{k: SC.get([TB], BF16 if k == "xlb" else F32, f"{k}{c}") for k in ("xl", "xlb", "rg", "ig", "aa", "mu", "hl", "g1", "g2")}
                          for c in range(2)]
                    lraw_s = [Buf() for _ in range(2)]
                    for sb_ in lraw_s:
                        sb_.writer = lraw.b.writer
                        sb_.readers = list(lraw.b.readers)
                    pgl = [None, None]

                    def l1(c):
                        T_ = LT[c]
                        pso, pbs = proj(OLX + c * 128)
                        cp("scalar", lraw.ap[:, c, 3:3 + TB], pso, pbs, [lraw_s[c]])
                        pgl[c] = proj(OLG + c * 128)
                        cp("scalar", T_["g1"].ap, pgl[c][0], pgl[c][1], [T_["g1"]])

                    def l2(c):
                        T_ = LT[c]
                        wc = lambda k: pv.ap[:, o + LCW + c * 4 + k:o + LCW + c * 4 + k + 1]
                        ts("vector", T_["xl"].ap, lraw.ap[:, c, 0:TB], wc(0), pv.ap[:, o + LCB + c:o + LCB + c + 1], ALU.mult, ALU.add,
                           [lraw_s[c], pv], [T_["xl"]])
                        for k in range(1, 4):
                            stt(T_["xl"].ap, lraw.ap[:, c, k:k + TB], wc(k), T_["xl"].ap, ALU.mult, ALU.add, [lraw_s[c], pv, T_["xl"]], [T_["xl"]])
                        cp("gpsimd", T_["xlb"].ap, T_["xl"].ap, [T_["xl"]], [T_["xlb"]])
                        tt("gpsimd", T_["g2"].ap, T_["g1"].ap, T_["g1"].ap, ALU.mult, [T_["g1"]], [T_["g2"]])
                        ts("vector", T_["g2"].ap, T_["g2"].ap, 0.044715, 1.0, ALU.mult, ALU.add, [T_["g2"]], [T_["g2"]])
                        tt("gpsimd", T_["g2"].ap, T_["g2"].ap, T_["g1"].ap, ALU.mult, [T_["g2"], T_["g1"]], [T_["g2"]])

                    def l3(c):
                        T_ = LT[c]
                        pa, pab = PS(2, 0, 256) if c == 0 else PS(5, 0, 256)
                        mm(pa, bdm.ap[:, c, :], T_["xlb"].ap, True, True, [bdm, T_["xlb"]], pab)
                        act(T_["rg"].ap, pa, AF.Sigmoid, pab + [pv], [T_["rg"]], bias=pv.ap[:, o + LBA + c:o + LBA + c + 1])
                        mm(pa, bdm.ap[:, 2 + c, :], T_["xlb"].ap, True, True, [bdm, T_["xlb"]], pab)
                        act(T_["ig"].ap, pa, AF.Sigmoid, pab + [pv], [T_["ig"]], bias=pv.ap[:, o + LBX + c:o + LBX + c + 1])
                        act(T_["g2"].ap, T_["g2"].ap, AF.Sigmoid, [T_["g2"]], [T_["g2"]], scale=1.5957691216057308)

                    def l4(c):
                        T_ = LT[c]
                        ts("vector", T_["aa"].ap, T_["rg"].ap, dv.ap[:, l, 6 + c:7 + c], None, ALU.mult, None, [T_["rg"], dv], [T_["aa"]])
                        act(T_["aa"].ap, T_["aa"].ap, AF.Exp, [T_["aa"]], [T_["aa"]])
                        tt("gpsimd", T_["ig"].ap, T_["ig"].ap, T_["xl"].ap, ALU.mult, [T_["ig"], T_["xl"]], [T_["ig"]])
                        tt("gpsimd", T_["g2"].ap, T_["g2"].ap, T_["g1"].ap, ALU.mult, [T_["g2"], T_["g1"]], [T_["g2"]])

                    def l5(c):
                        T_ = LT[c]
                        tt("gpsimd", T_["mu"].ap, T_["aa"].ap, T_["aa"].ap, ALU.mult, [T_["aa"]], [T_["mu"]])
                        act(T_["mu"].ap, T_["mu"].ap, AF.Ln, [T_["mu"]], [T_["mu"]], bias=1.0, scale=-1.0)
                        act(T_["mu"].ap, T_["mu"].ap, AF.Exp, [T_["mu"]], [T_["mu"]], scale=0.5)

                    def l6(c):
                        T_ = LT[c]
                        tt("vector", T_["mu"].ap, T_["mu"].ap, T_["ig"].ap, ALU.mult, [T_["mu"], T_["ig"]], [T_["mu"]])
                        scan(T_["hl"].ap, T_["aa"].ap, T_["mu"].ap, lcar.ap[:, c:c + 1], [T_["aa"], T_["mu"], lcar], [T_["hl"]])
                        cp("vector", lcar.ap[:, c:c + 1], T_["hl"].ap[:, TB - 1:TB], [T_["hl"]], [lcar])
                        tt("vector", T_["hl"].ap, T_["hl"].ap, T_["g2"].ap, ALU.mult, [T_["hl"], T_["g2"]], [T_["hl"]])

                    for st_fn in (l1, l2, l3, l4, l5, l6):
                        for c in range(2):
                            st_fn(c)
                        if st_fn is l2:
                            cp("gpsimd", lraw.ap[:, :, 0:3], lraw.ap[:, :, TB:TB + 3], lraw_s, lraw_s + [lraw])
                    hl_aps = [LT[c]["hl"].ap for c in range(2)]
                    hl_bufs = [LT[c]["hl"] for c in range(2)]
                    rmsnorm(hl_aps, hl_bufs, o + LNG, 2, 256,
                            [ycat.ap[:, 3 + c, :] for c in range(2)], ycat, TB, (6, 256))

                    S.enabled = True
                    SC.reset()
                    if "fox" not in DBG:
                        for c in range(3):
                            mset("vector", ycat.ap[:, 5 + c, :], 0.0, [ycat])
                    S.enabled = "fox" in DBG
                    qT = SC.get([3, TB], BF16, "qT")
                    pT = [SC.get([TB], BF16, f"pT{i}") for i in range(4)]
                    lnd = SC.get([TB], F32, "lnd")
                    yfox = SC.get([3, TB], F32, "yfox")
                    for c in range(3):
                        pso, pbs = proj(OQ + c * 128)
                        cp("scalar", qT.ap[:, c, :], pso, pbs, [qT])
                    for c in range(3):
                        pso, pbs = proj(OK_ + c * 128)
                        cp("scalar", KT.ap[:, c, t0:t0 + TB], pso, pbs, [KT])
                    nkb = (t0 + TB) // 128
                    nb = SC.get([NT128, 6], F32, "nb")
                    tt("vector", nb.ap[:, 0:nkb, :], ncum.ap[:, 0:nkb, :], fref.ap.unsqueeze(1).to_broadcast([128, nkb, 6]), ALU.subtract,
                       [ncum, fref], [nb])
                    pi = 0
                    for c in range(3):
                        py, pyb = PS(5, 0, 256)
                        pdn, pdb = PS(4, 0, 256)
                        its = [(hh, kb) for kb in range(nkb) for hh in range(2)]

                        def stA(idx, it):
                            hh, kb = it
                            h = 2 * c + hh
                            hp = hh * 64
                            rel = kb * 128 - t0
                            q0 = 0 if rel < 0 else rel
                            sbank = 6 if idx % 2 == 0 else 3
                            psS, psSb = PS(sbank, 0, 256)
                            so = ps_t[sbank][:, q0:TB]
                            diag = rel >= 0
                            mm(so, KT.ap[hp:hp + 64, c, kb * 128:(kb + 1) * 128], qT.ap[hp:hp + 64, c, q0:TB], True, not diag,
                               [KT, qT], psSb)
                            if diag:
                                mm(ps_t[sbank][:, q0:q0 + 128], ident_b, mneg_b, False, True, [cb], psSb)
                            pt = pT[idx % 4]
                            if q0 > 0:
                                mset("gpsimd", pt.ap[:, 0:q0], 0.0, [pt])
                            act(pt.ap[:, q0:TB], so, AF.Exp, psSb + [nb], [pt], bias=nb.ap[:, kb, h:h + 1], scale=0.125)

                        def stB(idx, it):
                            hh, kb = it
                            h = 2 * c + hh
                            hp = hh * 64
                            pt = pT[idx % 4]
                            first, last = kb == 0, kb == nkb - 1
                            mm(ps_t[5][hp:hp + 64, 0:TB], V.ap[:, kb, h * 64:(h + 1) * 64], pt.ap[:, 0:TB], first, last, [V, pt], pyb)
                            mm(ps_t[4][hp:hp + 64, 0:TB], ones_b[:, 0:64], pt.ap[:, 0:TB], first, last, [cb, pt], pdb)

                        stA(pi, its[0])
                        for i_, it in enumerate(its):
                            if i_ + 1 < len(its):
                                stA(pi + i_ + 1, its[i_ + 1])
                            stB(pi + i_, it)
                        pi += len(its)
                        act(lnd.ap, pdn, AF.Ln, pdb, [lnd])
                        act(lnd.ap, lnd.ap, AF.Exp, [lnd], [lnd], scale=-1.0)
                        tt("vector", yfox.ap[:, c, :], py, lnd.ap, ALU.mult, pyb + [lnd], [yfox])
                    dump("qT0", qT.ap[:, 0, :], [qT])
                    dump("KT0", KT.ap[:, 0, 0:TB], [KT])
                    dump("nc0", ncum.ap[:, 0, :], [ncum])
                    dump("nc1", ncum.ap[:, 1, :], [ncum])
                    dump("V0", V.ap[:, 0, :], [V])
                    dump("V1", V.ap[:, 1, :], [V])
                    dump("yfox0", yfox.ap[:, 0, :], [yfox])
                    rmsnorm([yfox.ap[:, c, :] for c in range(3)], [yfox], o + FNG, 3, 384,
                            [ycat.ap[:, 5 + c, :] for c in range(3)], ycat, TB, (6, 256))

                    S.enabled = True
                    for c in range(8):
                        dump(f"ycat{c}", ycat.ap[:, c, :], [ycat])
                    for dc in range(8):
                        bank, half = dc % 2, (dc // 2) % 2
                        pso, pbs = PS(bank, half * 256, half * 256 + 256)
                        for c in range(8):
                            mm(pso, wout.ap[:, c, dc * 128:(dc + 1) * 128], ycat.ap[:, c, :], c == 0, c == 7, wout.R(c) + [ycat], pbs)
                        tt("vector", hsl[dc], hsl[dc], pso, ALU.add, [hb] + pbs, [hb])

                if KSTOP <= 3:
                    S.dead = True
                FFN_B.reset()
                uTs = FFN_B.get([8, NT], BF16, "uTs")
                wgt = [FFN_B.get([8, 512], BF16, f"wg{i}") for i in range(2)]
                wut = [FFN_B.get([8, 512], BF16, f"wu{i}") for i in range(2)]
                wdt = [FFN_B.get([4, D_MODEL], BF16, f"wd{i}") for i in range(2)]
                actb = FFN_B.get([4, 512], BF16, "actb")
                sgt = [FFN_B.get([512], F32, f"sg{i}") for i in range(2)]
                assert FFN_B.off <= PLE_BASE, (FFN_B.off, PLE_BASE)
                PLE_B.reset()
                wpg = PLE_B.get([8, D_MODEL], BF16, "wpg")
                wpp = PLE_B.get([2, D_MODEL], BF16, "wpp")
                pTt = PLE_B.get([2, TB], BF16, "pTt")
                pst_ = [PLE_B.get([256], F32, f"pst{i}") for i in range(2)]
                gat = [PLE_B.get([TB], F32, f"gat{i}") for i in range(2)]
                wpg.split(8)
                wpp.split(2)
                for c in range(8):
                    dma("gpsimd", wpg.ap[:, c, :], wpg_d[l, c * 128:(c + 1) * 128, :], [], [wpg.subs[c]], cast=True)
                for c in range(2):
                    dma("gpsimd", wpp.ap[:, c, :], wpp_d[l, c * 128:(c + 1) * 128, :], [], [wpp.subs[c]], cast=True)
                S.enabled = "ffn" in DBG
                for blk in range(NBLK):
                    t0 = blk * TB
                    rmsnorm([hT[:, c, t0:t0 + TB] for c in range(8)], [hTb[blk]], o + G2, 8, D_MODEL,
                            [uTs.ap[:, c, t0:t0 + TB] for c in range(8)], uTs, TB, (6, 256))
                groups = [(g * 512, 512) for g in range(5)] + [(2560, 256)]
                NT512 = NT // 512
                for gi, (f0, fw_) in enumerate(groups):
                    wg_, wu_, wd_ = wgt[gi % 2], wut[gi % 2], wdt[gi % 2]
                    nj = fw_ // 128
                    wg_.split(8)
                    wu_.split(8)
                    wd_.split(4)
                    wg_.b.readers = []
                    wu_.b.readers = []
                    wd_.b.readers = []
                    for c in range(8):
                        dma("gpsimd", wg_.ap[:, c, 0:fw_], wg_d[l, c * 128:(c + 1) * 128, f0:f0 + fw_], [], [wg_.subs[c]], cast=True)
                    for c in range(8):
                        dma("gpsimd", wu_.ap[:, c, 0:fw_], wu_d[l, c * 128:(c + 1) * 128, f0:f0 + fw_], [], [wu_.subs[c]], cast=True)
                    for j in range(nj):
                        dma("gpsimd", wd_.ap[:, j, :], wd_d[l, f0 + j * 128:f0 + (j + 1) * 128, :], [], [wd_.subs[j]], cast=True)
                    for tb4 in range(NT512):
                        tsl = slice(tb4 * 512, (tb4 + 1) * 512)
                        for j in range(nj):
                            pg_, pgb_ = PS(j % 2, 0, 512)
                            pu_, pub_ = PS(2 + j % 2, 0, 512)
                            for c in range(8):
                                mm(pg_, wg_.ap[:, c, j * 128:(j + 1) * 128], uTs.ap[:, c, tsl], c == 0, c == 7, wg_.R(c) + [uTs], pgb_)
                            for c in range(8):
                                mm(pu_, wu_.ap[:, c, j * 128:(j + 1) * 128], uTs.ap[:, c, tsl], c == 0, c == 7, wu_.R(c) + [uTs], pub_)
                            sg_ = sgt[j % 2]
                            act(sg_.ap, pg_, AF.Silu, pgb_, [sg_])
                            tt("vector", actb.ap[:, j, :], sg_.ap, pu_, ALU.mult, [sg_] + pub_, [actb])
                        for dc in range(8):
                            pd_, pdb_ = PS(4 + dc % 2, 0, 512)
                            for j in range(nj):
                                mm(pd_, wd_.ap[:, j, dc * 128:(dc + 1) * 128], actb.ap[:, j, :], j == 0, j == nj - 1, wd_.R(j) + [actb], pdb_)
                            hbs = [hTb[2 * tb4], hTb[2 * tb4 + 1]]
                            tt("vector", hT[:, dc, tsl], hT[:, dc, tsl], pd_, ALU.add, hbs + pdb_, hbs)

                S.enabled = True
                nxt = None
                if l + 1 < DEPTH:
                    nxt = l + 1
                elif s + 1 < NSEQ:
                    nxt = 0
                if nxt is not None:
                    PREFETCHED[0] = load_mixer_weights(nxt)
                S.enabled = "ple" in DBG
                for blk in range(NBLK):
                    t0 = blk * TB
                    hb = hTb[blk]
                    hsl = [hT[:, c, t0:t0 + TB] for c in range(8)]
                    ub = uT if blk % 2 == 0 else ycat
                    rmsnorm(hsl, [hb], o + G3, 8, D_MODEL, [ub.ap[:, c, :] for c in range(8)], ub, TB, (6, 256))
                    for j in range(TB // 128):
                        ps_ = pst_[j % 2]
                        dma("sync", ps_.ap, p_d[l, s, t0 + j * 128:t0 + (j + 1) * 128, :], [], [ps_])
                        pt_, ptb = PS(4, 0, 256)
                        for c in range(2):
                            tr(ps_t[4][:, c * 128:(c + 1) * 128], ps_.ap[:, c * 128:(c + 1) * 128], ident_f, [ps_, cf], ptb)
                        cp("scalar", pTt.ap[:, :, j * 128:(j + 1) * 128], pt_.rearrange("p (a b) -> p a b", a=2), ptb, [pTt])
                    for dc in range(8):
                        if KPLE < 2:
                            break
                        pg_, pgb_ = PS(dc % 2, 0, 256)
                        pp_, ppb_ = PS(2 + dc % 2, 0, 256)
                        for c in range(8):
                            mm(pg_, wpg.ap[:, c, dc * 128:(dc + 1) * 128], ub.ap[:, c, :], c == 0, c == 7, wpg.R(c) + [ub], pgb_)
                        for c in range(2):
                            mm(pp_, wpp.ap[:, c, dc * 128:(dc + 1) * 128], pTt.ap[:, c, :], c == 0, c == 1, wpp.R(c) + [pTt], ppb_)
                        if KPLE < 3:
                            continue
                        ga = gat[dc % 2]
                        act(ga.ap, pg_, AF.Sigmoid, pgb_ + [pv], [ga], bias=pv.ap[:, o + BPG + dc:o + BPG + dc + 1])
                        tt("vector", ga.ap, ga.ap, pp_, ALU.mult, [ga] + ppb_, [ga])
                        tt("vector", hsl[dc], hsl[dc], ga.ap, ALU.add, [hb, ga], [hb])

            S.enabled = True
            if KSTOP <= 4:
                S.dead = True
            SC.reset()
            onT = SC.get([8, TB], F32, "onT")
            ost = [uT_f, ycat_f]
            for blk in range(NBLK):
                t0 = blk * TB
                rmsnorm([hT[:, c, t0:t0 + TB] for c in range(8)], [hTb[blk]], GF, 8, D_MODEL,
                        [onT.ap[:, c, :] for c in range(8)], onT, TB, (6, 256))
                for j in range(TB // 128):
                    if KFIN < 2:
                        break
                    os_ = ost[j % 2]
                    for half in range(2):
                        bank = half
                        pso, pbs = PS(bank, 0, 512)
                        for q in range(4):
                            c = half * 4 + q
                            tr(ps_t[bank][:, q * 128:(q + 1) * 128], onT.ap[:, c, j * 128:(j + 1) * 128], ident_f, [onT, cf], pbs)
                        cp("vector" if half == 0 else "scalar", os_.ap[:, half * 512:(half + 1) * 512], pso, pbs, [os_])
                    if KFIN >= 3:
                        dma("sync", y_d[s, t0 + j * 128:t0 + (j + 1) * 128, :], os_.ap, [os_], [], is_out=True)
        if KDUMP:
            S.enabled = True
            S.dead = False
            dma("sync", dbg_d, dbgt.ap, [dbgt], [], is_out=True)
        S.emit()
    return nc


PREFETCHED = [None]
import os
DBG = set(os.environ.get("KDBG", "ssd,lru,fox,ffn,ple").split(","))
KSTOP = int(os.environ.get("KSTOP", "99"))
KFIN = int(os.environ.get("KFIN", "3"))
KPLE = int(os.environ.get("KPLE", "3"))
KDUMP = int(os.environ.get("KDUMP", "0"))
DUMPS = {}

def _consts():
    ident = np.eye(128, dtype=np.float32)
    tri = np.triu(np.ones((128, 128), np.float32))
    ones = np.ones((128, 128), np.float32)
    mneg = np.where(np.arange(128)[:, None] > np.arange(128)[None, :], np.float32(-30000.0), np.float32(0.0)).astype(np.float32)
    cf = np.concatenate([ident, tri, ones], axis=1)
    cb = np.concatenate([ident, ones, mneg], axis=1)
    sel = np.zeros((6, 6, 128), np.float32)
    for h in range(6):
        sel[h, h, :] = 1.0
    return cf, cb, sel.reshape(6, 768)


def _pcol(v, nch):
    return np.ascontiguousarray(np.asarray(v, np.float32).reshape(nch, 128).T)


def _pack(inp, DEPTH):
    f = lambda k: np.asarray(inp[k], np.float32)
    pvs = []
    w_in_r, bds = [], []
    for l in range(DEPTH):
        cols = np.zeros((128, NPV), np.float32)
        cols[:, G1:G1 + 8] = _pcol(f("norm1_g")[l], 8)
        cols[:, G2:G2 + 8] = _pcol(f("norm2_g")[l], 8)
        cols[:, G3:G3 + 8] = _pcol(f("norm3_g")[l], 8)
        scw = f("ssd_conv_w")[l]
        for c in range(7):
            for k in range(4):
                cols[:, SCW + c * 4 + k] = scw[k, c * 128:(c + 1) * 128]
        cols[:, SCB:SCB + 7] = _pcol(f("ssd_conv_b")[l], 7)
        cols[:, DSK:DSK + 3] = _pcol(np.repeat(f("ssd_d")[l], 64), 3)
        cols[:, SNG:SNG + 3] = _pcol(f("ssd_norm_g")[l], 3)
        lcw = f("lru_conv_w")[l]
        for c in range(2):
            for k in range(4):
                cols[:, LCW + c * 4 + k] = lcw[k, c * 128:(c + 1) * 128]
        cols[:, LCB:LCB + 2] = _pcol(f("lru_conv_b")[l], 2)
        cols[:, LBA:LBA + 2] = _pcol(f("lru_b_a")[l], 2)
        cols[:, LBX:LBX + 2] = _pcol(f("lru_b_x")[l], 2)
        cols[:, LAM:LAM + 2] = _pcol(f("lru_lambda")[l], 2)
        cols[:, LNG:LNG + 2] = _pcol(f("lru_norm_g")[l], 2)
        cols[:, FNG:FNG + 3] = _pcol(f("fox_norm_g")[l], 3)
        cols[:, BPG:BPG + 8] = _pcol(f("b_ple_gate")[l], 8)
        cols[:, DTB:DTB + 6] = np.broadcast_to(f("ssd_dt_bias")[l][None, :], (128, 6))
        cols[:, ALOG:ALOG + 6] = np.broadcast_to(f("ssd_a_log")[l][None, :], (128, 6))
        cols[:, FBF:FBF + 6] = np.broadcast_to(f("fox_b_f")[l][None, :], (128, 6))
        pvs.append(cols)
        w = f("w_in")[l]
        z, xbc, dt, lx, lg, q, k, v, fr = np.split(w, np.cumsum([384, 896, 6, 256, 256, 384, 384, 384])[:], axis=1)
        w_in_r.append(np.concatenate([z, xbc, lx, lg, q, k, v, dt, fr], axis=1))
        bd = np.zeros((128, 4, 128), np.float32)
        wa, wx = f("lru_w_a")[l], f("lru_w_x")[l]
        for c in range(2):
            for i in range(2):
                bd[i * 64:(i + 1) * 64, c, i * 64:(i + 1) * 64] = wa[2 * c + i]
                bd[i * 64:(i + 1) * 64, 2 + c, i * 64:(i + 1) * 64] = wx[2 * c + i]
        bds.append(bd.reshape(128, 512))
    pv = np.concatenate(pvs + [_pcol(f("final_norm_g"), 8)], axis=1)
    cf, cb, sel = _consts()
    shared = {
        "w_in": np.ascontiguousarray(np.stack(w_in_r)), "w_out": f("w_out")[:DEPTH], "bd": np.stack(bds),
        "w_gate": f("w_gate")[:DEPTH], "w_up": f("w_up")[:DEPTH], "w_down": f("w_down")[:DEPTH],
        "w_pg": f("w_ple_gate")[:DEPTH], "w_pp": f("w_ple_proj")[:DEPTH],
        "pv": np.ascontiguousarray(pv), "cf": cf, "cb": cb, "sel": sel,
    }
    return shared


_NC_CACHE = {}


def run(inp, NCORES, DEPTH):
    x = np.asarray(inp["x"], np.float32)
    p = np.asarray(inp["p"], np.float32)
    B, NT, _ = x.shape
    NSEQ = B // NCORES
    key = (NSEQ, NT, DEPTH)
    if key not in _NC_CACHE:
        PREFETCHED[0] = None
        _NC_CACHE[key] = build(NSEQ, NT, DEPTH)
    nc = _NC_CACHE[key]
    shared = _pack(inp, DEPTH)
    in_maps = []
    for c in range(NCORES):
        m = dict(shared)
        m["x"] = np.ascontiguousarray(x[c * NSEQ:(c + 1) * NSEQ])
        m["p"] = np.ascontiguousarray(p[:DEPTH, c * NSEQ:(c + 1) * NSEQ])
        in_maps.append(m)
    res = run_bass_kernel_spmd(nc, in_maps, core_ids=list(range(NCORES)))
    if KDUMP:
        global LAST_DBG
        LAST_DBG = res.results[0]["dbg"]
    return np.concatenate([r["y"] for r in res.results], axis=0)


def kernel(**inputs):
    return run(inputs, 8, 2)
```

```python
import contextlib
import numpy as np
import concourse.bass as bass
import concourse.mybir as mybir
from concourse.bass_utils import run_bass_kernel_spmd

F32 = mybir.dt.float32
BF16 = mybir.dt.bfloat16
AF = mybir.ActivationFunctionType
ALU = mybir.AluOpType

ENGS = ["tensor", "vector", "scalar", "gpsimd", "sync"]
SEM_CHUNK = 3000

D_MODEL = 1024
D_FF = 2816
IN_COLS = 2956
OZ, OXS, OB, OC, OLX, OLG, OQ, OK_, OV, ODT, OFR = 0, 384, 768, 1024, 1280, 1536, 1792, 2176, 2560, 2944, 2950
G1, G2, G3, SCW, SCB, DSK, SNG, LCW, LCB, LBA, LBX, LAM, LNG, FNG, BPG, DTB, ALOG, FBF, NPV = (
    0, 8, 16, 24, 52, 59, 62, 65, 73, 75, 77, 79, 81, 83, 86, 94, 100, 106, 112)
EPS = 1e-6
TB = 256


class Buf:
    __slots__ = ("name", "writer", "readers")

    def __init__(self, name=""):
        self.name = name
        self.writer = None
        self.readers = []


class Sched:
    def __init__(self, nc):
        self.nc = nc
        self.prog = {e: [] for e in ENGS}
        self.clock = {e: {} for e in ENGS}
        self.dma_count = []
        self.dma_last = []
        self.out_stamps = []
        self.pool = {}
        self.pool_i = {}

    def new_dma_sem(self):
        self.dma_count.append(0)
        self.dma_last.append(None)
        return len(self.dma_count) - 1

    def pool_sem(self, q, n=28):
        if q not in self.pool:
            self.pool[q] = [self.new_dma_sem() for _ in range(n)]
            self.pool_i[q] = 0
        k = self.pool[q][self.pool_i[q] % n]
        self.pool_i[q] += 1
        return k

    def _need(self, eng, stamp, waits):
        if stamp is None:
            return
        kind, key, val, snap = stamp
        if kind == "c" and key == eng and eng == "tensor":
            return
        ck = self.clock[eng]
        k = (kind, key)
        if ck.get(k, -1) >= val:
            return
        waits.append((kind, key, val))
        if kind == "c":
            self.prog[key][val]["inc"] = True
        for kk, vv in snap.items():
            if ck.get(kk, -1) < vv:
                ck[kk] = vv
        ck[k] = val

    enabled = True
    dead = False

    def op(self, eng, fn, reads=(), writes=(), dma_sem=None, is_out=False):
        if not self.enabled or self.dead:
            return None
        waits = []
        for b in reads:
            self._need(eng, b.writer, waits)
        for b in writes:
            self._need(eng, b.writer, waits)
            for r in b.readers:
                self._need(eng, r, waits)
        if dma_sem is not None:
            self._need(eng, self.dma_last[dma_sem], waits)
        idx = len(self.prog[eng])
        ent = {"waits": waits, "fn": fn, "inc": False, "dma": None}
        self.prog[eng].append(ent)
        snap = dict(self.clock[eng])
        if dma_sem is None:
            stamp = ("c", eng, idx, snap)
        else:
            self.dma_count[dma_sem] += 1
            val = 16 * self.dma_count[dma_sem]
            ent["dma"] = (dma_sem, val)
            stamp = ("d", dma_sem, val, snap)
            self.dma_last[dma_sem] = stamp
            if is_out:
                self.out_stamps.append(stamp)
        for b in reads:
            b.readers.append(stamp)
        for b in writes:
            b.writer = stamp
            b.readers = []
        return stamp

    def emit(self, final_eng="sync"):
        nc = self.nc
        waits = []
        for st in self.out_stamps:
            self._need(final_eng, st, waits)
        self.prog[final_eng].append({"waits": waits, "fn": None, "inc": False, "dma": None})
        with contextlib.ExitStack() as es:
            csem, rank = {}, {}
            for e in ENGS:
                r = 0
                rank[e] = {}
                for i, ent in enumerate(self.prog[e]):
                    if ent["inc"]:
                        rank[e][i] = r
                        r += 1
                nsem = (r + SEM_CHUNK - 1) // SEM_CHUNK
                csem[e] = [es.enter_context(nc.semaphore(f"c_{e}_{k}")) for k in range(nsem)]
            dsem = [es.enter_context(nc.semaphore(f"d_{k}")) for k in range(len(self.dma_count))]
            block = es.enter_context(nc.Block())

            def mk(e):
                def body(engh):
                    for i, ent in enumerate(self.prog[e]):
                        for (kind, key, val) in ent["waits"]:
                            if kind == "c":
                                r = rank[key][val]
                                engh.wait_ge(csem[key][r // SEM_CHUNK], r % SEM_CHUNK + 1)
                            else:
                                engh.wait_ge(dsem[key], val)
                        if ent["fn"] is None:
                            continue
                        ins = ent["fn"](engh)
                        if ent["dma"] is not None:
                            ins.then_inc(dsem[ent["dma"][0]], 16)
                        elif ent["inc"]:
                            r = rank[e][i]
                            ins.then_inc(csem[e][r // SEM_CHUNK], 1)
                return body

            for e in ENGS:
                if self.prog[e]:
                    getattr(block, e)(mk(e))


class Tl:
    __slots__ = ("ap", "b", "subs")

    def __init__(self, ap, b):
        self.ap = ap
        self.b = b
        self.subs = None

    def split(self, n):
        self.subs = [Buf() for _ in range(n)]
        for sb_ in self.subs:
            sb_.readers = list(self.b.readers)
        return self

    def R(self, i):
        return [self.b, self.subs[i]]


class Arena:
    def __init__(self, tensor_bf16, nbytes):
        self.t = tensor_bf16
        self.n = nbytes
        self.live = []

    def view(self, off, shape, dt, name=""):
        n = 1
        for s in shape:
            n *= s
        size = n * (4 if dt == F32 else 2)
        assert off % 4 == 0 and off + size <= self.n, (name, off, size, self.n)
        ap = self.t[:, off // 2:(off + size) // 2]
        if dt == F32:
            ap = ap.bitcast(F32)
        if len(shape) == 2:
            ap = ap.rearrange("p (a b) -> p a b", a=shape[0])
        elif len(shape) == 3:
            ap = ap.rearrange("p (a b c) -> p a b c", a=shape[0], b=shape[1])
        buf = Buf(name)
        newlive = []
        for (s, e, b) in self.live:
            if s < off + size and off < e:
                if b.writer is not None:
                    buf.readers.append(b.writer)
                buf.readers.extend(b.readers)
                if not (s >= off and e <= off + size):
                    newlive.append((s, e, b))
            else:
                newlive.append((s, e, b))
        newlive.append((off, off + size, buf))
        self.live = newlive
        return Tl(ap, buf)


class Bump:
    def __init__(self, arena, base):
        self.a = arena
        self.base = base
        self.off = base

    def reset(self):
        self.off = self.base

    def get(self, shape, dt, name=""):
        n = 1
        for s in shape:
            n *= s
        size = (n * (4 if dt == F32 else 2) + 31) // 32 * 32
        t = self.a.view(self.off, shape, dt, name)
        self.off += size
        return t


def build(NSEQ, NT, DEPTH):
    NBLK = NT // TB
    NT128 = NT // 128
    nc = bass.Bass("TRN2", target_bir_lowering=False)
    din = lambda name, shape: nc.dram_tensor(name, shape, F32, kind="ExternalInput").ap()
    x_d = din("x", [NSEQ, NT, D_MODEL])
    p_d = din("p", [DEPTH, NSEQ, NT, 256])
    win_d = din("w_in", [DEPTH, D_MODEL, IN_COLS])
    wout_d = din("w_out", [DEPTH, D_MODEL, D_MODEL])
    bd_d = din("bd", [DEPTH, 128, 4 * 128])
    wg_d = din("w_gate", [DEPTH, D_MODEL, D_FF])
    wu_d = din("w_up", [DEPTH, D_MODEL, D_FF])
    wd_d = din("w_down", [DEPTH, D_FF, D_MODEL])
    wpg_d = din("w_pg", [DEPTH, D_MODEL, D_MODEL])
    wpp_d = din("w_pp", [DEPTH, 256, D_MODEL])
    pv_d = din("pv", [128, DEPTH * NPV + 8])
    cf_d = din("cf", [128, 3 * 128])
    cb_d = din("cb", [128, 3 * 128])
    sel_d = din("sel", [6, 6 * 128])
    y_d = nc.dram_tensor("y", [NSEQ, NT, D_MODEL], F32, kind="ExternalOutput").ap()
    if KDUMP:
        dbg_d = nc.dram_tensor("dbg", [128, 8192], F32, kind="ExternalOutput").ap()

    es = contextlib.ExitStack()
    with es:
        sb = lambda name, shape, dt: es.enter_context(nc.sbuf_tensor("s_" + name, shape, dt))
        S = Sched(nc)
        GF = DEPTH * NPV

        def static(name, shape, dt):
            return Tl(sb(name, shape, dt)[:], Buf(name))

        hT = sb("hT", [128, 8, NT], F32)
        hTb = [Buf(f"hT{i}") for i in range(NBLK)]
        cf = static("cf", [128, 3, 128], F32)
        cb = static("cb", [128, 3, 128], BF16)
        pv = static("pv", [128, DEPTH * NPV + 8], F32)
        dv = static("dv", [128, DEPTH, 12], F32)
        assert TB == 256
        uT_f = static("uT", [128, 1024], F32)
        ycat_f = static("ycat", [128, 1024], F32)
        uT = Tl(uT_f.ap.bitcast(BF16).rearrange("p (c t) -> p c t", c=8), uT_f.b)
        ycat = Tl(ycat_f.ap.bitcast(BF16).rearrange("p (c t) -> p c t", c=8), ycat_f.b)
        sq = static("sq", [128, 8, TB], BF16)
        rs0 = static("rs0", [128, TB], F32)
        rs1 = static("rs1", [128, TB], F32)
        ARENA_BYTES = 125 * 1024
        arena_t = sb("arena", [128, ARENA_BYTES // 2], BF16)
        A = Arena(arena_t, ARENA_BYTES)
        ps_t = [es.enter_context(nc.psum_tensor(f"ps{i}", [128, 512], F32)) for i in range(7)]
        psb_t = es.enter_context(nc.psum_tensor("psb", [128, 1024], BF16))
        pbuf = [[Buf(f"ps{i}_{h}") for h in range(2)] for i in range(7)]
        psbb = Buf("psb")

        def PS(bank, c0, c1, p0=0, p1=128):
            bs = [pbuf[bank][0], pbuf[bank][1]]
            return ps_t[bank][p0:p1, c0:c1], bs

        ident_f = cf.ap[:, 0, :]
        tri_f = cf.ap[:, 1, :]
        ones_f = cf.ap[:, 2, :]
        ident_b = cb.ap[:, 0, :]
        ones_b = cb.ap[:, 1, :]
        mneg_b = cb.ap[:, 2, :]

        def bl(xs):
            out = []
            for x_ in xs:
                if isinstance(x_, Tl):
                    out.append(x_.b)
                elif isinstance(x_, (list, tuple)):
                    out.extend(bl(x_))
                elif x_ is not None:
                    out.append(x_)
            return out

        def mm(out, lhsT, rhs, start, stop, r, w):
            S.op("tensor", lambda e: e.matmul(out, lhsT=lhsT, rhs=rhs, start=start, stop=stop), bl(r), bl(w))

        def tr(out, in_, ident, r, w):
            S.op("tensor", lambda e: e.transpose(out, in_, ident), bl(r), bl(w))

        def act(out, in_, func, r, w, bias=None, scale=None):
            kw = {}
            if bias is not None:
                kw["bias"] = bias
            if scale is not None:
                kw["scale"] = scale
            S.op("scalar", lambda e: e.activation(out=out, in_=in_, func=func, **kw), bl(r), bl(w))

        def tt(eng, out, in0, in1, op, r, w):
            S.op(eng, lambda e: e.tensor_tensor(out=out, in0=in0, in1=in1, op=op), bl(r), bl(w))

        def ts(eng, out, in0, s1, s2, op0, op1, r, w):
            if op1 is None:
                S.op(eng, lambda e: e.tensor_scalar(out=out, in0=in0, scalar1=s1, scalar2=None, op0=op0), bl(r), bl(w))
            else:
                S.op(eng, lambda e: e.tensor_scalar(out=out, in0=in0, scalar1=s1, scalar2=s2, op0=op0, op1=op1), bl(r), bl(w))

        def stt(out, in0, scalar, in1, op0, op1, r, w):
            S.op("vector", lambda e: e.scalar_tensor_tensor(out=out, in0=in0, scalar=scalar, in1=in1, op0=op0, op1=op1), bl(r), bl(w))

        def cp(eng, out, in_, r, w):
            if eng == "scalar":
                act(out, in_, AF.Copy, r, w)
            else:
                S.op(eng, lambda e: e.tensor_copy(out=out, in_=in_), bl(r), bl(w))

        def mset(eng, ap, val, w):
            S.op(eng, lambda e: e.memset(ap, val), [], bl(w))

        def scan(out, d0, d1, init, r, w):
            S.op("vector", lambda e: e.tensor_tensor_scan(out=out, data0=d0, data1=d1, initial=init, op0=ALU.mult, op1=ALU.add),
                 bl(r), bl(w))

        def dma(q, out, in_, r, w, is_out=False, cast=False):
            kw = {"max_dma_last_dim": 8192} if cast else {}
            return S.op(q, lambda e: e.dma_start(out=out, in_=in_, **kw), bl(r), bl(w), dma_sem=S.pool_sem(q), is_out=is_out)

        dbg_state = [0]
        if KDUMP:
            dbgt = static("dbgt", [128, 8192], F32)

        def dump(name, ap, r, p0=0, p1=128):
            if not KDUMP or name in DUMPS or S.dead or not S.enabled:
                return
            n = ap.shape[-1]
            c0 = dbg_state[0]
            DUMPS[name] = (c0, n, p0, p1)
            dbg_state[0] += n
            S.op("vector", lambda e: e.tensor_copy(out=dbgt.ap[p0:p1, c0:c0 + n], in_=ap), bl(r), [dbgt.b])

        dma("sync", cf.ap, cf_d.rearrange("p (a b) -> p a b", a=3), [], [cf])
        dma("sync", pv.ap, pv_d, [], [pv])
        dma("gpsimd", cb.ap, cb_d.rearrange("p (a b) -> p a b", a=3), [], [cb], cast=True)
        for l in range(DEPTH):
            o = l * NPV
            act(dv.ap[:, l, 0:6], pv.ap[:, o + ALOG:o + ALOG + 6], AF.Exp, [pv], [dv])
            ts("vector", dv.ap[:, l, 0:6], dv.ap[:, l, 0:6], -1.0, None, ALU.mult, None, [dv], [dv])
            act(dv.ap[:, l, 6:8], pv.ap[:, o + LAM:o + LAM + 2], AF.Exp, [pv], [dv], scale=-1.0)
            act(dv.ap[:, l, 6:8], dv.ap[:, l, 6:8], AF.Ln, [dv], [dv], bias=1.0)
            ts("vector", dv.ap[:, l, 6:8], dv.ap[:, l, 6:8], -8.0, None, ALU.mult, None, [dv], [dv])
            ts("vector", dv.ap[:, l, 8:10], pv.ap[:, o + LBA:o + LBA + 2], -1.0, None, ALU.mult, None, [pv], [dv])
            ts("vector", dv.ap[:, l, 10:12], pv.ap[:, o + LBX:o + LBX + 2], -1.0, None, ALU.mult, None, [pv], [dv])

        if KSTOP <= 1:
            S.dead = True
        def rmsnorm(src_aps, src_bufs, gcol0, nch, width, out_aps, out_tl, ncols, psum_loc):
            bank, c0 = psum_loc
            pso, psb_ = PS(bank, c0, c0 + ncols)
            for c in range(nch):
                eng = "gpsimd" if c % 2 == 0 else "vector"
                tt(eng, sq.ap[:, c, 0:ncols], src_aps[c], src_aps[c], ALU.mult, src_bufs, [sq])
            for c in range(nch):
                mm(pso, ones_b, sq.ap[:, c, 0:ncols], c == 0, c == nch - 1, [cb, sq], psb_)
            act(rs0.ap[:, 0:ncols], pso, AF.Ln, psb_, [rs0], bias=EPS_AP, scale=1.0 / width)
            act(rs1.ap[:, 0:ncols], rs0.ap[:, 0:ncols], AF.Exp, [rs0], [rs1], scale=-0.5)
            for c in range(nch):
                stt(out_aps[c], src_aps[c], pv.ap[:, gcol0 + c:gcol0 + c + 1], rs1.ap[:, 0:ncols], ALU.mult, ALU.mult,
                    src_bufs + [pv, rs1], [out_tl])

        epst = static("epst", [128, 1], F32)
        mset("vector", epst.ap, EPS, [epst])
        EPS_AP = epst.ap[:, 0:1]

        MP = Bump(A, 0)

        def mixer_persist():
            MP.reset()
            d = {}
            d["win"] = MP.get([8, IN_COLS], BF16, "win")
            d["wout"] = MP.get([8, D_MODEL], BF16, "wout")
            d["bd"] = MP.get([4, 128], BF16, "bd")
            d["KT"] = MP.get([3, NT], BF16, "KT")
            d["V"] = MP.get([NT128, 384], BF16, "V")
            d["ncum"] = MP.get([NT128, 6], F32, "ncum")
            d["xraw"] = MP.get([7, TB + 3], F32, "xraw")
            d["lraw"] = MP.get([2, TB + 3], F32, "lraw")
            d["st"] = MP.get([384], F32, "st")
            d["stb"] = MP.get([384], BF16, "stb")
            d["lcar"] = MP.get([2], F32, "lcar")
            d["fcar"] = MP.get([6], F32, "fcar")
            d["fref"] = MP.get([6], F32, "fref")
            return d

        _tmp = mixer_persist()
        SCR_BASE = MP.off
        A.live = []
        SC = Bump(A, SCR_BASE)
        FFN_B = Bump(A, 0)
        PLE_BASE = SCR_BASE
        PLE_B = Bump(A, PLE_BASE)

        def load_mixer_weights(l):
            d = mixer_persist()
            d["win"].split(8)
            d["wout"].split(8)
            for c in range(8):
                dma("gpsimd", d["win"].ap[:, c, :], win_d[l, c * 128:(c + 1) * 128, :], [], [d["win"].subs[c]], cast=True)
            for c in range(8):
                dma("gpsimd", d["wout"].ap[:, c, :], wout_d[l, c * 128:(c + 1) * 128, :], [], [d["wout"].subs[c]], cast=True)
            dma("gpsimd", d["bd"].ap, bd_d[l].rearrange("p (a b) -> p a b", a=4), [], [d["bd"]], cast=True)
            return d

        for s in range(NSEQ):
            SC.reset()
            xst = [uT_f, ycat_f]
            for t in range(NT128):
                st_ = xst[t % 2]
                dma("sync", st_.ap, x_d[s, t * 128:(t + 1) * 128, :], [], [st_])
                for half in range(2):
                    bank = (2 * t + half) % 4
                    pso, pbs = PS(bank, 0, 512)
                    for j in range(4):
                        c = half * 4 + j
                        tr(ps_t[bank][:, j * 128:(j + 1) * 128], st_.ap[:, c * 128:(c + 1) * 128], ident_f, [st_, cf], pbs)
                    eng = "vector" if half == 0 else "scalar"
                    cp(eng, hT[:, half * 4:half * 4 + 4, t * 128:(t + 1) * 128],
                       pso.rearrange("p (a b) -> p a b", a=4), pbs, [hTb[(t * 128) // TB]])

            if KSTOP <= 2:
                S.dead = True
            for l in range(DEPTH):
                o = l * NPV
                W = PREFETCHED[0] if PREFETCHED[0] is not None else load_mixer_weights(l)
                PREFETCHED[0] = None
                win, wout, bdm = W["win"], W["wout"], W["bd"]
                KT, V, ncum, xraw, lraw = W["KT"], W["V"], W["ncum"], W["xraw"], W["lraw"]
                st, stb, lcar, fcar = W["st"], W["stb"], W["lcar"], W["fcar"]
                fref = W["fref"]
                mset("vector", xraw.ap[:, :, 0:3], 0.0, [xraw])
                mset("vector", lraw.ap[:, :, 0:3], 0.0, [lraw])
                mset("gpsimd", st.ap, 0.0, [st])
                mset("gpsimd", stb.ap, 0.0, [stb])
                mset("gpsimd", lcar.ap, 0.0, [lcar])
                mset("gpsimd", fcar.ap, 0.0, [fcar])

                for blk in range(NBLK):
                    t0 = blk * TB
                    hb = hTb[blk]
                    hsl = [hT[:, c, t0:t0 + TB] for c in range(8)]
                    rmsnorm(hsl, [hb], o + G1, 8, D_MODEL, [uT.ap[:, c, :] for c in range(8)], uT, TB, (6, 256))

                    slot = [0]

                    def proj(col0, ncolsM=128):
                        i = slot[0]
                        slot[0] += 1
                        bank, half = i % 2, (i // 2) % 2
                        pso, pbs = PS(bank, half * 256, half * 256 + 256)
                        for c in range(8):
                            mm(pso, win.ap[:, c, col0:col0 + ncolsM], uT.ap[:, c, :], c == 0, c == 7, win.R(c) + [uT], pbs)
                        return pso, pbs

                    SC.reset()
                    if "ssd" not in DBG:
                        for c in range(3):
                            mset("vector", ycat.ap[:, c, :], 0.0, [ycat])
                    S.enabled = "ssd" in DBG
                    xbc = SC.get([7, TB], BF16, "xbc")
                    zs = SC.get([3, TB], F32, "zs")
                    yssd = SC.get([3, TB], F32, "yssd")
                    cacc = [SC.get([TB], F32, f"cacc{i}") for i in range(2)]
                    sm = SC.get([8, 6], F32, "sm")
                    xdt = SC.get([384], BF16, "xdt")
                    xdts = SC.get([384], BF16, "xdts")
                    btok = SC.get([256], BF16, "btok")
                    Dm = [SC.get([128], F32, f"D{i}") for i in range(2)]
                    MT = [SC.get([128], BF16, f"MT{i}") for i in range(2)]
                    ebc = [SC.get([128], F32, f"ebc{i}") for i in range(2)]
                    Cs = [SC.get([128], BF16, f"Cs{i}") for i in range(2)]
                    stt_tmp = SC.get([384], F32, "sttmp")
                    smp = SC.get([24], F32, "smp")

                    for c in range(3):
                        pso, pbs = proj(OZ + c * 128)
                        act(zs.ap[:, c, :], pso, AF.Silu, pbs, [zs])
                    xbc.split(7)
                    xbc.b.readers = []
                    xraw_s = [Buf() for _ in range(7)]
                    for sb_ in xraw_s:
                        sb_.writer = xraw.b.writer
                        sb_.readers = list(xraw.b.readers)

                    def silu_c(c):
                        act(xbc.ap[:, c, :], cacc[c % 2].ap, AF.Silu, [cacc[c % 2]], [xbc.subs[c]])

                    for c in range(7):
                        pso, pbs = proj(OXS + c * 128)
                        cp("scalar", xraw.ap[:, c, 3:3 + TB], pso, pbs, [xraw_s[c]])
                        if c > 0:
                            silu_c(c - 1)
                        ca = cacc[c % 2]
                        wc = lambda k: pv.ap[:, o + SCW + c * 4 + k:o + SCW + c * 4 + k + 1]
                        ts("vector", ca.ap, xraw.ap[:, c, 0:TB], wc(0), pv.ap[:, o + SCB + c:o + SCB + c + 1], ALU.mult, ALU.add,
                           [xraw_s[c], pv], [ca])
                        for k in range(1, 4):
                            stt(ca.ap, xraw.ap[:, c, k:k + TB], wc(k), ca.ap, ALU.mult, ALU.add, [xraw_s[c], pv, ca], [ca])
                    silu_c(6)
                    cp("gpsimd", xraw.ap[:, :, 0:3], xraw.ap[:, :, TB:TB + 3], xraw_s, xraw_s + [xraw])
                    cp("vector", fref.ap, fcar.ap, [fcar], [fref])
                    for j in range(TB // 128):
                        cs = slice(j * 128, (j + 1) * 128)
                        tg = t0 + j * 128
                        pso, pbs = PS(2, 0, 396)
                        for c in range(8):
                            mm(pso, uT.ap[:, c, cs], win.ap[:, c, OV:OV + 396], c == 0, c == 7, win.R(c) + [uT], pbs)
                        kb = tg // 128
                        cp("vector", V.ap[:, kb, :], ps_t[2][:, 0:384], pbs, [V])
                        dt, adt, nacs, tmp6, decs, cdec, sdt, nlf = [sm.ap[:, i, :] for i in range(8)]
                        tt("vector", tmp6, ps_t[2][:, 384:390], pv.ap[:, o + DTB:o + DTB + 6], ALU.add, pbs + [pv], [sm])
                        tt("vector", nlf, ps_t[2][:, 390:396], pv.ap[:, o + FBF:o + FBF + 6], ALU.add, pbs + [pv], [sm])
                        act(tmp6, tmp6, AF.Exp, [sm], [sm])
                        act(nlf, nlf, AF.Exp, [sm], [sm], scale=-1.0)
                        act(dt, tmp6, AF.Ln, [sm], [sm], bias=1.0)
                        act(nlf, nlf, AF.Ln, [sm], [sm], bias=1.0)
                        tt("vector", adt, dt, dv.ap[:, l, 0:6], ALU.mult, [sm, dv], [sm])
                        p4, p4b = PS(4, 384, 408)
                        mm(ps_t[4][:, 384:390], tri_f, adt, True, True, [cf, sm], p4b)
                        mm(ps_t[4][:, 390:396], ones_f, adt, True, True, [cf, sm], p4b)
                        mm(ps_t[4][:, 396:402], tri_f, nlf, True, True, [cf, sm], p4b)
                        mm(ps_t[4][:, 402:408], ones_f, nlf, True, True, [cf, sm], p4b)
                        cp("vector", smp.ap, ps_t[4][:, 384:408], p4b, [smp])
                        ts("vector", nacs, smp.ap[:, 0:6], -1.0, None, ALU.mult, None, [smp], [sm])
                        tt("vector", tmp6, smp.ap[:, 6:12], nacs, ALU.add, [smp, sm], [sm])
                        act(decs, tmp6, AF.Exp, [sm], [sm])
                        act(cdec, smp.ap[:, 6:12], AF.Exp, [smp], [sm])
                        tt("vector", sdt, dt, decs, ALU.mult, [sm], [sm])
                        tt("vector", ncum.ap[:, kb, :], smp.ap[:, 12:18], fcar.ap, ALU.add, [smp, fcar], [ncum])
                        tt("vector", fcar.ap, smp.ap[:, 18:24], fcar.ap, ALU.add, [smp, fcar], [fcar])
                        for c in range(5):
                            tr(psb_t[:, c * 128:(c + 1) * 128], xbc.ap[:, c, cs], ident_b, xbc.R(c) + [cb], [psbb])
                        xs_tok = psb_t[:, 0:384].rearrange("p (h e) -> p h e", h=6)
                        tt("vector", xdt.ap.rearrange("p (h e) -> p h e", h=6), xs_tok, dt.unsqueeze(2).to_broadcast([128, 6, 64]),
                           ALU.mult, [psbb, sm], [xdt])
                        tt("vector", xdts.ap.rearrange("p (h e) -> p h e", h=6), xs_tok, sdt.unsqueeze(2).to_broadcast([128, 6, 64]),
                           ALU.mult, [psbb, sm], [xdts])
                        cp("vector", btok.ap, psb_t[:, 384:640], [psbb], [btok])
                        pg, pgb = PS(3, 0, 256)
                        for g in range(2):
                            mm(ps_t[3][:, g * 128:(g + 1) * 128], xbc.ap[:, 3 + g, cs], xbc.ap[:, 5 + g, cs], True, True, xbc.R(3 + g) + xbc.R(5 + g), pgb)
                        py, pyb = PS(4, 0, 384)

                        def stageA(h):
                            g = h // 3
                            i2 = h % 2
                            eb = 6 if i2 == 0 else 2
                            pE, pEb = PS(eb, 0, 256)
                            adt_b = adt[:, h:h + 1].to_broadcast([128, 128])
                            mm(ps_t[eb][:, 0:128], adt_b, tri_f, True, False, [sm, cf], pEb)
                            mm(ps_t[eb][:, 0:128], ident_b, mneg_b, False, True, [cb], pEb)
                            mm(ps_t[eb][:, 128:256], adt_b, tri_f, True, True, [sm, cf], pEb)
                            act(Dm[i2].ap, ps_t[eb][:, 0:128], AF.Exp, pEb + [sm], [Dm[i2]], bias=nacs[:, h:h + 1])
                            act(ebc[i2].ap, ps_t[eb][:, 128:256], AF.Exp, pEb, [ebc[i2]])
                            tt("vector", MT[i2].ap, ps_t[3][:, g * 128:(g + 1) * 128], Dm[i2].ap, ALU.mult, pgb + [Dm[i2]], [MT[i2]])
                            tt("gpsimd", Cs[i2].ap, xbc.ap[:, 5 + g, cs], ebc[i2].ap, ALU.mult, xbc.R(5 + g) + [ebc[i2]], [Cs[i2]])

                        def stageB(h):
                            i2 = h % 2
                            hp = (h % 2) * 64
                            yo = ps_t[4][hp:hp + 64, (h // 2) * 128:(h // 2) * 128 + 128]
                            mm(yo, xdt.ap[:, h * 64:(h + 1) * 64], MT[i2].ap, True, False, [xdt, MT[i2]], pyb)
                            mm(yo, stb.ap[:, h * 64:(h + 1) * 64], Cs[i2].ap, False, True, [stb, Cs[i2]], pyb)

                        stageA(0)
                        for h in range(6):
                            if h + 1 < 6:
                                stageA(h + 1)
                            stageB(h)
                        for c in range(3):
                            stt(yssd.ap[:, c, cs], xbc.ap[:, c, cs], pv.ap[:, o + DSK + c:o + DSK + c + 1],
                                ps_t[4][:, c * 128:(c + 1) * 128], ALU.mult, ALU.add, xbc.R(c) + [pv] + pyb, [yssd])
                        pst, pstb = PS(5, 0, 384)
                        for g in range(2):
                            mm(ps_t[5][:, g * 192:(g + 1) * 192], btok.ap[:, g * 128:(g + 1) * 128], xdts.ap[:, g * 192:(g + 1) * 192],
                               True, True, [btok, xdts], pstb)
                        tt("vector", stt_tmp.ap.rearrange("p (h e) -> p h e", h=6), st.ap.rearrange("p (h e) -> p h e", h=6),
                           cdec.unsqueeze(2).to_broadcast([128, 6, 64]), ALU.mult, [st, sm], [stt_tmp])
                        tt("vector", st.ap, stt_tmp.ap, pst, ALU.add, [stt_tmp] + pstb, [st])
                        cp("gpsimd", stb.ap, st.ap, [st], [stb])
                    dump("yssd1_pre", yssd.ap[:, 1, :], [yssd])
                    dump("zs1", zs.ap[:, 1, :], [zs])
                    for c in range(3):
                        tt("gpsimd", yssd.ap[:, c, :], yssd.ap[:, c, :], zs.ap[:, c, :], ALU.mult, [yssd, zs], [yssd])
                    rmsnorm([yssd.ap[:, c, :] for c in range(3)], [yssd], o + SNG, 3, 384,
                            [ycat.ap[:, c, :] for c in range(3)], ycat, TB, (6, 256))

                    dump("ycat1_early", ycat.ap[:, 1, :], [ycat])
                    dump("yssd1", yssd.ap[:, 1, :], [yssd])
                    S.enabled = True
                    SC.reset()
                    LT = [{k: SC.get([TB], BF16 if k == "xlb" else F32, f"{k}{c}") for k in ("xl", "xlb", "rg", "ig", "aa", "mu", "hl", "g1", "g2")}
                          for c in range(2)]
                    lraw_s = [Buf() for _ in range(2)]
                    for sb_ in lraw_s:
                        sb_.writer = lraw.b.writer
                        sb_.readers = list(lraw.b.readers)
                    pgl = [None, None]

                    def l1(c):
                        T_ = LT[c]
                        pso, pbs = proj(OLX + c * 128)
                        cp("scalar", lraw.ap[:, c, 3:3 + TB], pso, pbs, [lraw_s[c]])
                        pgl[c] = proj(OLG + c * 128)
                        cp("scalar", T_["g1"].ap, pgl[c][0], pgl[c][1], [T_["g1"]])

                    def l2(c):
                        T_ = LT[c]
                        wc = lambda k: pv.ap[:, o + LCW + c * 4 + k:o + LCW + c * 4 + k + 1]
                        ts("vector", T_["xl"].ap, lraw.ap[:, c, 0:TB], wc(0), pv.ap[:, o + LCB + c:o + LCB + c + 1], ALU.mult, ALU.add,
                           [lraw_s[c], pv], [T_["xl"]])
                        for k in range(1, 4):
                            stt(T_["xl"].ap, lraw.ap[:, c, k:k + TB], wc(k), T_["xl"].ap, ALU.mult, ALU.add, [lraw_s[c], pv, T_["xl"]], [T_["xl"]])
                        cp("gpsimd", T_["xlb"].ap, T_["xl"].ap, [T_["xl"]], [T_["xlb"]])
                        tt("gpsimd", T_["g2"].ap, T_["g1"].ap, T_["g1"].ap, ALU.mult, [T_["g1"]], [T_["g2"]])
                        ts("vector", T_["g2"].ap, T_["g2"].ap, 0.044715, 1.0, ALU.mult, ALU.add, [T_["g2"]], [T_["g2"]])
                        tt("gpsimd", T_["g2"].ap, T_["g2"].ap, T_["g1"].ap, ALU.mult, [T_["g2"], T_["g1"]], [T_["g2"]])

                    def sig3(out_ap, in_ap, r, w, scale, bias=None):
                        act(out_ap, in_ap, AF.Exp, r, w, scale=scale, bias=bias)
                        act(out_ap, out_ap, AF.Ln, w, w, bias=1.0)
                        act(out_ap, out_ap, AF.Exp, w, w, scale=-1.0)

                    def l3(c):
                        T_ = LT[c]
                        pa, pab = PS(2, c * 256, c * 256 + 256)
                        mm(pa, bdm.ap[:, c, :], T_["xlb"].ap, True, True, [bdm, T_["xlb"]], pab)
                        sig3(T_["rg"].ap, pa, pab + [dv], [T_["rg"]], -1.0, dv.ap[:, l, 8 + c:9 + c])
                        mm(pa, bdm.ap[:, 2 + c, :], T_["xlb"].ap, True, True, [bdm, T_["xlb"]], pab)
                        sig3(T_["ig"].ap, pa, pab + [dv], [T_["ig"]], -1.0, dv.ap[:, l, 10 + c:11 + c])
                        sig3(T_["g2"].ap, T_["g2"].ap, [T_["g2"]], [T_["g2"]], -1.5957691216057308)

                    def l4(c):
                        T_ = LT[c]
                        ts("vector", T_["aa"].ap, T_["rg"].ap, dv.ap[:, l, 6 + c:7 + c], None, ALU.mult, None, [T_["rg"], dv], [T_["aa"]])
                        act(T_["aa"].ap, T_["aa"].ap, AF.Exp, [T_["aa"]], [T_["aa"]])
                        tt("gpsimd", T_["ig"].ap, T_["ig"].ap, T_["xl"].ap, ALU.mult, [T_["ig"], T_["xl"]], [T_["ig"]])
                        tt("gpsimd", T_["g2"].ap, T_["g2"].ap, T_["g1"].ap, ALU.mult, [T_["g2"], T_["g1"]], [T_["g2"]])

                    def l5(c):
                        T_ = LT[c]
                        tt("gpsimd", T_["mu"].ap, T_["aa"].ap, T_["aa"].ap, ALU.mult, [T_["aa"]], [T_["mu"]])
                        act(T_["mu"].ap, T_["mu"].ap, AF.Ln, [T_["mu"]], [T_["mu"]], bias=1.0, scale=-1.0)
                        act(T_["mu"].ap, T_["mu"].ap, AF.Exp, [T_["mu"]], [T_["mu"]], scale=0.5)

                    def l6(c):
                        T_ = LT[c]
                        tt("vector", T_["mu"].ap, T_["mu"].ap, T_["ig"].ap, ALU.mult, [T_["mu"], T_["ig"]], [T_["mu"]])
                        scan(T_["hl"].ap, T_["aa"].ap, T_["mu"].ap, lcar.ap[:, c:c + 1], [T_["aa"], T_["mu"], lcar], [T_["hl"]])
                        cp("vector", lcar.ap[:, c:c + 1], T_["hl"].ap[:, TB - 1:TB], [T_["hl"]], [lcar])
                        tt("vector", T_["hl"].ap, T_["hl"].ap, T_["g2"].ap, ALU.mult, [T_["hl"], T_["g2"]], [T_["hl"]])

                    def lru_gen():
                        for st_fn in (l1, l2, l3, l4, l5, l6):
                            for c in range(2):
                                st_fn(c)
                                yield
                            if st_fn is l2:
                                cp("gpsimd", lraw.ap[:, :, 0:3], lraw.ap[:, :, TB:TB + 3], lraw_s, lraw_s + [lraw])
                        hl_aps = [LT[c]["hl"].ap for c in range(2)]
                        hl_bufs = [LT[c]["hl"] for c in range(2)]
                        rmsnorm(hl_aps, hl_bufs, o + LNG, 2, 256,
                                [ycat.ap[:, 3 + c, :] for c in range(2)], ycat, TB, (2, 256))
                        yield

                    lgen = lru_gen()

                    qT = SC.get([3, TB], BF16, "qT")
                    pT = [SC.get([TB], BF16, f"pT{i}") for i in range(4)]
                    lnd = SC.get([TB], F32, "lnd")
                    yfox = SC.get([3, TB], F32, "yfox")
                    for c in range(3):
                        pso, pbs = proj(OQ + c * 128)
                        cp("scalar", qT.ap[:, c, :], pso, pbs, [qT])
                    for c in range(3):
                        pso, pbs = proj(OK_ + c * 128)
                        cp("scalar", KT.ap[:, c, t0:t0 + TB], pso, pbs, [KT])
                    nkb = (t0 + TB) // 128
                    nb = SC.get([NT128, 6], F32, "nb")
                    tt("vector", nb.ap[:, 0:nkb, :], ncum.ap[:, 0:nkb, :], fref.ap.unsqueeze(1).to_broadcast([128, nkb, 6]), ALU.subtract,
                       [ncum, fref], [nb])
                    pi = 0
                    for c in range(3):
                        py, pyb = PS(5, 0, 256)
                        pdn, pdb = PS(4, 0, 256)
                        its = [(hh, kb) for hh in range(2) for kb in range(nkb)]

                        def stA(idx, it):
                            hh, kb = it
                            h = 2 * c + hh
                            hp = hh * 64
                            rel = kb * 128 - t0
                            q0 = 0 if rel < 0 else rel
                            sbank = 6 if idx % 2 == 0 else 3
                            psS, psSb = PS(sbank, 0, 256)
                            so = ps_t[sbank][:, q0:TB]
                            diag = rel >= 0
                            mm(so, KT.ap[hp:hp + 64, c, kb * 128:(kb + 1) * 128], qT.ap[hp:hp + 64, c, q0:TB], True, not diag,
                               [KT, qT], psSb)
                            if diag:
                                mm(ps_t[sbank][:, q0:q0 + 128], ident_b, mneg_b, False, True, [cb], psSb)
                            pt = pT[idx % 4]
                            if q0 > 0:
                                mset("gpsimd", pt.ap[:, 0:q0], 0.0, [pt])
                            act(pt.ap[:, q0:TB], so, AF.Exp, psSb + [nb], [pt], bias=nb.ap[:, kb, h:h + 1], scale=0.125)

                        def stB(idx, it):
                            hh, kb = it
                            h = 2 * c + hh
                            hp = hh * 64
                            pt = pT[idx % 4]
                            first, last = kb == 0, kb == nkb - 1
                            mm(ps_t[5][hp:hp + 64, 0:TB], V.ap[:, kb, h * 64:(h + 1) * 64], pt.ap[:, 0:TB], first, last, [V, pt], pyb)
                            mm(ps_t[4][hp:hp + 64, 0:TB], ones_b[:, 0:64], pt.ap[:, 0:TB], first, last, [cb, pt], pdb)

                        stA(pi, its[0])
                        for i_, it in enumerate(its):
                            if i_ + 1 < len(its):
                                stA(pi + i_ + 1, its[i_ + 1])
                            stB(pi + i_, it)
                            next(lgen, None)
                        pi += len(its)
                        act(lnd.ap, pdn, AF.Ln, pdb, [lnd])
                        act(lnd.ap, lnd.ap, AF.Exp, [lnd], [lnd], scale=-1.0)
                        tt("vector", yfox.ap[:, c, :], py, lnd.ap, ALU.mult, pyb + [lnd], [yfox])
                    for _ in lgen:
                        pass
                    dump("qT0", qT.ap[:, 0, :], [qT])
                    dump("KT0", KT.ap[:, 0, 0:TB], [KT])
                    dump("nc0", ncum.ap[:, 0, :], [ncum])
                    dump("nc1", ncum.ap[:, 1, :], [ncum])
                    dump("V0", V.ap[:, 0, :], [V])
                    dump("V1", V.ap[:, 1, :], [V])
                    dump("yfox0", yfox.ap[:, 0, :], [yfox])
                    rmsnorm([yfox.ap[:, c, :] for c in range(3)], [yfox], o + FNG, 3, 384,
                            [ycat.ap[:, 5 + c, :] for c in range(3)], ycat, TB, (6, 256))

                    S.enabled = True
                    for c in range(8):
                        dump(f"ycat{c}", ycat.ap[:, c, :], [ycat])
                    for dc in range(8):
                        bank, half = dc % 2, (dc // 2) % 2
                        pso, pbs = PS(bank, half * 256, half * 256 + 256)
                        for c in range(8):
                            mm(pso, wout.ap[:, c, dc * 128:(dc + 1) * 128], ycat.ap[:, c, :], c == 0, c == 7, wout.R(c) + [ycat], pbs)
                        tt("vector", hsl[dc], hsl[dc], pso, ALU.add, [hb] + pbs, [hb])

                if KSTOP <= 3:
                    S.dead = True
                FFN_B.reset()
                uTs = FFN_B.get([8, NT], BF16, "uTs")
                wgt = [FFN_B.get([8, 512], BF16, f"wg{i}") for i in range(2)]
                wut = [FFN_B.get([8, 512], BF16, f"wu{i}") for i in range(2)]
                wdt = [FFN_B.get([4, D_MODEL], BF16, f"wd{i}") for i in range(2)]
                actb = FFN_B.get([4, 512], BF16, "actb")
                sgt = [FFN_B.get([512], F32, f"sg{i}") for i in range(2)]
                assert FFN_B.off <= PLE_BASE, (FFN_B.off, PLE_BASE)
                PLE_B.reset()
                wpg = PLE_B.get([8, D_MODEL], BF16, "wpg")
                wpp = PLE_B.get([2, D_MODEL], BF16, "wpp")
                pTt = PLE_B.get([2, TB], BF16, "pTt")
                pst_ = [PLE_B.get([256], F32, f"pst{i}") for i in range(2)]
                gat = [PLE_B.get([TB], F32, f"gat{i}") for i in range(2)]
                wpg.split(8)
                wpp.split(2)
                for c in range(8):
                    dma("gpsimd", wpg.ap[:, c, :], wpg_d[l, c * 128:(c + 1) * 128, :], [], [wpg.subs[c]], cast=True)
                for c in range(2):
                    dma("gpsimd", wpp.ap[:, c, :], wpp_d[l, c * 128:(c + 1) * 128, :], [], [wpp.subs[c]], cast=True)
                S.enabled = "ffn" in DBG
                for blk in range(NBLK):
                    t0 = blk * TB
                    rmsnorm([hT[:, c, t0:t0 + TB] for c in range(8)], [hTb[blk]], o + G2, 8, D_MODEL,
                            [uTs.ap[:, c, t0:t0 + TB] for c in range(8)], uTs, TB, (6, 256))
                groups = [(g * 512, 512) for g in range(5)] + [(2560, 256)]
                NT512 = NT // 512
                for gi, (f0, fw_) in enumerate(groups):
                    wg_, wu_, wd_ = wgt[gi % 2], wut[gi % 2], wdt[gi % 2]
                    nj = fw_ // 128
                    wg_.split(8)
                    wu_.split(8)
                    wd_.split(4)
                    wg_.b.readers = []
                    wu_.b.readers = []
                    wd_.b.readers = []
                    for c in range(8):
                        dma("gpsimd", wg_.ap[:, c, 0:fw_], wg_d[l, c * 128:(c + 1) * 128, f0:f0 + fw_], [], [wg_.subs[c]], cast=True)
                    for c in range(8):
                        dma("gpsimd", wu_.ap[:, c, 0:fw_], wu_d[l, c * 128:(c + 1) * 128, f0:f0 + fw_], [], [wu_.subs[c]], cast=True)
                    for j in range(nj):
                        dma("gpsimd", wd_.ap[:, j, :], wd_d[l, f0 + j * 128:f0 + (j + 1) * 128, :], [], [wd_.subs[j]], cast=True)
                    for tb4 in range(NT512):
                        tsl = slice(tb4 * 512, (tb4 + 1) * 512)
                        for j in range(nj):
                            pg_, pgb_ = PS(j % 2, 0, 512)
                            pu_, pub_ = PS(2 + j % 2, 0, 512)
                            for c in range(8):
                                mm(pg_, wg_.ap[:, c, j * 128:(j + 1) * 128], uTs.ap[:, c, tsl], c == 0, c == 7, wg_.R(c) + [uTs], pgb_)
                            for c in range(8):
                                mm(pu_, wu_.ap[:, c, j * 128:(j + 1) * 128], uTs.ap[:, c, tsl], c == 0, c == 7, wu_.R(c) + [uTs], pub_)
                            sg_ = sgt[j % 2]
                            act(sg_.ap, pg_, AF.Silu, pgb_, [sg_])
                            tt("vector", actb.ap[:, j, :], sg_.ap, pu_, ALU.mult, [sg_] + pub_, [actb])
                        for dc in range(8):
                            pd_, pdb_ = PS(4 + dc % 2, 0, 512)
                            for j in range(nj):
                                mm(pd_, wd_.ap[:, j, dc * 128:(dc + 1) * 128], actb.ap[:, j, :], j == 0, j == nj - 1, wd_.R(j) + [actb], pdb_)
                            hbs = [hTb[2 * tb4], hTb[2 * tb4 + 1]]
                            tt("vector", hT[:, dc, tsl], hT[:, dc, tsl], pd_, ALU.add, hbs + pdb_, hbs)

                S.enabled = True
                nxt = None
                if l + 1 < DEPTH:
                    nxt = l + 1
                elif s + 1 < NSEQ:
                    nxt = 0
                if nxt is not None:
                    PREFETCHED[0] = load_mixer_weights(nxt)
                S.enabled = "ple" in DBG
                for blk in range(NBLK):
                    t0 = blk * TB
                    hb = hTb[blk]
                    hsl = [hT[:, c, t0:t0 + TB] for c in range(8)]
                    ub = uT if blk % 2 == 0 else ycat
                    rmsnorm(hsl, [hb], o + G3, 8, D_MODEL, [ub.ap[:, c, :] for c in range(8)], ub, TB, (6, 256))
                    for j in range(TB // 128):
                        ps_ = pst_[j % 2]
                        dma("sync", ps_.ap, p_d[l, s, t0 + j * 128:t0 + (j + 1) * 128, :], [], [ps_])
                        pt_, ptb = PS(4, 0, 256)
                        for c in range(2):
                            tr(ps_t[4][:, c * 128:(c + 1) * 128], ps_.ap[:, c * 128:(c + 1) * 128], ident_f, [ps_, cf], ptb)
                        cp("scalar", pTt.ap[:, :, j * 128:(j + 1) * 128], pt_.rearrange("p (a b) -> p a b", a=2), ptb, [pTt])
                    for dc in range(8):
                        if KPLE < 2:
                            break
                        pg_, pgb_ = PS(dc % 2, 0, 256)
                        pp_, ppb_ = PS(2 + dc % 2, 0, 256)
                        for c in range(8):
                            mm(pg_, wpg.ap[:, c, dc * 128:(dc + 1) * 128], ub.ap[:, c, :], c == 0, c == 7, wpg.R(c) + [ub], pgb_)
                        for c in range(2):
                            mm(pp_, wpp.ap[:, c, dc * 128:(dc + 1) * 128], pTt.ap[:, c, :], c == 0, c == 1, wpp.R(c) + [pTt], ppb_)
                        if KPLE < 3:
                            continue
                        ga = gat[dc % 2]
                        act(ga.ap, pg_, AF.Sigmoid, pgb_ + [pv], [ga], bias=pv.ap[:, o + BPG + dc:o + BPG + dc + 1])
                        tt("vector", ga.ap, ga.ap, pp_, ALU.mult, [ga] + ppb_, [ga])
                        tt("vector", hsl[dc], hsl[dc], ga.ap, ALU.add, [hb, ga], [hb])

            S.enabled = True
            if KSTOP <= 4:
                S.dead = True
            SC.reset()
            onT = SC.get([8, TB], F32, "onT")
            ost = [uT_f, ycat_f]
            for blk in range(NBLK):
                t0 = blk * TB
                rmsnorm([hT[:, c, t0:t0 + TB] for c in range(8)], [hTb[blk]], GF, 8, D_MODEL,
                        [onT.ap[:, c, :] for c in range(8)], onT, TB, (6, 256))
                for j in range(TB // 128):
                    if KFIN < 2:
                        break
                    os_ = ost[j % 2]
                    for half in range(2):
                        bank = half
                        pso, pbs = PS(bank, 0, 512)
                        for q in range(4):
                            c = half * 4 + q
                            tr(ps_t[bank][:, q * 128:(q + 1) * 128], onT.ap[:, c, j * 128:(j + 1) * 128], ident_f, [onT, cf], pbs)
                        cp("vector" if half == 0 else "scalar", os_.ap[:, half * 512:(half + 1) * 512], pso, pbs, [os_])
                    if KFIN >= 3:
                        dma("sync", y_d[s, t0 + j * 128:t0 + (j + 1) * 128, :], os_.ap, [os_], [], is_out=True)
        if KDUMP:
            S.enabled = True
            S.dead = False
            dma("sync", dbg_d, dbgt.ap, [dbgt], [], is_out=True)
        S.emit()
    return nc


PREFETCHED = [None]
import os
DBG = set(os.environ.get("KDBG", "ssd,lru,fox,ffn,ple").split(","))
KSTOP = int(os.environ.get("KSTOP", "99"))
KFIN = int(os.environ.get("KFIN", "3"))
KPLE = int(os.environ.get("KPLE", "3"))
KDUMP = int(os.environ.get("KDUMP", "0"))
DUMPS = {}

def _consts():
    ident = np.eye(128, dtype=np.float32)
    tri = np.triu(np.ones((128, 128), np.float32))
    ones = np.ones((128, 128), np.float32)
    mneg = np.where(np.arange(128)[:, None] > np.arange(128)[None, :], np.float32(-30000.0), np.float32(0.0)).astype(np.float32)
    cf = np.concatenate([ident, tri, ones], axis=1)
    cb = np.concatenate([ident, ones, mneg], axis=1)
    sel = np.zeros((6, 6, 128), np.float32)
    for h in range(6):
        sel[h, h, :] = 1.0
    return cf, cb, sel.reshape(6, 768)


def _pcol(v, nch):
    return np.ascontiguousarray(np.asarray(v, np.float32).reshape(nch, 128).T)


def _pack(inp, DEPTH):
    f = lambda k: np.asarray(inp[k], np.float32)
    pvs = []
    w_in_r, bds = [], []
    for l in range(DEPTH):
        cols = np.zeros((128, NPV), np.float32)
        cols[:, G1:G1 + 8] = _pcol(f("norm1_g")[l], 8)
        cols[:, G2:G2 + 8] = _pcol(f("norm2_g")[l], 8)
        cols[:, G3:G3 + 8] = _pcol(f("norm3_g")[l], 8)
        scw = f("ssd_conv_w")[l]
        for c in range(7):
            for k in range(4):
                cols[:, SCW + c * 4 + k] = scw[k, c * 128:(c + 1) * 128]
        cols[:, SCB:SCB + 7] = _pcol(f("ssd_conv_b")[l], 7)
        cols[:, DSK:DSK + 3] = _pcol(np.repeat(f("ssd_d")[l], 64), 3)
        cols[:, SNG:SNG + 3] = _pcol(f("ssd_norm_g")[l], 3)
        lcw = f("lru_conv_w")[l]
        for c in range(2):
            for k in range(4):
                cols[:, LCW + c * 4 + k] = lcw[k, c * 128:(c + 1) * 128]
        cols[:, LCB:LCB + 2] = _pcol(f("lru_conv_b")[l], 2)
        cols[:, LBA:LBA + 2] = _pcol(f("lru_b_a")[l], 2)
        cols[:, LBX:LBX + 2] = _pcol(f("lru_b_x")[l], 2)
        cols[:, LAM:LAM + 2] = _pcol(f("lru_lambda")[l], 2)
        cols[:, LNG:LNG + 2] = _pcol(f("lru_norm_g")[l], 2)
        cols[:, FNG:FNG + 3] = _pcol(f("fox_norm_g")[l], 3)
        cols[:, BPG:BPG + 8] = _pcol(f("b_ple_gate")[l], 8)
        cols[:, DTB:DTB + 6] = np.broadcast_to(f("ssd_dt_bias")[l][None, :], (128, 6))
        cols[:, ALOG:ALOG + 6] = np.broadcast_to(f("ssd_a_log")[l][None, :], (128, 6))
        cols[:, FBF:FBF + 6] = np.broadcast_to(f("fox_b_f")[l][None, :], (128, 6))
        pvs.append(cols)
        w = f("w_in")[l]
        z, xbc, dt, lx, lg, q, k, v, fr = np.split(w, np.cumsum([384, 896, 6, 256, 256, 384, 384, 384])[:], axis=1)
        w_in_r.append(np.concatenate([z, xbc, lx, lg, q, k, v, dt, fr], axis=1))
        bd = np.zeros((128, 4, 128), np.float32)
        wa, wx = f("lru_w_a")[l], f("lru_w_x")[l]
        for c in range(2):
            for i in range(2):
                bd[i * 64:(i + 1) * 64, c, i * 64:(i + 1) * 64] = wa[2 * c + i]
                bd[i * 64:(i + 1) * 64, 2 + c, i * 64:(i + 1) * 64] = wx[2 * c + i]
        bds.append(bd.reshape(128, 512))
    pv = np.concatenate(pvs + [_pcol(f("final_norm_g"), 8)], axis=1)
    cf, cb, sel = _consts()
    shared = {
        "w_in": np.ascontiguousarray(np.stack(w_in_r)), "w_out": f("w_out")[:DEPTH], "bd": np.stack(bds),
        "w_gate": f("w_gate")[:DEPTH], "w_up": f("w_up")[:DEPTH], "w_down": f("w_down")[:DEPTH],
        "w_pg": f("w_ple_gate")[:DEPTH], "w_pp": f("w_ple_proj")[:DEPTH],
        "pv": np.ascontiguousarray(pv), "cf": cf, "cb": cb, "sel": sel,
    }
    return shared


_NC_CACHE = {}


def run(inp, NCORES, DEPTH):
    x = np.asarray(inp["x"], np.float32)
    p = np.asarray(inp["p"], np.float32)
    B, NT, _ = x.shape
    NSEQ = B // NCORES
    key = (NSEQ, NT, DEPTH)
    if key not in _NC_CACHE:
        PREFETCHED[0] = None
        _NC_CACHE[key] = build(NSEQ, NT, DEPTH)
    nc = _NC_CACHE[key]
    shared = _pack(inp, DEPTH)
    in_maps = []
    for c in range(NCORES):
        m = dict(shared)
        m["x"] = np.ascontiguousarray(x[c * NSEQ:(c + 1) * NSEQ])
        m["p"] = np.ascontiguousarray(p[:DEPTH, c * NSEQ:(c + 1) * NSEQ])
        in_maps.append(m)
    res = run_bass_kernel_spmd(nc, in_maps, core_ids=list(range(NCORES)))
    if KDUMP:
        global LAST_DBG
        LAST_DBG = res.results[0]["dbg"]
    return np.concatenate([r["y"] for r in res.results], axis=0)


def kernel(**inputs):
    return run(inputs, 8, 2)
```

```python
import contextlib
import numpy as np
import concourse.bass as bass
import concourse.mybir as mybir
from concourse.bass_utils import run_bass_kernel_spmd

F32 = mybir.dt.float32
BF16 = mybir.dt.bfloat16
AF = mybir.ActivationFunctionType
ALU = mybir.AluOpType

ENGS = ["tensor", "vector", "scalar", "gpsimd", "sync"]
SEM_CHUNK = 3000

D_MODEL = 1024
D_FF = 2816
IN_COLS = 2956
OZ, OXS, OB, OC, OLX, OLG, OQ, OK_, OV, ODT, OFR = 0, 384, 768, 1024, 1280, 1536, 1792, 2176, 2560, 2944, 2950
G1, G2, G3, SCW, SCB, DSK, SNG, LCW, LCB, LBA, LBX, LAM, LNG, FNG, BPG, DTB, ALOG, FBF, NPV = (
    0, 8, 16, 24, 52, 59, 62, 65, 73, 75, 77, 79, 81, 83, 86, 94, 100, 106, 112)
EPS = 1e-6
TB = 256


class Buf:
    __slots__ = ("name", "writer", "readers")

    def __init__(self, name=""):
        self.name = name
        self.writer = None
        self.readers = []


class Sched:
    def __init__(self, nc):
        self.nc = nc
        self.prog = {e: [] for e in ENGS}
        self.clock = {e: {} for e in ENGS}
        self.dma_count = []
        self.dma_last = []
        self.out_stamps = []
        self.pool = {}
        self.pool_i = {}

    def new_dma_sem(self):
        self.dma_count.append(0)
        self.dma_last.append(None)
        return len(self.dma_count) - 1

    def pool_sem(self, q, n=28):
        if q not in self.pool:
            self.pool[q] = [self.new_dma_sem() for _ in range(n)]
            self.pool_i[q] = 0
        k = self.pool[q][self.pool_i[q] % n]
        self.pool_i[q] += 1
        return k

    def _need(self, eng, stamp, waits):
        if stamp is None:
            return
        kind, key, val, snap = stamp
        if kind == "c" and key == eng and eng == "tensor":
            return
        ck = self.clock[eng]
        k = (kind, key)
        if ck.get(k, -1) >= val:
            return
        waits.append((kind, key, val))
        if kind == "c":
            self.prog[key][val]["inc"] = True
        for kk, vv in snap.items():
            if ck.get(kk, -1) < vv:
                ck[kk] = vv
        ck[k] = val

    enabled = True
    dead = False

    def op(self, eng, fn, reads=(), writes=(), dma_sem=None, is_out=False):
        if not self.enabled or self.dead:
            return None
        waits = []
        for b in reads:
            self._need(eng, b.writer, waits)
        for b in writes:
            self._need(eng, b.writer, waits)
            for r in b.readers:
                self._need(eng, r, waits)
        if dma_sem is not None:
            self._need(eng, self.dma_last[dma_sem], waits)
        idx = len(self.prog[eng])
        ent = {"waits": waits, "fn": fn, "inc": False, "dma": None}
        self.prog[eng].append(ent)
        snap = dict(self.clock[eng])
        if dma_sem is None:
            stamp = ("c", eng, idx, snap)
        else:
            self.dma_count[dma_sem] += 1
            val = 16 * self.dma_count[dma_sem]
            ent["dma"] = (dma_sem, val)
            stamp = ("d", dma_sem, val, snap)
            self.dma_last[dma_sem] = stamp
            if is_out:
                self.out_stamps.append(stamp)
        for b in reads:
            b.readers.append(stamp)
        for b in writes:
            b.writer = stamp
            b.readers = []
        return stamp

    def emit(self, final_eng="sync"):
        nc = self.nc
        waits = []
        for st in self.out_stamps:
            self._need(final_eng, st, waits)
        self.prog[final_eng].append({"waits": waits, "fn": None, "inc": False, "dma": None})
        with contextlib.ExitStack() as es:
            csem, rank = {}, {}
            for e in ENGS:
                r = 0
                rank[e] = {}
                for i, ent in enumerate(self.prog[e]):
                    if ent["inc"]:
                        rank[e][i] = r
                        r += 1
                nsem = (r + SEM_CHUNK - 1) // SEM_CHUNK
                csem[e] = [es.enter_context(nc.semaphore(f"c_{e}_{k}")) for k in range(nsem)]
            dsem = [es.enter_context(nc.semaphore(f"d_{k}")) for k in range(len(self.dma_count))]
            block = es.enter_context(nc.Block())

            def mk(e):
                def body(engh):
                    for i, ent in enumerate(self.prog[e]):
                        for (kind, key, val) in ent["waits"]:
                            if kind == "c":
                                r = rank[key][val]
                                engh.wait_ge(csem[key][r // SEM_CHUNK], r % SEM_CHUNK + 1)
                            else:
                                engh.wait_ge(dsem[key], val)
                        if ent["fn"] is None:
                            continue
                        ins = ent["fn"](engh)
                        if ent["dma"] is not None:
                            ins.then_inc(dsem[ent["dma"][0]], 16)
                        elif ent["inc"]:
                            r = rank[e][i]
                            ins.then_inc(csem[e][r // SEM_CHUNK], 1)
                return body

            for e in ENGS:
                if self.prog[e]:
                    getattr(block, e)(mk(e))


class Tl:
    __slots__ = ("ap", "b", "subs")

    def __init__(self, ap, b):
        self.ap = ap
        self.b = b
        self.subs = None

    def split(self, n):
        self.subs = [Buf() for _ in range(n)]
        for sb_ in self.subs:
            sb_.readers = list(self.b.readers)
        return self

    def R(self, i):
        return [self.b, self.subs[i]]


class Arena:
    def __init__(self, tensor_bf16, nbytes):
        self.t = tensor_bf16
        self.n = nbytes
        self.live = []

    def view(self, off, shape, dt, name=""):
        n = 1
        for s in shape:
            n *= s
        size = n * (4 if dt == F32 else 2)
        assert off % 4 == 0 and off + size <= self.n, (name, off, size, self.n)
        ap = self.t[:, off // 2:(off + size) // 2]
        if dt == F32:
            ap = ap.bitcast(F32)
        if len(shape) == 2:
            ap = ap.rearrange("p (a b) -> p a b", a=shape[0])
        elif len(shape) == 3:
            ap = ap.rearrange("p (a b c) -> p a b c", a=shape[0], b=shape[1])
        buf = Buf(name)
        newlive = []
        for (s, e, b) in self.live:
            if s < off + size and off < e:
                if b.writer is not None:
                    buf.readers.append(b.writer)
                buf.readers.extend(b.readers)
                if not (s >= off and e <= off + size):
                    newlive.append((s, e, b))
            else:
                newlive.append((s, e, b))
        newlive.append((off, off + size, buf))
        self.live = newlive
        return Tl(ap, buf)


class Bump:
    def __init__(self, arena, base):
        self.a = arena
        self.base = base
        self.off = base

    def reset(self):
        self.off = self.base

    def get(self, shape, dt, name=""):
        n = 1
        for s in shape:
            n *= s
        size = (n * (4 if dt == F32 else 2) + 31) // 32 * 32
        t = self.a.view(self.off, shape, dt, name)
        self.off += size
        return t


def build(NSEQ, NT, DEPTH):
    NBLK = NT // TB
    NT128 = NT // 128
    nc = bass.Bass("TRN2", target_bir_lowering=False)
    din = lambda name, shape: nc.dram_tensor(name, shape, F32, kind="ExternalInput").ap()
    x_d = din("x", [NSEQ, NT, D_MODEL])
    p_d = din("p", [DEPTH, NSEQ, NT, 256])
    win_d = din("w_in", [DEPTH, D_MODEL, IN_COLS])
    wout_d = din("w_out", [DEPTH, D_MODEL, D_MODEL])
    bd_d = din("bd", [DEPTH, 128, 4 * 128])
    wg_d = din("w_gate", [DEPTH, D_MODEL, D_FF])
    wu_d = din("w_up", [DEPTH, D_MODEL, D_FF])
    wd_d = din("w_down", [DEPTH, D_FF, D_MODEL])
    wpg_d = din("w_pg", [DEPTH, D_MODEL, D_MODEL])
    wpp_d = din("w_pp", [DEPTH, 256, D_MODEL])
    pv_d = din("pv", [128, DEPTH * NPV + 8])
    cf_d = din("cf", [128, 3 * 128])
    cb_d = din("cb", [128, 3 * 128])
    sel_d = din("sel", [6, 6 * 128])
    y_d = nc.dram_tensor("y", [NSEQ, NT, D_MODEL], F32, kind="ExternalOutput").ap()
    if KDUMP:
        dbg_d = nc.dram_tensor("dbg", [128, 8192], F32, kind="ExternalOutput").ap()

    es = contextlib.ExitStack()
    with es:
        sb = lambda name, shape, dt: es.enter_context(nc.sbuf_tensor("s_" + name, shape, dt))
        S = Sched(nc)
        GF = DEPTH * NPV

        def static(name, shape, dt):
            return Tl(sb(name, shape, dt)[:], Buf(name))

        hT = sb("hT", [128, 8, NT], F32)
        hTb = [Buf(f"hT{i}") for i in range(NBLK)]
        cf = static("cf", [128, 3, 128], F32)
        cb = static("cb", [128, 3, 128], BF16)
        pv = static("pv", [128, DEPTH * NPV + 8], F32)
        dv = static("dv", [128, DEPTH, 12], F32)
        assert TB == 256
        uT_f = static("uT", [128, 1024], F32)
        ycat_f = static("ycat", [128, 1024], F32)
        uT = Tl(uT_f.ap.bitcast(BF16).rearrange("p (c t) -> p c t", c=8), uT_f.b)
        ycat = Tl(ycat_f.ap.bitcast(BF16).rearrange("p (c t) -> p c t", c=8), ycat_f.b)
        sq = static("sq", [128, 8, TB], BF16)
        rs0 = static("rs0", [128, TB], F32)
        rs1 = static("rs1", [128, TB], F32)
        ARENA_BYTES = 125 * 1024
        arena_t = sb("arena", [128, ARENA_BYTES // 2], BF16)
        A = Arena(arena_t, ARENA_BYTES)
        ps_t = [es.enter_context(nc.psum_tensor(f"ps{i}", [128, 512], F32)) for i in range(7)]
        psb_t = es.enter_context(nc.psum_tensor("psb", [128, 1024], BF16))
        pbuf = [[Buf(f"ps{i}_{h}") for h in range(2)] for i in range(7)]
        psbb = Buf("psb")

        def PS(bank, c0, c1, p0=0, p1=128):
            bs = [pbuf[bank][0], pbuf[bank][1]]
            return ps_t[bank][p0:p1, c0:c1], bs

        ident_f = cf.ap[:, 0, :]
        tri_f = cf.ap[:, 1, :]
        ones_f = cf.ap[:, 2, :]
        ident_b = cb.ap[:, 0, :]
        ones_b = cb.ap[:, 1, :]
        mneg_b = cb.ap[:, 2, :]

        def bl(xs):
            out = []
            for x_ in xs:
                if isinstance(x_, Tl):
                    out.append(x_.b)
                elif isinstance(x_, (list, tuple)):
                    out.extend(bl(x_))
                elif x_ is not None:
                    out.append(x_)
            return out

        def mm(out, lhsT, rhs, start, stop, r, w):
            S.op("tensor", lambda e: e.matmul(out, lhsT=lhsT, rhs=rhs, start=start, stop=stop), bl(r), bl(w))

        def tr(out, in_, ident, r, w):
            S.op("tensor", lambda e: e.transpose(out, in_, ident), bl(r), bl(w))

        def act(out, in_, func, r, w, bias=None, scale=None):
            kw = {}
            if bias is not None:
                kw["bias"] = bias
            if scale is not None:
                kw["scale"] = scale
            S.op("scalar", lambda e: e.activation(out=out, in_=in_, func=func, **kw), bl(r), bl(w))

        def tt(eng, out, in0, in1, op, r, w):
            S.op(eng, lambda e: e.tensor_tensor(out=out, in0=in0, in1=in1, op=op), bl(r), bl(w))

        def ts(eng, out, in0, s1, s2, op0, op1, r, w):
            if op1 is None:
                S.op(eng, lambda e: e.tensor_scalar(out=out, in0=in0, scalar1=s1, scalar2=None, op0=op0), bl(r), bl(w))
            else:
                S.op(eng, lambda e: e.tensor_scalar(out=out, in0=in0, scalar1=s1, scalar2=s2, op0=op0, op1=op1), bl(r), bl(w))

        def stt(out, in0, scalar, in1, op0, op1, r, w):
            S.op("vector", lambda e: e.scalar_tensor_tensor(out=out, in0=in0, scalar=scalar, in1=in1, op0=op0, op1=op1), bl(r), bl(w))

        def cp(eng, out, in_, r, w):
            if eng == "scalar":
                act(out, in_, AF.Copy, r, w)
            else:
                S.op(eng, lambda e: e.tensor_copy(out=out, in_=in_), bl(r), bl(w))

        def mset(eng, ap, val, w):
            S.op(eng, lambda e: e.memset(ap, val), [], bl(w))

        def scan(out, d0, d1, init, r, w):
            S.op("vector", lambda e: e.tensor_tensor_scan(out=out, data0=d0, data1=d1, initial=init, op0=ALU.mult, op1=ALU.add),
                 bl(r), bl(w))

        def dma(q, out, in_, r, w, is_out=False, cast=False):
            kw = {"max_dma_last_dim": 8192} if cast else {}
            return S.op(q, lambda e: e.dma_start(out=out, in_=in_, **kw), bl(r), bl(w), dma_sem=S.pool_sem(q), is_out=is_out)

        dbg_state = [0]
        if KDUMP:
            dbgt = static("dbgt", [128, 8192], F32)

        def dump(name, ap, r, p0=0, p1=128):
            if not KDUMP or name in DUMPS or S.dead or not S.enabled:
                return
            n = ap.shape[-1]
            c0 = dbg_state[0]
            DUMPS[name] = (c0, n, p0, p1)
            dbg_state[0] += n
            S.op("vector", lambda e: e.tensor_copy(out=dbgt.ap[p0:p1, c0:c0 + n], in_=ap), bl(r), [dbgt.b])

        dma("sync", cf.ap, cf_d.rearrange("p (a b) -> p a b", a=3), [], [cf])
        dma("sync", pv.ap, pv_d, [], [pv])
        dma("gpsimd", cb.ap, cb_d.rearrange("p (a b) -> p a b", a=3), [], [cb], cast=True)
        for l in range(DEPTH):
            o = l * NPV
            act(dv.ap[:, l, 0:6], pv.ap[:, o + ALOG:o + ALOG + 6], AF.Exp, [pv], [dv])
            ts("vector", dv.ap[:, l, 0:6], dv.ap[:, l, 0:6], -1.0, None, ALU.mult, None, [dv], [dv])
            act(dv.ap[:, l, 6:8], pv.ap[:, o + LAM:o + LAM + 2], AF.Exp, [pv], [dv], scale=-1.0)
            act(dv.ap[:, l, 6:8], dv.ap[:, l, 6:8], AF.Ln, [dv], [dv], bias=1.0)
            ts("vector", dv.ap[:, l, 6:8], dv.ap[:, l, 6:8], -8.0, None, ALU.mult, None, [dv], [dv])
            ts("vector", dv.ap[:, l, 8:10], pv.ap[:, o + LBA:o + LBA + 2], -1.0, None, ALU.mult, None, [pv], [dv])
            ts("vector", dv.ap[:, l, 10:12], pv.ap[:, o + LBX:o + LBX + 2], -1.0, None, ALU.mult, None, [pv], [dv])

        if KSTOP <= 1:
            S.dead = True
        def rmsnorm(src_aps, src_bufs, gcol0, nch, width, out_aps, out_tl, ncols, psum_loc, sq_eng="gpsimd"):
            bank, c0 = psum_loc
            pso, psb_ = PS(bank, c0, c0 + ncols)
            for c in range(nch):
                eng = sq_eng if c % 2 == 0 else "vector"
                if eng == "scalar":
                    act(sq.ap[:, c, 0:ncols], src_aps[c], AF.Square, src_bufs, [sq])
                else:
                    tt(eng, sq.ap[:, c, 0:ncols], src_aps[c], src_aps[c], ALU.mult, src_bufs, [sq])
            for c in range(nch):
                mm(pso, ones_b, sq.ap[:, c, 0:ncols], c == 0, c == nch - 1, [cb, sq], psb_)
            act(rs0.ap[:, 0:ncols], pso, AF.Ln, psb_, [rs0], bias=EPS_AP, scale=1.0 / width)
            act(rs1.ap[:, 0:ncols], rs0.ap[:, 0:ncols], AF.Exp, [rs0], [rs1], scale=-0.5)
            for c in range(nch):
                stt(out_aps[c], src_aps[c], pv.ap[:, gcol0 + c:gcol0 + c + 1], rs1.ap[:, 0:ncols], ALU.mult, ALU.mult,
                    src_bufs + [pv, rs1], [out_tl])

        epst = static("epst", [128, 1], F32)
        mset("vector", epst.ap, EPS, [epst])
        EPS_AP = epst.ap[:, 0:1]

        MP = Bump(A, 0)

        def mixer_persist():
            MP.reset()
            d = {}
            d["win"] = MP.get([8, IN_COLS], BF16, "win")
            d["wout"] = MP.get([8, D_MODEL], BF16, "wout")
            d["bd"] = MP.get([4, 128], BF16, "bd")
            d["KT"] = MP.get([3, NT], BF16, "KT")
            d["V"] = MP.get([NT128, 384], BF16, "V")
            d["ncum"] = MP.get([NT128, 6], F32, "ncum")
            d["xraw"] = MP.get([7, TB + 3], F32, "xraw")
            d["lraw"] = MP.get([2, TB + 3], F32, "lraw")
            d["st"] = MP.get([384], F32, "st")
            d["stb"] = MP.get([384], BF16, "stb")
            d["lcar"] = MP.get([2], F32, "lcar")
            d["fcar"] = MP.get([6], F32, "fcar")
            d["fref"] = MP.get([6], F32, "fref")
            return d

        _tmp = mixer_persist()
        SCR_BASE = MP.off
        A.live = []
        SC = Bump(A, SCR_BASE)
        FFN_B = Bump(A, 0)
        PLE_BASE = SCR_BASE
        PLE_B = Bump(A, PLE_BASE)

        def load_mixer_weights(l):
            d = mixer_persist()
            d["win"].split(8)
            d["wout"].split(8)
            for c in range(8):
                dma("gpsimd", d["win"].ap[:, c, :], win_d[l, c * 128:(c + 1) * 128, :], [], [d["win"].subs[c]], cast=True)
            for c in range(8):
                dma("gpsimd", d["wout"].ap[:, c, :], wout_d[l, c * 128:(c + 1) * 128, :], [], [d["wout"].subs[c]], cast=True)
            dma("gpsimd", d["bd"].ap, bd_d[l].rearrange("p (a b) -> p a b", a=4), [], [d["bd"]], cast=True)
            return d

        for s in range(NSEQ):
            SC.reset()
            xst = [uT_f, ycat_f]
            for t in range(NT128):
                st_ = xst[t % 2]
                dma("sync", st_.ap, x_d[s, t * 128:(t + 1) * 128, :], [], [st_])
                for half in range(2):
                    bank = (2 * t + half) % 4
                    pso, pbs = PS(bank, 0, 512)
                    for j in range(4):
                        c = half * 4 + j
                        tr(ps_t[bank][:, j * 128:(j + 1) * 128], st_.ap[:, c * 128:(c + 1) * 128], ident_f, [st_, cf], pbs)
                    eng = "vector" if half == 0 else "scalar"
                    cp(eng, hT[:, half * 4:half * 4 + 4, t * 128:(t + 1) * 128],
                       pso.rearrange("p (a b) -> p a b", a=4), pbs, [hTb[(t * 128) // TB]])

            if KSTOP <= 2:
                S.dead = True
            for l in range(DEPTH):
                o = l * NPV
                W = PREFETCHED[0] if PREFETCHED[0] is not None else load_mixer_weights(l)
                PREFETCHED[0] = None
                win, wout, bdm = W["win"], W["wout"], W["bd"]
                KT, V, ncum, xraw, lraw = W["KT"], W["V"], W["ncum"], W["xraw"], W["lraw"]
                st, stb, lcar, fcar = W["st"], W["stb"], W["lcar"], W["fcar"]
                fref = W["fref"]
                mset("vector", xraw.ap[:, :, 0:3], 0.0, [xraw])
                mset("vector", lraw.ap[:, :, 0:3], 0.0, [lraw])
                mset("gpsimd", st.ap, 0.0, [st])
                mset("gpsimd", stb.ap, 0.0, [stb])
                mset("gpsimd", lcar.ap, 0.0, [lcar])
                mset("gpsimd", fcar.ap, 0.0, [fcar])

                for blk in range(NBLK):
                    t0 = blk * TB
                    hb = hTb[blk]
                    hsl = [hT[:, c, t0:t0 + TB] for c in range(8)]
                    rmsnorm(hsl, [hb], o + G1, 8, D_MODEL, [uT.ap[:, c, :] for c in range(8)], uT, TB, (6, 256))

                    slot = [0]

                    def proj(col0, ncolsM=128):
                        i = slot[0]
                        slot[0] += 1
                        bank, half = i % 2, (i // 2) % 2
                        pso, pbs = PS(bank, half * 256, half * 256 + 256)
                        for c in range(8):
                            mm(pso, win.ap[:, c, col0:col0 + ncolsM], uT.ap[:, c, :], c == 0, c == 7, win.R(c) + [uT], pbs)
                        return pso, pbs

                    SC.reset()
                    if "ssd" not in DBG:
                        for c in range(3):
                            mset("vector", ycat.ap[:, c, :], 0.0, [ycat])
                    S.enabled = "ssd" in DBG
                    xbc = SC.get([7, TB], BF16, "xbc")
                    zs = SC.get([3, TB], F32, "zs")
                    yssd = SC.get([3, TB], F32, "yssd")
                    cacc = [SC.get([TB], F32, f"cacc{i}") for i in range(2)]
                    sm = SC.get([8, 6], F32, "sm")
                    xdt = SC.get([384], BF16, "xdt")
                    xdts = SC.get([384], BF16, "xdts")
                    btok = SC.get([256], BF16, "btok")
                    Dm = [SC.get([128], F32, f"D{i}") for i in range(2)]
                    MT = [SC.get([128], BF16, f"MT{i}") for i in range(2)]
                    ebc = [SC.get([128], F32, f"ebc{i}") for i in range(2)]
                    Cs = [SC.get([128], BF16, f"Cs{i}") for i in range(2)]
                    stt_tmp = SC.get([384], F32, "sttmp")
                    smp = SC.get([24], F32, "smp")

                    for c in range(3):
                        pso, pbs = proj(OZ + c * 128)
                        act(zs.ap[:, c, :], pso, AF.Silu, pbs, [zs])
                    xbc.split(7)
                    xbc.b.readers = []
                    xraw_s = [Buf() for _ in range(7)]
                    for sb_ in xraw_s:
                        sb_.writer = xraw.b.writer
                        sb_.readers = list(xraw.b.readers)

                    def silu_c(c):
                        act(xbc.ap[:, c, :], cacc[c % 2].ap, AF.Silu, [cacc[c % 2]], [xbc.subs[c]])

                    for c in range(7):
                        pso, pbs = proj(OXS + c * 128)
                        cp("scalar", xraw.ap[:, c, 3:3 + TB], pso, pbs, [xraw_s[c]])
                        if c > 0:
                            silu_c(c - 1)
                        ca = cacc[c % 2]
                        wc = lambda k: pv.ap[:, o + SCW + c * 4 + k:o + SCW + c * 4 + k + 1]
                        ts("vector", ca.ap, xraw.ap[:, c, 0:TB], wc(0), pv.ap[:, o + SCB + c:o + SCB + c + 1], ALU.mult, ALU.add,
                           [xraw_s[c], pv], [ca])
                        for k in range(1, 4):
                            stt(ca.ap, xraw.ap[:, c, k:k + TB], wc(k), ca.ap, ALU.mult, ALU.add, [xraw_s[c], pv, ca], [ca])
                    silu_c(6)
                    cp("gpsimd", xraw.ap[:, :, 0:3], xraw.ap[:, :, TB:TB + 3], xraw_s, xraw_s + [xraw])
                    cp("vector", fref.ap, fcar.ap, [fcar], [fref])
                    for j in range(TB // 128):
                        cs = slice(j * 128, (j + 1) * 128)
                        tg = t0 + j * 128
                        pso, pbs = PS(2, 0, 396)
                        for c in range(8):
                            mm(pso, uT.ap[:, c, cs], win.ap[:, c, OV:OV + 396], c == 0, c == 7, win.R(c) + [uT], pbs)
                        kb = tg // 128
                        cp("vector", V.ap[:, kb, :], ps_t[2][:, 0:384], pbs, [V])
                        dt, adt, nacs, tmp6, decs, cdec, sdt, nlf = [sm.ap[:, i, :] for i in range(8)]
                        tt("vector", tmp6, ps_t[2][:, 384:390], pv.ap[:, o + DTB:o + DTB + 6], ALU.add, pbs + [pv], [sm])
                        tt("vector", nlf, ps_t[2][:, 390:396], pv.ap[:, o + FBF:o + FBF + 6], ALU.add, pbs + [pv], [sm])
                        act(tmp6, tmp6, AF.Exp, [sm], [sm])
                        act(nlf, nlf, AF.Exp, [sm], [sm], scale=-1.0)
                        act(dt, tmp6, AF.Ln, [sm], [sm], bias=1.0)
                        act(nlf, nlf, AF.Ln, [sm], [sm], bias=1.0)
                        tt("vector", adt, dt, dv.ap[:, l, 0:6], ALU.mult, [sm, dv], [sm])
                        p4, p4b = PS(4, 384, 408)
                        mm(ps_t[4][:, 384:390], tri_f, adt, True, True, [cf, sm], p4b)
                        mm(ps_t[4][:, 390:396], ones_f, adt, True, True, [cf, sm], p4b)
                        mm(ps_t[4][:, 396:402], tri_f, nlf, True, True, [cf, sm], p4b)
                        mm(ps_t[4][:, 402:408], ones_f, nlf, True, True, [cf, sm], p4b)
                        cp("vector", smp.ap, ps_t[4][:, 384:408], p4b, [smp])
                        ts("vector", nacs, smp.ap[:, 0:6], -1.0, None, ALU.mult, None, [smp], [sm])
                        tt("vector", tmp6, smp.ap[:, 6:12], nacs, ALU.add, [smp, sm], [sm])
                        act(decs, tmp6, AF.Exp, [sm], [sm])
                        act(cdec, smp.ap[:, 6:12], AF.Exp, [smp], [sm])
                        tt("vector", sdt, dt, decs, ALU.mult, [sm], [sm])
                        tt("vector", ncum.ap[:, kb, :], smp.ap[:, 12:18], fcar.ap, ALU.add, [smp, fcar], [ncum])
                        tt("vector", fcar.ap, smp.ap[:, 18:24], fcar.ap, ALU.add, [smp, fcar], [fcar])
                        for c in range(5):
                            tr(psb_t[:, c * 128:(c + 1) * 128], xbc.ap[:, c, cs], ident_b, xbc.R(c) + [cb], [psbb])
                        xs_tok = psb_t[:, 0:384].rearrange("p (h e) -> p h e", h=6)
                        tt("vector", xdt.ap.rearrange("p (h e) -> p h e", h=6), xs_tok, dt.unsqueeze(2).to_broadcast([128, 6, 64]),
                           ALU.mult, [psbb, sm], [xdt])
                        tt("vector", xdts.ap.rearrange("p (h e) -> p h e", h=6), xs_tok, sdt.unsqueeze(2).to_broadcast([128, 6, 64]),
                           ALU.mult, [psbb, sm], [xdts])
                        cp("vector", btok.ap, psb_t[:, 384:640], [psbb], [btok])
                        pg, pgb = PS(3, 0, 256)
                        for g in range(2):
                            mm(ps_t[3][:, g * 128:(g + 1) * 128], xbc.ap[:, 3 + g, cs], xbc.ap[:, 5 + g, cs], True, True, xbc.R(3 + g) + xbc.R(5 + g), pgb)
                        py, pyb = PS(4, 0, 384)

                        def stageA(h):
                            g = h // 3
                            i2 = h % 2
                            eb = 6 if i2 == 0 else 2
                            pE, pEb = PS(eb, 0, 256)
                            adt_b = adt[:, h:h + 1].to_broadcast([128, 128])
                            mm(ps_t[eb][:, 0:128], adt_b, tri_f, True, False, [sm, cf], pEb)
                            mm(ps_t[eb][:, 0:128], ident_b, mneg_b, False, True, [cb], pEb)
                            mm(ps_t[eb][:, 128:256], adt_b, tri_f, True, True, [sm, cf], pEb)
                            act(Dm[i2].ap, ps_t[eb][:, 0:128], AF.Exp, pEb + [sm], [Dm[i2]], bias=nacs[:, h:h + 1])
                            act(ebc[i2].ap, ps_t[eb][:, 128:256], AF.Exp, pEb, [ebc[i2]])
                            tt("vector", MT[i2].ap, ps_t[3][:, g * 128:(g + 1) * 128], Dm[i2].ap, ALU.mult, pgb + [Dm[i2]], [MT[i2]])
                            tt("gpsimd", Cs[i2].ap, xbc.ap[:, 5 + g, cs], ebc[i2].ap, ALU.mult, xbc.R(5 + g) + [ebc[i2]], [Cs[i2]])

                        def stageB(h):
                            i2 = h % 2
                            hp = (h % 2) * 64
                            yo = ps_t[4][hp:hp + 64, (h // 2) * 128:(h // 2) * 128 + 128]
                            mm(yo, xdt.ap[:, h * 64:(h + 1) * 64], MT[i2].ap, True, False, [xdt, MT[i2]], pyb)
                            mm(yo, stb.ap[:, h * 64:(h + 1) * 64], Cs[i2].ap, False, True, [stb, Cs[i2]], pyb)

                        stageA(0)
                        for h in range(6):
                            if h + 1 < 6:
                                stageA(h + 1)
                            stageB(h)
                        for c in range(3):
                            stt(yssd.ap[:, c, cs], xbc.ap[:, c, cs], pv.ap[:, o + DSK + c:o + DSK + c + 1],
                                ps_t[4][:, c * 128:(c + 1) * 128], ALU.mult, ALU.add, xbc.R(c) + [pv] + pyb, [yssd])
                        pst, pstb = PS(5, 0, 384)
                        for g in range(2):
                            mm(ps_t[5][:, g * 192:(g + 1) * 192], btok.ap[:, g * 128:(g + 1) * 128], xdts.ap[:, g * 192:(g + 1) * 192],
                               True, True, [btok, xdts], pstb)
                        tt("vector", stt_tmp.ap.rearrange("p (h e) -> p h e", h=6), st.ap.rearrange("p (h e) -> p h e", h=6),
                           cdec.unsqueeze(2).to_broadcast([128, 6, 64]), ALU.mult, [st, sm], [stt_tmp])
                        tt("vector", st.ap, stt_tmp.ap, pst, ALU.add, [stt_tmp] + pstb, [st])
                        cp("gpsimd", stb.ap, st.ap, [st], [stb])
                    dump("yssd1_pre", yssd.ap[:, 1, :], [yssd])
                    dump("zs1", zs.ap[:, 1, :], [zs])
                    for c in range(3):
                        tt("gpsimd", yssd.ap[:, c, :], yssd.ap[:, c, :], zs.ap[:, c, :], ALU.mult, [yssd, zs], [yssd])
                    rmsnorm([yssd.ap[:, c, :] for c in range(3)], [yssd], o + SNG, 3, 384,
                            [ycat.ap[:, c, :] for c in range(3)], ycat, TB, (6, 256))

                    dump("ycat1_early", ycat.ap[:, 1, :], [ycat])
                    dump("yssd1", yssd.ap[:, 1, :], [yssd])
                    S.enabled = True
                    SC.reset()
                    LT = [{k: SC.get([TB], BF16 if k == "xlb" else F32, f"{k}{c}") for k in ("xl", "xlb", "rg", "ig", "aa", "mu", "hl", "g1", "g2")}
                          for c in range(2)]
                    lraw_s = [Buf() for _ in range(2)]
                    for sb_ in lraw_s:
                        sb_.writer = lraw.b.writer
                        sb_.readers = list(lraw.b.readers)
                    pgl = [None, None]

                    def l1(c):
                        T_ = LT[c]
                        pso, pbs = proj(OLX + c * 128)
                        cp("scalar", lraw.ap[:, c, 3:3 + TB], pso, pbs, [lraw_s[c]])
                        pgl[c] = proj(OLG + c * 128)
                        cp("scalar", T_["g1"].ap, pgl[c][0], pgl[c][1], [T_["g1"]])

                    def l2(c):
                        T_ = LT[c]
                        wc = lambda k: pv.ap[:, o + LCW + c * 4 + k:o + LCW + c * 4 + k + 1]
                        ts("vector", T_["xl"].ap, lraw.ap[:, c, 0:TB], wc(0), pv.ap[:, o + LCB + c:o + LCB + c + 1], ALU.mult, ALU.add,
                           [lraw_s[c], pv], [T_["xl"]])
                        for k in range(1, 4):
                            stt(T_["xl"].ap, lraw.ap[:, c, k:k + TB], wc(k), T_["xl"].ap, ALU.mult, ALU.add, [lraw_s[c], pv, T_["xl"]], [T_["xl"]])
                        cp("gpsimd", T_["xlb"].ap, T_["xl"].ap, [T_["xl"]], [T_["xlb"]])
                        tt("gpsimd", T_["g2"].ap, T_["g1"].ap, T_["g1"].ap, ALU.mult, [T_["g1"]], [T_["g2"]])
                        ts("vector", T_["g2"].ap, T_["g2"].ap, 0.044715, 1.0, ALU.mult, ALU.add, [T_["g2"]], [T_["g2"]])
                        tt("gpsimd", T_["g2"].ap, T_["g2"].ap, T_["g1"].ap, ALU.mult, [T_["g2"], T_["g1"]], [T_["g2"]])

                    def sig3(out_ap, in_ap, r, w, scale, bias=None):
                        act(out_ap, in_ap, AF.Exp, r, w, scale=scale, bias=bias)
                        act(out_ap, out_ap, AF.Ln, w, w, bias=1.0)
                        act(out_ap, out_ap, AF.Exp, w, w, scale=-1.0)

                    def l3(c):
                        T_ = LT[c]
                        pa, pab = PS(2, c * 256, c * 256 + 256)
                        mm(pa, bdm.ap[:, c, :], T_["xlb"].ap, True, True, [bdm, T_["xlb"]], pab)
                        sig3(T_["rg"].ap, pa, pab + [dv], [T_["rg"]], -1.0, dv.ap[:, l, 8 + c:9 + c])
                        mm(pa, bdm.ap[:, 2 + c, :], T_["xlb"].ap, True, True, [bdm, T_["xlb"]], pab)
                        sig3(T_["ig"].ap, pa, pab + [dv], [T_["ig"]], -1.0, dv.ap[:, l, 10 + c:11 + c])
                        sig3(T_["g2"].ap, T_["g2"].ap, [T_["g2"]], [T_["g2"]], -1.5957691216057308)

                    def l4(c):
                        T_ = LT[c]
                        ts("vector", T_["aa"].ap, T_["rg"].ap, dv.ap[:, l, 6 + c:7 + c], None, ALU.mult, None, [T_["rg"], dv], [T_["aa"]])
                        act(T_["aa"].ap, T_["aa"].ap, AF.Exp, [T_["aa"]], [T_["aa"]])
                        tt("gpsimd", T_["ig"].ap, T_["ig"].ap, T_["xl"].ap, ALU.mult, [T_["ig"], T_["xl"]], [T_["ig"]])
                        tt("gpsimd", T_["g2"].ap, T_["g2"].ap, T_["g1"].ap, ALU.mult, [T_["g2"], T_["g1"]], [T_["g2"]])

                    def l5(c):
                        T_ = LT[c]
                        tt("gpsimd", T_["mu"].ap, T_["aa"].ap, T_["aa"].ap, ALU.mult, [T_["aa"]], [T_["mu"]])
                        act(T_["mu"].ap, T_["mu"].ap, AF.Ln, [T_["mu"]], [T_["mu"]], bias=1.0, scale=-1.0)
                        act(T_["mu"].ap, T_["mu"].ap, AF.Exp, [T_["mu"]], [T_["mu"]], scale=0.5)

                    def l6(c):
                        T_ = LT[c]
                        tt("vector", T_["mu"].ap, T_["mu"].ap, T_["ig"].ap, ALU.mult, [T_["mu"], T_["ig"]], [T_["mu"]])
                        scan(T_["hl"].ap, T_["aa"].ap, T_["mu"].ap, lcar.ap[:, c:c + 1], [T_["aa"], T_["mu"], lcar], [T_["hl"]])
                        cp("vector", lcar.ap[:, c:c + 1], T_["hl"].ap[:, TB - 1:TB], [T_["hl"]], [lcar])
                        tt("vector", T_["hl"].ap, T_["hl"].ap, T_["g2"].ap, ALU.mult, [T_["hl"], T_["g2"]], [T_["hl"]])

                    def lru_gen():
                        for st_fn in (l1, l2, l3, l4, l5, l6):
                            for c in range(2):
                                st_fn(c)
                                yield
                            if st_fn is l2:
                                cp("gpsimd", lraw.ap[:, :, 0:3], lraw.ap[:, :, TB:TB + 3], lraw_s, lraw_s + [lraw])
                        hl_aps = [LT[c]["hl"].ap for c in range(2)]
                        hl_bufs = [LT[c]["hl"] for c in range(2)]
                        rmsnorm(hl_aps, hl_bufs, o + LNG, 2, 256,
                                [ycat.ap[:, 3 + c, :] for c in range(2)], ycat, TB, (2, 256))
                        yield

                    lgen = lru_gen()

                    qT = SC.get([3, TB], BF16, "qT")
                    pT = [SC.get([TB], BF16, f"pT{i}") for i in range(4)]
                    lnd = SC.get([TB], F32, "lnd")
                    yfox = SC.get([3, TB], F32, "yfox")
                    for c in range(3):
                        pso, pbs = proj(OQ + c * 128)
                        cp("scalar", qT.ap[:, c, :], pso, pbs, [qT])
                    for c in range(3):
                        pso, pbs = proj(OK_ + c * 128)
                        cp("scalar", KT.ap[:, c, t0:t0 + TB], pso, pbs, [KT])
                    nkb = (t0 + TB) // 128
                    nb = SC.get([NT128, 6], F32, "nb")
                    tt("vector", nb.ap[:, 0:nkb, :], ncum.ap[:, 0:nkb, :], fref.ap.unsqueeze(1).to_broadcast([128, nkb, 6]), ALU.subtract,
                       [ncum, fref], [nb])
                    pi = 0
                    for c in range(3):
                        py, pyb = PS(5, 0, 256)
                        pdn, pdb = PS(4, 0, 256)
                        its = [(hh, kb) for hh in range(2) for kb in range(nkb)]

                        def stA(idx, it):
                            hh, kb = it
                            h = 2 * c + hh
                            hp = hh * 64
                            rel = kb * 128 - t0
                            q0 = 0 if rel < 0 else rel
                            sbank = 6 if idx % 2 == 0 else 3
                            psS, psSb = PS(sbank, 0, 256)
                            so = ps_t[sbank][:, q0:TB]
                            diag = rel >= 0
                            mm(so, KT.ap[hp:hp + 64, c, kb * 128:(kb + 1) * 128], qT.ap[hp:hp + 64, c, q0:TB], True, not diag,
                               [KT, qT], psSb)
                            if diag:
                                mm(ps_t[sbank][:, q0:q0 + 128], ident_b, mneg_b, False, True, [cb], psSb)
                            pt = pT[idx % 4]
                            if q0 > 0:
                                mset("gpsimd", pt.ap[:, 0:q0], 0.0, [pt])
                            act(pt.ap[:, q0:TB], so, AF.Exp, psSb + [nb], [pt], bias=nb.ap[:, kb, h:h + 1], scale=0.125)

                        def stB(idx, it):
                            hh, kb = it
                            h = 2 * c + hh
                            hp = hh * 64
                            pt = pT[idx % 4]
                            first, last = kb == 0, kb == nkb - 1
                            mm(ps_t[5][hp:hp + 64, 0:TB], V.ap[:, kb, h * 64:(h + 1) * 64], pt.ap[:, 0:TB], first, last, [V, pt], pyb)
                            mm(ps_t[4][hp:hp + 64, 0:TB], ones_b[:, 0:64], pt.ap[:, 0:TB], first, last, [cb, pt], pdb)

                        stA(pi, its[0])
                        for i_, it in enumerate(its):
                            if i_ + 1 < len(its):
                                stA(pi + i_ + 1, its[i_ + 1])
                            stB(pi + i_, it)
                            next(lgen, None)
                        pi += len(its)
                        act(lnd.ap, pdn, AF.Ln, pdb, [lnd])
                        act(lnd.ap, lnd.ap, AF.Exp, [lnd], [lnd], scale=-1.0)
                        tt("vector", yfox.ap[:, c, :], py, lnd.ap, ALU.mult, pyb + [lnd], [yfox])
                    for _ in lgen:
                        pass
                    dump("qT0", qT.ap[:, 0, :], [qT])
                    dump("KT0", KT.ap[:, 0, 0:TB], [KT])
                    dump("nc0", ncum.ap[:, 0, :], [ncum])
                    dump("nc1", ncum.ap[:, 1, :], [ncum])
                    dump("V0", V.ap[:, 0, :], [V])
                    dump("V1", V.ap[:, 1, :], [V])
                    dump("yfox0", yfox.ap[:, 0, :], [yfox])
                    rmsnorm([yfox.ap[:, c, :] for c in range(3)], [yfox], o + FNG, 3, 384,
                            [ycat.ap[:, 5 + c, :] for c in range(3)], ycat, TB, (6, 256))

                    S.enabled = True
                    for c in range(8):
                        dump(f"ycat{c}", ycat.ap[:, c, :], [ycat])
                    for dc in range(8):
                        bank, half = dc % 2, (dc // 2) % 2
                        pso, pbs = PS(bank, half * 256, half * 256 + 256)
                        for c in range(8):
                            mm(pso, wout.ap[:, c, dc * 128:(dc + 1) * 128], ycat.ap[:, c, :], c == 0, c == 7, wout.R(c) + [ycat], pbs)
                        tt("vector", hsl[dc], hsl[dc], pso, ALU.add, [hb] + pbs, [hb])

                if KSTOP <= 3:
                    S.dead = True
                FFN_B.reset()
                uTs = FFN_B.get([8, NT], BF16, "uTs")
                wgt = [FFN_B.get([8, 512], BF16, f"wg{i}") for i in range(2)]
                wut = [FFN_B.get([8, 512], BF16, f"wu{i}") for i in range(2)]
                wdt = [FFN_B.get([4, D_MODEL], BF16, f"wd{i}") for i in range(2)]
                actb = FFN_B.get([4, 512], BF16, "actb")
                sgt = [FFN_B.get([512], F32, f"sg{i}") for i in range(2)]
                assert FFN_B.off <= PLE_BASE, (FFN_B.off, PLE_BASE)
                PLE_B.reset()
                wpg = PLE_B.get([8, D_MODEL], BF16, "wpg")
                wpp = PLE_B.get([2, D_MODEL], BF16, "wpp")
                pTt = PLE_B.get([2, TB], BF16, "pTt")
                pTt2 = PLE_B.get([2, TB], BF16, "pTt2")
                pst_ = [PLE_B.get([256], F32, "pst0")]
                gat = [PLE_B.get([TB], F32, f"gat{i}") for i in range(2)]
                wpg.split(8)
                wpp.split(2)
                for c in range(8):
                    dma("gpsimd", wpg.ap[:, c, :], wpg_d[l, c * 128:(c + 1) * 128, :], [], [wpg.subs[c]], cast=True)
                for c in range(2):
                    dma("gpsimd", wpp.ap[:, c, :], wpp_d[l, c * 128:(c + 1) * 128, :], [], [wpp.subs[c]], cast=True)
                S.enabled = "ffn" in DBG
                uTs.split(NT // 512)

                def ffn_norm(tb4_):
                    for blk in (2 * tb4_, 2 * tb4_ + 1):
                        t0 = blk * TB
                        rmsnorm([hT[:, c, t0:t0 + TB] for c in range(8)], [hTb[blk]], o + G2, 8, D_MODEL,
                                [uTs.ap[:, c, t0:t0 + TB] for c in range(8)], Tl(None, uTs.subs[tb4_]), TB, (6, 256), sq_eng="scalar")
                groups = [(g * 512, 512) for g in range(5)] + [(2560, 256)]
                NT512 = NT // 512
                for gi, (f0, fw_) in enumerate(groups):
                    wg_, wu_, wd_ = wgt[gi % 2], wut[gi % 2], wdt[gi % 2]
                    nj = fw_ // 128
                    wg_.split(8)
                    wu_.split(8)
                    wd_.split(4)
                    wg_.b.readers = []
                    wu_.b.readers = []
                    wd_.b.readers = []
                    for c in range(8):
                        dma("gpsimd", wg_.ap[:, c, 0:fw_], wg_d[l, c * 128:(c + 1) * 128, f0:f0 + fw_], [], [wg_.subs[c]], cast=True)
                    for c in range(8):
                        dma("gpsimd", wu_.ap[:, c, 0:fw_], wu_d[l, c * 128:(c + 1) * 128, f0:f0 + fw_], [], [wu_.subs[c]], cast=True)
                    for j in range(nj):
                        dma("gpsimd", wd_.ap[:, j, :], wd_d[l, f0 + j * 128:f0 + (j + 1) * 128, :], [], [wd_.subs[j]], cast=True)
                    for tb4 in range(NT512):
                        tsl = slice(tb4 * 512, (tb4 + 1) * 512)
                        if gi == 0:
                            if tb4 == 0:
                                ffn_norm(0)
                            if tb4 + 1 < NT512:
                                ffn_norm(tb4 + 1)
                        for j in range(nj):
                            pg_, pgb_ = PS(j % 2, 0, 512)
                            pu_, pub_ = PS(2 + j % 2, 0, 512)
                            for c in range(8):
                                mm(pg_, wg_.ap[:, c, j * 128:(j + 1) * 128], uTs.ap[:, c, tsl], c == 0, c == 7, wg_.R(c) + uTs.R(tb4), pgb_)
                            for c in range(8):
                                mm(pu_, wu_.ap[:, c, j * 128:(j + 1) * 128], uTs.ap[:, c, tsl], c == 0, c == 7, wu_.R(c) + uTs.R(tb4), pub_)
                            sg_ = sgt[j % 2]
                            act(sg_.ap, pg_, AF.Silu, pgb_, [sg_])
                            tt("vector", actb.ap[:, j, :], sg_.ap, pu_, ALU.mult, [sg_] + pub_, [actb])
                        for dc in range(8):
                            pd_, pdb_ = PS(4 + dc % 2, 0, 512)
                            for j in range(nj):
                                mm(pd_, wd_.ap[:, j, dc * 128:(dc + 1) * 128], actb.ap[:, j, :], j == 0, j == nj - 1, wd_.R(j) + [actb], pdb_)
                            hbs = [hTb[2 * tb4], hTb[2 * tb4 + 1]]
                            tt("vector", hT[:, dc, tsl], hT[:, dc, tsl], pd_, ALU.add, hbs + pdb_, hbs)

                S.enabled = True
                nxt = None
                if l + 1 < DEPTH:
                    nxt = l + 1
                elif s + 1 < NSEQ:
                    nxt = 0
                if nxt is not None:
                    PREFETCHED[0] = load_mixer_weights(nxt)
                S.enabled = "ple" in DBG
                pTts = [pTt, pTt2]

                def ple_pre(blk):
                    t0 = blk * TB
                    hb = hTb[blk]
                    hsl = [hT[:, c, t0:t0 + TB] for c in range(8)]
                    ub = uT if blk % 2 == 0 else ycat
                    rmsnorm(hsl, [hb], o + G3, 8, D_MODEL, [ub.ap[:, c, :] for c in range(8)], ub, TB, (6, 256), sq_eng="scalar")
                    pTb = pTts[blk % 2]
                    for j in range(TB // 128):
                        ps_ = pst_[0]
                        dma("sync", ps_.ap, p_d[l, s, t0 + j * 128:t0 + (j + 1) * 128, :], [], [ps_])
                        pt_, ptb = PS(4, 0, 256)
                        for c in range(2):
                            tr(ps_t[4][:, c * 128:(c + 1) * 128], ps_.ap[:, c * 128:(c + 1) * 128], ident_f, [ps_, cf], ptb)
                        cp("scalar", pTb.ap[:, :, j * 128:(j + 1) * 128], pt_.rearrange("p (a b) -> p a b", a=2), ptb, [pTb])

                def ple_main(blk):
                    t0 = blk * TB
                    hb = hTb[blk]
                    hsl = [hT[:, c, t0:t0 + TB] for c in range(8)]
                    ub = uT if blk % 2 == 0 else ycat
                    pTb = pTts[blk % 2]
                    for dc in range(8):
                        pg_, pgb_ = PS(dc % 2, 0, 256)
                        pp_, ppb_ = PS(2 + dc % 2, 0, 256)
                        for c in range(8):
                            mm(pg_, wpg.ap[:, c, dc * 128:(dc + 1) * 128], ub.ap[:, c, :], c == 0, c == 7, wpg.R(c) + [ub], pgb_)
                        for c in range(2):
                            mm(pp_, wpp.ap[:, c, dc * 128:(dc + 1) * 128], pTb.ap[:, c, :], c == 0, c == 1, wpp.R(c) + [pTb], ppb_)
                        ga = gat[dc % 2]
                        act(ga.ap, pg_, AF.Sigmoid, pgb_ + [pv], [ga], bias=pv.ap[:, o + BPG + dc:o + BPG + dc + 1])
                        tt("vector", ga.ap, ga.ap, pp_, ALU.mult, [ga] + ppb_, [ga])
                        tt("vector", hsl[dc], hsl[dc], ga.ap, ALU.add, [hb, ga], [hb])

                if "ple" in DBG:
                    ple_pre(0)
                    for blk in range(NBLK):
                        if blk + 1 < NBLK:
                            ple_pre(blk + 1)
                        ple_main(blk)

            S.enabled = True
            if KSTOP <= 4:
                S.dead = True
            SC.reset()
            onT = SC.get([8, TB], F32, "onT")
            ost = [uT_f, ycat_f]
            for blk in range(NBLK):
                t0 = blk * TB
                rmsnorm([hT[:, c, t0:t0 + TB] for c in range(8)], [hTb[blk]], GF, 8, D_MODEL,
                        [onT.ap[:, c, :] for c in range(8)], onT, TB, (6, 256))
                for j in range(TB // 128):
                    if KFIN < 2:
                        break
                    os_ = ost[j % 2]
                    for half in range(2):
                        bank = half
                        pso, pbs = PS(bank, 0, 512)
                        for q in range(4):
                            c = half * 4 + q
                            tr(ps_t[bank][:, q * 128:(q + 1) * 128], onT.ap[:, c, j * 128:(j + 1) * 128], ident_f, [onT, cf], pbs)
                        cp("vector" if half == 0 else "scalar", os_.ap[:, half * 512:(half + 1) * 512], pso, pbs, [os_])
                    if KFIN >= 3:
                        dma("sync", y_d[s, t0 + j * 128:t0 + (j + 1) * 128, :], os_.ap, [os_], [], is_out=True)
        if KDUMP:
            S.enabled = True
            S.dead = False
            dma("sync", dbg_d, dbgt.ap, [dbgt], [], is_out=True)
        S.emit()
    return nc


PREFETCHED = [None]
import os
DBG = set(os.environ.get("KDBG", "ssd,lru,fox,ffn,ple").split(","))
KSTOP = int(os.environ.get("KSTOP", "99"))
KFIN = int(os.environ.get("KFIN", "3"))
KPLE = int(os.environ.get("KPLE", "3"))
KDUMP = int(os.environ.get("KDUMP", "0"))
DUMPS = {}

def _consts():
    ident = np.eye(128, dtype=np.float32)
    tri = np.triu(np.ones((128, 128), np.float32))
    ones = np.ones((128, 128), np.float32)
    mneg = np.where(np.arange(128)[:, None] > np.arange(128)[None, :], np.float32(-30000.0), np.float32(0.0)).astype(np.float32)
    cf = np.concatenate([ident, tri, ones], axis=1)
    cb = np.concatenate([ident, ones, mneg], axis=1)
    sel = np.zeros((6, 6, 128), np.float32)
    for h in range(6):
        sel[h, h, :] = 1.0
    return cf, cb, sel.reshape(6, 768)


def _pcol(v, nch):
    return np.ascontiguousarray(np.asarray(v, np.float32).reshape(nch, 128).T)


def _pack(inp, DEPTH):
    f = lambda k: np.asarray(inp[k], np.float32)
    pvs = []
    w_in_r, bds = [], []
    for l in range(DEPTH):
        cols = np.zeros((128, NPV), np.float32)
        cols[:, G1:G1 + 8] = _pcol(f("norm1_g")[l], 8)
        cols[:, G2:G2 + 8] = _pcol(f("norm2_g")[l], 8)
        cols[:, G3:G3 + 8] = _pcol(f("norm3_g")[l], 8)
        scw = f("ssd_conv_w")[l]
        for c in range(7):
            for k in range(4):
                cols[:, SCW + c * 4 + k] = scw[k, c * 128:(c + 1) * 128]
        cols[:, SCB:SCB + 7] = _pcol(f("ssd_conv_b")[l], 7)
        cols[:, DSK:DSK + 3] = _pcol(np.repeat(f("ssd_d")[l], 64), 3)
        cols[:, SNG:SNG + 3] = _pcol(f("ssd_norm_g")[l], 3)
        lcw = f("lru_conv_w")[l]
        for c in range(2):
            for k in range(4):
                cols[:, LCW + c * 4 + k] = lcw[k, c * 128:(c + 1) * 128]
        cols[:, LCB:LCB + 2] = _pcol(f("lru_conv_b")[l], 2)
        cols[:, LBA:LBA + 2] = _pcol(f("lru_b_a")[l], 2)
        cols[:, LBX:LBX + 2] = _pcol(f("lru_b_x")[l], 2)
        cols[:, LAM:LAM + 2] = _pcol(f("lru_lambda")[l], 2)
        cols[:, LNG:LNG + 2] = _pcol(f("lru_norm_g")[l], 2)
        cols[:, FNG:FNG + 3] = _pcol(f("fox_norm_g")[l], 3)
        cols[:, BPG:BPG + 8] = _pcol(f("b_ple_gate")[l], 8)
        cols[:, DTB:DTB + 6] = np.broadcast_to(f("ssd_dt_bias")[l][None, :], (128, 6))
        cols[:, ALOG:ALOG + 6] = np.broadcast_to(f("ssd_a_log")[l][None, :], (128, 6))
        cols[:, FBF:FBF + 6] = np.broadcast_to(f("fox_b_f")[l][None, :], (128, 6))
        pvs.append(cols)
        w = f("w_in")[l]
        z, xbc, dt, lx, lg, q, k, v, fr = np.split(w, np.cumsum([384, 896, 6, 256, 256, 384, 384, 384])[:], axis=1)
        w_in_r.append(np.concatenate([z, xbc, lx, lg, q, k, v, dt, fr], axis=1))
        bd = np.zeros((128, 4, 128), np.float32)
        wa, wx = f("lru_w_a")[l], f("lru_w_x")[l]
        for c in range(2):
            for i in range(2):
                bd[i * 64:(i + 1) * 64, c, i * 64:(i + 1) * 64] = wa[2 * c + i]
                bd[i * 64:(i + 1) * 64, 2 + c, i * 64:(i + 1) * 64] = wx[2 * c + i]
        bds.append(bd.reshape(128, 512))
    pv = np.concatenate(pvs + [_pcol(f("final_norm_g"), 8)], axis=1)
    cf, cb, sel = _consts()
    shared = {
        "w_in": np.ascontiguousarray(np.stack(w_in_r)), "w_out": f("w_out")[:DEPTH], "bd": np.stack(bds),
        "w_gate": f("w_gate")[:DEPTH], "w_up": f("w_up")[:DEPTH], "w_down": f("w_down")[:DEPTH],
        "w_pg": f("w_ple_gate")[:DEPTH], "w_pp": f("w_ple_proj")[:DEPTH],
        "pv": np.ascontiguousarray(pv), "cf": cf, "cb": cb, "sel": sel,
    }
    return shared


_NC_CACHE = {}


def run(inp, NCORES, DEPTH):
    x = np.asarray(inp["x"], np.float32)
    p = np.asarray(inp["p"], np.float32)
    B, NT, _ = x.shape
    NSEQ = B // NCORES
    key = (NSEQ, NT, DEPTH)
    if key not in _NC_CACHE:
        PREFETCHED[0] = None
        _NC_CACHE[key] = build(NSEQ, NT, DEPTH)
    nc = _NC_CACHE[key]
    shared = _pack(inp, DEPTH)
    in_maps = []
    for c in range(NCORES):
        m = dict(shared)
        m["x"] = np.ascontiguousarray(x[c * NSEQ:(c + 1) * NSEQ])
        m["p"] = np.ascontiguousarray(p[:DEPTH, c * NSEQ:(c + 1) * NSEQ])
        in_maps.append(m)
    res = run_bass_kernel_spmd(nc, in_maps, core_ids=list(range(NCORES)))
    if KDUMP:
        global LAST_DBG
        LAST_DBG = res.results[0]["dbg"]
    return np.concatenate([r["y"] for r in res.results], axis=0)


def kernel(**inputs):
    return run(inputs, 8, 2)
```

```python
import contextlib
import numpy as np
import concourse.bass as bass
import concourse.mybir as mybir
from concourse.bass_utils import run_bass_kernel_spmd

F32 = mybir.dt.float32
BF16 = mybir.dt.bfloat16
AF = mybir.ActivationFunctionType
ALU = mybir.AluOpType

ENGS = ["tensor", "vector", "scalar", "gpsimd", "sync"]
SEM_CHUNK = 3000

D_MODEL = 1024
D_FF = 2816
IN_COLS = 2956
OZ, OXS, OB, OC, OLX, OLG, OQ, OK_, OV, ODT, OFR = 0, 384, 768, 1024, 1280, 1536, 1792, 2176, 2560, 2944, 2950
G1, G2, G3, SCW, SCB, DSK, SNG, LCW, LCB, LBA, LBX, LAM, LNG, FNG, BPG, DTB, ALOG, FBF, NPV = (
    0, 8, 16, 24, 52, 59, 62, 65, 73, 75, 77, 79, 81, 83, 86, 94, 100, 106, 112)
EPS = 1e-6
TB = 256


class Buf:
    __slots__ = ("name", "writer", "readers")

    def __init__(self, name=""):
        self.name = name
        self.writer = None
        self.readers = []


class Sched:
    def __init__(self, nc):
        self.nc = nc
        self.prog = {e: [] for e in ENGS}
        self.clock = {e: {} for e in ENGS}
        self.dma_count = []
        self.dma_last = []
        self.out_stamps = []
        self.pool = {}
        self.pool_i = {}

    def new_dma_sem(self):
        self.dma_count.append(0)
        self.dma_last.append(None)
        return len(self.dma_count) - 1

    def pool_sem(self, q, n=28):
        if q not in self.pool:
            self.pool[q] = [self.new_dma_sem() for _ in range(n)]
            self.pool_i[q] = 0
        k = self.pool[q][self.pool_i[q] % n]
        self.pool_i[q] += 1
        return k

    def _need(self, eng, stamp, waits):
        if stamp is None:
            return
        kind, key, val, snap = stamp
        if kind == "c" and key == eng and eng == "tensor":
            return
        ck = self.clock[eng]
        k = (kind, key)
        if ck.get(k, -1) >= val:
            return
        waits.append((kind, key, val))
        if kind == "c":
            self.prog[key][val]["inc"] = True
        for kk, vv in snap.items():
            if ck.get(kk, -1) < vv:
                ck[kk] = vv
        ck[k] = val

    enabled = True
    dead = False

    def op(self, eng, fn, reads=(), writes=(), dma_sem=None, is_out=False):
        if not self.enabled or self.dead:
            return None
        waits = []
        for b in reads:
            self._need(eng, b.writer, waits)
        for b in writes:
            self._need(eng, b.writer, waits)
            for r in b.readers:
                self._need(eng, r, waits)
        if dma_sem is not None:
            self._need(eng, self.dma_last[dma_sem], waits)
        idx = len(self.prog[eng])
        ent = {"waits": waits, "fn": fn, "inc": False, "dma": None}
        self.prog[eng].append(ent)
        snap = dict(self.clock[eng])
        if dma_sem is None:
            stamp = ("c", eng, idx, snap)
        else:
            self.dma_count[dma_sem] += 1
            val = 16 * self.dma_count[dma_sem]
            ent["dma"] = (dma_sem, val)
            stamp = ("d", dma_sem, val, snap)
            self.dma_last[dma_sem] = stamp
            if is_out:
                self.out_stamps.append(stamp)
        for b in reads:
            b.readers.append(stamp)
        for b in writes:
            b.writer = stamp
            b.readers = []
        return stamp

    def emit(self, final_eng="sync"):
        nc = self.nc
        waits = []
        for st in self.out_stamps:
            self._need(final_eng, st, waits)
        self.prog[final_eng].append({"waits": waits, "fn": None, "inc": False, "dma": None})
        with contextlib.ExitStack() as es:
            csem, rank = {}, {}
            for e in ENGS:
                r = 0
                rank[e] = {}
                for i, ent in enumerate(self.prog[e]):
                    if ent["inc"]:
                        rank[e][i] = r
                        r += 1
                nsem = (r + SEM_CHUNK - 1) // SEM_CHUNK
                csem[e] = [es.enter_context(nc.semaphore(f"c_{e}_{k}")) for k in range(nsem)]
            dsem = [es.enter_context(nc.semaphore(f"d_{k}")) for k in range(len(self.dma_count))]
            block = es.enter_context(nc.Block())

            def mk(e):
                def body(engh):
                    for i, ent in enumerate(self.prog[e]):
                        for (kind, key, val) in ent["waits"]:
                            if kind == "c":
                                r = rank[key][val]
                                engh.wait_ge(csem[key][r // SEM_CHUNK], r % SEM_CHUNK + 1)
                            else:
                                engh.wait_ge(dsem[key], val)
                        if ent["fn"] is None:
                            continue
                        ins = ent["fn"](engh)
                        if ent["dma"] is not None:
                            ins.then_inc(dsem[ent["dma"][0]], 16)
                        elif ent["inc"]:
                            r = rank[e][i]
                            ins.then_inc(csem[e][r // SEM_CHUNK], 1)
                return body

            for e in ENGS:
                if self.prog[e]:
                    getattr(block, e)(mk(e))


class Tl:
    __slots__ = ("ap", "b", "subs")

    def __init__(self, ap, b):
        self.ap = ap
        self.b = b
        self.subs = None

    def split(self, n):
        self.subs = [Buf() for _ in range(n)]
        for sb_ in self.subs:
            sb_.readers = list(self.b.readers)
        return self

    def R(self, i):
        return [self.b, self.subs[i]]


class Arena:
    def __init__(self, tensor_bf16, nbytes):
        self.t = tensor_bf16
        self.n = nbytes
        self.live = []

    def view(self, off, shape, dt, name=""):
        n = 1
        for s in shape:
            n *= s
        size = n * (4 if dt == F32 else 2)
        assert off % 4 == 0 and off + size <= self.n, (name, off, size, self.n)
        ap = self.t[:, off // 2:(off + size) // 2]
        if dt == F32:
            ap = ap.bitcast(F32)
        if len(shape) == 2:
            ap = ap.rearrange("p (a b) -> p a b", a=shape[0])
        elif len(shape) == 3:
            ap = ap.rearrange("p (a b c) -> p a b c", a=shape[0], b=shape[1])
        buf = Buf(name)
        newlive = []
        for (s, e, b) in self.live:
            if s < off + size and off < e:
                if b.writer is not None:
                    buf.readers.append(b.writer)
                buf.readers.extend(b.readers)
                if not (s >= off and e <= off + size):
                    newlive.append((s, e, b))
            else:
                newlive.append((s, e, b))
        newlive.append((off, off + size, buf))
        self.live = newlive
        return Tl(ap, buf)


class Bump:
    def __init__(self, arena, base):
        self.a = arena
        self.base = base
        self.off = base

    def reset(self):
        self.off = self.base

    def get(self, shape, dt, name=""):
        n = 1
        for s in shape:
            n *= s
        size = (n * (4 if dt == F32 else 2) + 31) // 32 * 32
        t = self.a.view(self.off, shape, dt, name)
        self.off += size
        return t


def build(NSEQ, NT, DEPTH):
    NBLK = NT // TB
    NT128 = NT // 128
    nc = bass.Bass("TRN2", target_bir_lowering=False)
    din = lambda name, shape: nc.dram_tensor(name, shape, F32, kind="ExternalInput").ap()
    x_d = din("x", [NSEQ, NT, D_MODEL])
    p_d = din("p", [DEPTH, NSEQ, NT, 256])
    win_d = din("w_in", [DEPTH, D_MODEL, IN_COLS])
    wout_d = din("w_out", [DEPTH, D_MODEL, D_MODEL])
    bd_d = din("bd", [DEPTH, 128, 4 * 128])
    wg_d = din("w_gate", [DEPTH, D_MODEL, D_FF])
    wu_d = din("w_up", [DEPTH, D_MODEL, D_FF])
    wd_d = din("w_down", [DEPTH, D_FF, D_MODEL])
    wpg_d = din("w_pg", [DEPTH, D_MODEL, D_MODEL])
    wpp_d = din("w_pp", [DEPTH, 256, D_MODEL])
    pv_d = din("pv", [128, DEPTH * NPV + 8])
    cf_d = din("cf", [128, 3 * 128])
    cb_d = din("cb", [128, 3 * 128])
    sel_d = din("sel", [6, 6 * 128])
    y_d = nc.dram_tensor("y", [NSEQ, NT, D_MODEL], F32, kind="ExternalOutput").ap()
    if KDUMP:
        dbg_d = nc.dram_tensor("dbg", [128, 8192], F32, kind="ExternalOutput").ap()

    es = contextlib.ExitStack()
    with es:
        sb = lambda name, shape, dt: es.enter_context(nc.sbuf_tensor("s_" + name, shape, dt))
        S = Sched(nc)
        GF = DEPTH * NPV

        def static(name, shape, dt):
            return Tl(sb(name, shape, dt)[:], Buf(name))

        hT = sb("hT", [128, 8, NT], F32)
        hTb = [Buf(f"hT{i}") for i in range(NBLK)]
        cf = static("cf", [128, 3, 128], F32)
        cb = static("cb", [128, 3, 128], BF16)
        pv = static("pv", [128, DEPTH * NPV + 8], F32)
        dv = static("dv", [128, DEPTH, 12], F32)
        assert TB == 256
        uT_f = static("uT", [128, 1024], F32)
        ycat_f = static("ycat", [128, 1024], F32)
        uT = Tl(uT_f.ap.bitcast(BF16).rearrange("p (c t) -> p c t", c=8), uT_f.b)
        ycat = Tl(ycat_f.ap.bitcast(BF16).rearrange("p (c t) -> p c t", c=8), ycat_f.b)
        sq = static("sq", [128, 8, TB], BF16)
        rs0 = static("rs0", [128, TB], F32)
        rs1 = static("rs1", [128, TB], F32)
        ARENA_BYTES = 125 * 1024
        arena_t = sb("arena", [128, ARENA_BYTES // 2], BF16)
        A = Arena(arena_t, ARENA_BYTES)
        ps_t = [es.enter_context(nc.psum_tensor(f"ps{i}", [128, 512], F32)) for i in range(7)]
        psb_t = es.enter_context(nc.psum_tensor("psb", [128, 1024], BF16))
        pbuf = [[Buf(f"ps{i}_{h}") for h in range(2)] for i in range(7)]
        psbb = Buf("psb")

        def PS(bank, c0, c1, p0=0, p1=128):
            bs = [pbuf[bank][0], pbuf[bank][1]]
            return ps_t[bank][p0:p1, c0:c1], bs

        ident_f = cf.ap[:, 0, :]
        tri_f = cf.ap[:, 1, :]
        ones_f = cf.ap[:, 2, :]
        ident_b = cb.ap[:, 0, :]
        ones_b = cb.ap[:, 1, :]
        mneg_b = cb.ap[:, 2, :]

        def bl(xs):
            out = []
            for x_ in xs:
                if isinstance(x_, Tl):
                    out.append(x_.b)
                elif isinstance(x_, (list, tuple)):
                    out.extend(bl(x_))
                elif x_ is not None:
                    out.append(x_)
            return out

        def mm(out, lhsT, rhs, start, stop, r, w):
            S.op("tensor", lambda e: e.matmul(out, lhsT=lhsT, rhs=rhs, start=start, stop=stop), bl(r), bl(w))

        def tr(out, in_, ident, r, w):
            S.op("tensor", lambda e: e.transpose(out, in_, ident), bl(r), bl(w))

        def act(out, in_, func, r, w, bias=None, scale=None):
            kw = {}
            if bias is not None:
                kw["bias"] = bias
            if scale is not None:
                kw["scale"] = scale
            S.op("scalar", lambda e: e.activation(out=out, in_=in_, func=func, **kw), bl(r), bl(w))

        def tt(eng, out, in0, in1, op, r, w):
            S.op(eng, lambda e: e.tensor_tensor(out=out, in0=in0, in1=in1, op=op), bl(r), bl(w))

        def ts(eng, out, in0, s1, s2, op0, op1, r, w):
            if op1 is None:
                S.op(eng, lambda e: e.tensor_scalar(out=out, in0=in0, scalar1=s1, scalar2=None, op0=op0), bl(r), bl(w))
            else:
                S.op(eng, lambda e: e.tensor_scalar(out=out, in0=in0, scalar1=s1, scalar2=s2, op0=op0, op1=op1), bl(r), bl(w))

        def stt(out, in0, scalar, in1, op0, op1, r, w):
            S.op("vector", lambda e: e.scalar_tensor_tensor(out=out, in0=in0, scalar=scalar, in1=in1, op0=op0, op1=op1), bl(r), bl(w))

        def cp(eng, out, in_, r, w):
            if eng == "scalar":
                act(out, in_, AF.Copy, r, w)
            else:
                S.op(eng, lambda e: e.tensor_copy(out=out, in_=in_), bl(r), bl(w))

        def mset(eng, ap, val, w):
            S.op(eng, lambda e: e.memset(ap, val), [], bl(w))

        def scan(out, d0, d1, init, r, w):
            S.op("vector", lambda e: e.tensor_tensor_scan(out=out, data0=d0, data1=d1, initial=init, op0=ALU.mult, op1=ALU.add),
                 bl(r), bl(w))

        def dma(q, out, in_, r, w, is_out=False, cast=False):
            kw = {"max_dma_last_dim": 8192} if cast else {}
            return S.op(q, lambda e: e.dma_start(out=out, in_=in_, **kw), bl(r), bl(w), dma_sem=S.pool_sem(q), is_out=is_out)

        dbg_state = [0]
        if KDUMP:
            dbgt = static("dbgt", [128, 8192], F32)

        def dump(name, ap, r, p0=0, p1=128):
            if not KDUMP or name in DUMPS or S.dead or not S.enabled:
                return
            n = ap.shape[-1]
            c0 = dbg_state[0]
            DUMPS[name] = (c0, n, p0, p1)
            dbg_state[0] += n
            S.op("vector", lambda e: e.tensor_copy(out=dbgt.ap[p0:p1, c0:c0 + n], in_=ap), bl(r), [dbgt.b])

        dma("sync", cf.ap, cf_d.rearrange("p (a b) -> p a b", a=3), [], [cf])
        dma("sync", pv.ap, pv_d, [], [pv])
        dma("gpsimd", cb.ap, cb_d.rearrange("p (a b) -> p a b", a=3), [], [cb], cast=True)
        for l in range(DEPTH):
            o = l * NPV
            act(dv.ap[:, l, 0:6], pv.ap[:, o + ALOG:o + ALOG + 6], AF.Exp, [pv], [dv])
            ts("vector", dv.ap[:, l, 0:6], dv.ap[:, l, 0:6], -1.0, None, ALU.mult, None, [dv], [dv])
            act(dv.ap[:, l, 6:8], pv.ap[:, o + LAM:o + LAM + 2], AF.Exp, [pv], [dv], scale=-1.0)
            act(dv.ap[:, l, 6:8], dv.ap[:, l, 6:8], AF.Ln, [dv], [dv], bias=1.0)
            ts("vector", dv.ap[:, l, 6:8], dv.ap[:, l, 6:8], -8.0, None, ALU.mult, None, [dv], [dv])
            ts("vector", dv.ap[:, l, 8:10], pv.ap[:, o + LBA:o + LBA + 2], -1.0, None, ALU.mult, None, [pv], [dv])
            ts("vector", dv.ap[:, l, 10:12], pv.ap[:, o + LBX:o + LBX + 2], -1.0, None, ALU.mult, None, [pv], [dv])

        if KSTOP <= 1:
            S.dead = True
        def rmsnorm(src_aps, src_bufs, gcol0, nch, width, out_aps, out_tl, ncols, psum_loc, sq_eng="gpsimd"):
            bank, c0 = psum_loc
            pso, psb_ = PS(bank, c0, c0 + ncols)
            for c in range(nch):
                eng = sq_eng if c % 2 == 0 else "vector"
                if eng == "scalar":
                    act(sq.ap[:, c, 0:ncols], src_aps[c], AF.Square, src_bufs, [sq])
                else:
                    tt(eng, sq.ap[:, c, 0:ncols], src_aps[c], src_aps[c], ALU.mult, src_bufs, [sq])
            for c in range(nch):
                mm(pso, ones_b, sq.ap[:, c, 0:ncols], c == 0, c == nch - 1, [cb, sq], psb_)
            act(rs0.ap[:, 0:ncols], pso, AF.Ln, psb_, [rs0], bias=EPS_AP, scale=1.0 / width)
            act(rs1.ap[:, 0:ncols], rs0.ap[:, 0:ncols], AF.Exp, [rs0], [rs1], scale=-0.5)
            for c in range(nch):
                stt(out_aps[c], src_aps[c], pv.ap[:, gcol0 + c:gcol0 + c + 1], rs1.ap[:, 0:ncols], ALU.mult, ALU.mult,
                    src_bufs + [pv, rs1], [out_tl])

        epst = static("epst", [128, 1], F32)
        mset("vector", epst.ap, EPS, [epst])
        EPS_AP = epst.ap[:, 0:1]

        MP = Bump(A, 0)

        def mixer_persist():
            MP.reset()
            d = {}
            d["win"] = MP.get([8, IN_COLS], BF16, "win")
            d["wout"] = MP.get([8, D_MODEL], BF16, "wout")
            d["bd"] = MP.get([4, 128], BF16, "bd")
            d["KT"] = MP.get([3, NT], BF16, "KT")
            d["V"] = MP.get([NT128, 384], BF16, "V")
            d["ncum"] = MP.get([NT128, 6], F32, "ncum")
            d["xraw"] = MP.get([7, TB + 3], F32, "xraw")
            d["lraw"] = MP.get([2, TB + 3], F32, "lraw")
            d["st"] = MP.get([384], F32, "st")
            d["stb"] = MP.get([384], BF16, "stb")
            d["lcar"] = MP.get([2], F32, "lcar")
            d["fcar"] = MP.get([6], F32, "fcar")
            d["fref"] = MP.get([6], F32, "fref")
            return d

        _tmp = mixer_persist()
        SCR_BASE = MP.off
        A.live = []
        SC = Bump(A, SCR_BASE)
        FFN_B = Bump(A, 0)
        PLE_BASE = SCR_BASE
        PLE_B = Bump(A, PLE_BASE)

        def load_mixer_weights(l):
            d = mixer_persist()
            d["win"].split(8)
            d["wout"].split(8)
            for c in range(8):
                dma("gpsimd", d["win"].ap[:, c, :], win_d[l, c * 128:(c + 1) * 128, :], [], [d["win"].subs[c]], cast=True)
            for c in range(8):
                dma("gpsimd", d["wout"].ap[:, c, :], wout_d[l, c * 128:(c + 1) * 128, :], [], [d["wout"].subs[c]], cast=True)
            dma("gpsimd", d["bd"].ap, bd_d[l].rearrange("p (a b) -> p a b", a=4), [], [d["bd"]], cast=True)
            return d

        for s in range(NSEQ):
            SC.reset()
            xst = [uT_f, ycat_f]
            for t in range(NT128):
                st_ = xst[t % 2]
                dma("sync", st_.ap, x_d[s, t * 128:(t + 1) * 128, :], [], [st_])
                for half in range(2):
                    bank = (2 * t + half) % 4
                    pso, pbs = PS(bank, 0, 512)
                    for j in range(4):
                        c = half * 4 + j
                        tr(ps_t[bank][:, j * 128:(j + 1) * 128], st_.ap[:, c * 128:(c + 1) * 128], ident_f, [st_, cf], pbs)
                    eng = "vector" if half == 0 else "scalar"
                    cp(eng, hT[:, half * 4:half * 4 + 4, t * 128:(t + 1) * 128],
                       pso.rearrange("p (a b) -> p a b", a=4), pbs, [hTb[(t * 128) // TB]])

            if KSTOP <= 2:
                S.dead = True
            for l in range(DEPTH):
                o = l * NPV
                W = PREFETCHED[0] if PREFETCHED[0] is not None else load_mixer_weights(l)
                PREFETCHED[0] = None
                win, wout, bdm = W["win"], W["wout"], W["bd"]
                KT, V, ncum, xraw, lraw = W["KT"], W["V"], W["ncum"], W["xraw"], W["lraw"]
                st, stb, lcar, fcar = W["st"], W["stb"], W["lcar"], W["fcar"]
                fref = W["fref"]
                mset("vector", xraw.ap[:, :, 0:3], 0.0, [xraw])
                mset("vector", lraw.ap[:, :, 0:3], 0.0, [lraw])
                mset("gpsimd", st.ap, 0.0, [st])
                mset("gpsimd", stb.ap, 0.0, [stb])
                mset("gpsimd", lcar.ap, 0.0, [lcar])
                mset("gpsimd", fcar.ap, 0.0, [fcar])

                for blk in range(NBLK):
                    t0 = blk * TB
                    hb = hTb[blk]
                    hsl = [hT[:, c, t0:t0 + TB] for c in range(8)]
                    if blk == 0:
                        rmsnorm(hsl, [hb], o + G1, 8, D_MODEL, [uT.ap[:, c, :] for c in range(8)], uT, TB, (6, 256))

                    slot = [0]

                    def proj(col0, ncolsM=128):
                        i = slot[0]
                        slot[0] += 1
                        bank, half = i % 2, (i // 2) % 2
                        pso, pbs = PS(bank, half * 256, half * 256 + 256)
                        for c in range(8):
                            mm(pso, win.ap[:, c, col0:col0 + ncolsM], uT.ap[:, c, :], c == 0, c == 7, win.R(c) + [uT], pbs)
                        return pso, pbs

                    SC.reset()
                    if "ssd" not in DBG:
                        for c in range(3):
                            mset("vector", ycat.ap[:, c, :], 0.0, [ycat])
                    S.enabled = "ssd" in DBG
                    xbc = SC.get([7, TB], BF16, "xbc")
                    zs = SC.get([3, TB], F32, "zs")
                    yssd = SC.get([3, TB], F32, "yssd")
                    cacc = [SC.get([TB], F32, f"cacc{i}") for i in range(2)]
                    sm = SC.get([8, 6], F32, "sm")
                    xdt = SC.get([384], BF16, "xdt")
                    xdts = SC.get([384], BF16, "xdts")
                    btok = SC.get([256], BF16, "btok")
                    Dm = [SC.get([128], F32, f"D{i}") for i in range(2)]
                    MT = [SC.get([128], BF16, f"MT{i}") for i in range(2)]
                    ebc = [SC.get([128], F32, f"ebc{i}") for i in range(2)]
                    Cs = [SC.get([128], BF16, f"Cs{i}") for i in range(2)]
                    stt_tmp = SC.get([384], F32, "sttmp")
                    smp = SC.get([24], F32, "smp")

                    for c in range(3):
                        pso, pbs = proj(OZ + c * 128)
                        act(zs.ap[:, c, :], pso, AF.Silu, pbs, [zs])
                    xbc.split(7)
                    xbc.b.readers = []
                    xraw_s = [Buf() for _ in range(7)]
                    for sb_ in xraw_s:
                        sb_.writer = xraw.b.writer
                        sb_.readers = list(xraw.b.readers)

                    def silu_c(c):
                        act(xbc.ap[:, c, :], cacc[c % 2].ap, AF.Silu, [cacc[c % 2]], [xbc.subs[c]])

                    for c in range(7):
                        pso, pbs = proj(OXS + c * 128)
                        cp("scalar", xraw.ap[:, c, 3:3 + TB], pso, pbs, [xraw_s[c]])
                        if c > 0:
                            silu_c(c - 1)
                        ca = cacc[c % 2]
                        wc = lambda k: pv.ap[:, o + SCW + c * 4 + k:o + SCW + c * 4 + k + 1]
                        ts("vector", ca.ap, xraw.ap[:, c, 0:TB], wc(0), pv.ap[:, o + SCB + c:o + SCB + c + 1], ALU.mult, ALU.add,
                           [xraw_s[c], pv], [ca])
                        for k in range(1, 4):
                            stt(ca.ap, xraw.ap[:, c, k:k + TB], wc(k), ca.ap, ALU.mult, ALU.add, [xraw_s[c], pv, ca], [ca])
                    silu_c(6)
                    cp("gpsimd", xraw.ap[:, :, 0:3], xraw.ap[:, :, TB:TB + 3], xraw_s, xraw_s + [xraw])
                    cp("vector", fref.ap, fcar.ap, [fcar], [fref])
                    for j in range(TB // 128):
                        cs = slice(j * 128, (j + 1) * 128)
                        tg = t0 + j * 128
                        pso, pbs = PS(2, 0, 396)
                        for c in range(8):
                            mm(pso, uT.ap[:, c, cs], win.ap[:, c, OV:OV + 396], c == 0, c == 7, win.R(c) + [uT], pbs)
                        kb = tg // 128
                        cp("vector", V.ap[:, kb, :], ps_t[2][:, 0:384], pbs, [V])
                        dt, adt, nacs, tmp6, decs, cdec, sdt, nlf = [sm.ap[:, i, :] for i in range(8)]
                        tt("vector", tmp6, ps_t[2][:, 384:390], pv.ap[:, o + DTB:o + DTB + 6], ALU.add, pbs + [pv], [sm])
                        tt("vector", nlf, ps_t[2][:, 390:396], pv.ap[:, o + FBF:o + FBF + 6], ALU.add, pbs + [pv], [sm])
                        act(tmp6, tmp6, AF.Exp, [sm], [sm])
                        act(nlf, nlf, AF.Exp, [sm], [sm], scale=-1.0)
                        act(dt, tmp6, AF.Ln, [sm], [sm], bias=1.0)
                        act(nlf, nlf, AF.Ln, [sm], [sm], bias=1.0)
                        tt("vector", adt, dt, dv.ap[:, l, 0:6], ALU.mult, [sm, dv], [sm])
                        p4, p4b = PS(4, 384, 408)
                        mm(ps_t[4][:, 384:390], tri_f, adt, True, True, [cf, sm], p4b)
                        mm(ps_t[4][:, 390:396], ones_f, adt, True, True, [cf, sm], p4b)
                        mm(ps_t[4][:, 396:402], tri_f, nlf, True, True, [cf, sm], p4b)
                        mm(ps_t[4][:, 402:408], ones_f, nlf, True, True, [cf, sm], p4b)
                        cp("vector", smp.ap, ps_t[4][:, 384:408], p4b, [smp])
                        ts("vector", nacs, smp.ap[:, 0:6], -1.0, None, ALU.mult, None, [smp], [sm])
                        tt("vector", tmp6, smp.ap[:, 6:12], nacs, ALU.add, [smp, sm], [sm])
                        act(decs, tmp6, AF.Exp, [sm], [sm])
                        act(cdec, smp.ap[:, 6:12], AF.Exp, [smp], [sm])
                        tt("vector", sdt, dt, decs, ALU.mult, [sm], [sm])
                        tt("vector", ncum.ap[:, kb, :], smp.ap[:, 12:18], fcar.ap, ALU.add, [smp, fcar], [ncum])
                        tt("vector", fcar.ap, smp.ap[:, 18:24], fcar.ap, ALU.add, [smp, fcar], [fcar])
                        for c in range(5):
                            tr(psb_t[:, c * 128:(c + 1) * 128], xbc.ap[:, c, cs], ident_b, xbc.R(c) + [cb], [psbb])
                        xs_tok = psb_t[:, 0:384].rearrange("p (h e) -> p h e", h=6)
                        tt("vector", xdt.ap.rearrange("p (h e) -> p h e", h=6), xs_tok, dt.unsqueeze(2).to_broadcast([128, 6, 64]),
                           ALU.mult, [psbb, sm], [xdt])
                        tt("vector", xdts.ap.rearrange("p (h e) -> p h e", h=6), xs_tok, sdt.unsqueeze(2).to_broadcast([128, 6, 64]),
                           ALU.mult, [psbb, sm], [xdts])
                        cp("vector", btok.ap, psb_t[:, 384:640], [psbb], [btok])
                        pg, pgb = PS(3, 0, 256)
                        for g in range(2):
                            mm(ps_t[3][:, g * 128:(g + 1) * 128], xbc.ap[:, 3 + g, cs], xbc.ap[:, 5 + g, cs], True, True, xbc.R(3 + g) + xbc.R(5 + g), pgb)
                        py, pyb = PS(4, 0, 384)

                        def stageA(h):
                            g = h // 3
                            i2 = h % 2
                            eb = 6 if i2 == 0 else 2
                            pE, pEb = PS(eb, 0, 256)
                            adt_b = adt[:, h:h + 1].to_broadcast([128, 128])
                            mm(ps_t[eb][:, 0:128], adt_b, tri_f, True, False, [sm, cf], pEb)
                            mm(ps_t[eb][:, 0:128], ident_b, mneg_b, False, True, [cb], pEb)
                            mm(ps_t[eb][:, 128:256], adt_b, tri_f, True, True, [sm, cf], pEb)
                            act(Dm[i2].ap, ps_t[eb][:, 0:128], AF.Exp, pEb + [sm], [Dm[i2]], bias=nacs[:, h:h + 1])
                            act(ebc[i2].ap, ps_t[eb][:, 128:256], AF.Exp, pEb, [ebc[i2]])
                            tt("vector", MT[i2].ap, ps_t[3][:, g * 128:(g + 1) * 128], Dm[i2].ap, ALU.mult, pgb + [Dm[i2]], [MT[i2]])
                            tt("gpsimd", Cs[i2].ap, xbc.ap[:, 5 + g, cs], ebc[i2].ap, ALU.mult, xbc.R(5 + g) + [ebc[i2]], [Cs[i2]])

                        def stageB(h):
                            i2 = h % 2
                            hp = (h % 2) * 64
                            yo = ps_t[4][hp:hp + 64, (h // 2) * 128:(h // 2) * 128 + 128]
                            mm(yo, xdt.ap[:, h * 64:(h + 1) * 64], MT[i2].ap, True, False, [xdt, MT[i2]], pyb)
                            mm(yo, stb.ap[:, h * 64:(h + 1) * 64], Cs[i2].ap, False, True, [stb, Cs[i2]], pyb)

                        stageA(0)
                        for h in range(6):
                            if h + 1 < 6:
                                stageA(h + 1)
                            stageB(h)
                        for c in range(3):
                            stt(yssd.ap[:, c, cs], xbc.ap[:, c, cs], pv.ap[:, o + DSK + c:o + DSK + c + 1],
                                ps_t[4][:, c * 128:(c + 1) * 128], ALU.mult, ALU.add, xbc.R(c) + [pv] + pyb, [yssd])
                        pst, pstb = PS(5, 0, 384)
                        for g in range(2):
                            mm(ps_t[5][:, g * 192:(g + 1) * 192], btok.ap[:, g * 128:(g + 1) * 128], xdts.ap[:, g * 192:(g + 1) * 192],
                               True, True, [btok, xdts], pstb)
                        tt("vector", stt_tmp.ap.rearrange("p (h e) -> p h e", h=6), st.ap.rearrange("p (h e) -> p h e", h=6),
                           cdec.unsqueeze(2).to_broadcast([128, 6, 64]), ALU.mult, [st, sm], [stt_tmp])
                        tt("vector", st.ap, stt_tmp.ap, pst, ALU.add, [stt_tmp] + pstb, [st])
                        cp("gpsimd", stb.ap, st.ap, [st], [stb])
                    dump("yssd1_pre", yssd.ap[:, 1, :], [yssd])
                    dump("zs1", zs.ap[:, 1, :], [zs])
                    for c in range(3):
                        tt("gpsimd", yssd.ap[:, c, :], yssd.ap[:, c, :], zs.ap[:, c, :], ALU.mult, [yssd, zs], [yssd])
                    rmsnorm([yssd.ap[:, c, :] for c in range(3)], [yssd], o + SNG, 3, 384,
                            [ycat.ap[:, c, :] for c in range(3)], ycat, TB, (6, 256))

                    dump("ycat1_early", ycat.ap[:, 1, :], [ycat])
                    dump("yssd1", yssd.ap[:, 1, :], [yssd])
                    S.enabled = True
                    SC.reset()
                    LT = [{k: SC.get([TB], BF16 if k == "xlb" else F32, f"{k}{c}") for k in ("xl", "xlb", "rg", "ig", "aa", "mu", "hl", "g1", "g2")}
                          for c in range(2)]
                    lraw_s = [Buf() for _ in range(2)]
                    for sb_ in lraw_s:
                        sb_.writer = lraw.b.writer
                        sb_.readers = list(lraw.b.readers)
                    pgl = [None, None]

                    def l1(c):
                        T_ = LT[c]
                        pso, pbs = proj(OLX + c * 128)
                        cp("scalar", lraw.ap[:, c, 3:3 + TB], pso, pbs, [lraw_s[c]])
                        pgl[c] = proj(OLG + c * 128)
                        cp("scalar", T_["g1"].ap, pgl[c][0], pgl[c][1], [T_["g1"]])

                    def l2(c):
                        T_ = LT[c]
                        wc = lambda k: pv.ap[:, o + LCW + c * 4 + k:o + LCW + c * 4 + k + 1]
                        ts("vector", T_["xl"].ap, lraw.ap[:, c, 0:TB], wc(0), pv.ap[:, o + LCB + c:o + LCB + c + 1], ALU.mult, ALU.add,
                           [lraw_s[c], pv], [T_["xl"]])
                        for k in range(1, 4):
                            stt(T_["xl"].ap, lraw.ap[:, c, k:k + TB], wc(k), T_["xl"].ap, ALU.mult, ALU.add, [lraw_s[c], pv, T_["xl"]], [T_["xl"]])
                        cp("gpsimd", T_["xlb"].ap, T_["xl"].ap, [T_["xl"]], [T_["xlb"]])
                        tt("gpsimd", T_["g2"].ap, T_["g1"].ap, T_["g1"].ap, ALU.mult, [T_["g1"]], [T_["g2"]])
                        ts("vector", T_["g2"].ap, T_["g2"].ap, 0.044715, 1.0, ALU.mult, ALU.add, [T_["g2"]], [T_["g2"]])
                        tt("gpsimd", T_["g2"].ap, T_["g2"].ap, T_["g1"].ap, ALU.mult, [T_["g2"], T_["g1"]], [T_["g2"]])

                    def sig3(out_ap, in_ap, r, w, scale, bias=None):
                        act(out_ap, in_ap, AF.Exp, r, w, scale=scale, bias=bias)
                        act(out_ap, out_ap, AF.Ln, w, w, bias=1.0)
                        act(out_ap, out_ap, AF.Exp, w, w, scale=-1.0)

                    def l3(c):
                        T_ = LT[c]
                        pa, pab = PS(2, c * 256, c * 256 + 256)
                        mm(pa, bdm.ap[:, c, :], T_["xlb"].ap, True, True, [bdm, T_["xlb"]], pab)
                        sig3(T_["rg"].ap, pa, pab + [dv], [T_["rg"]], -1.0, dv.ap[:, l, 8 + c:9 + c])
                        mm(pa, bdm.ap[:, 2 + c, :], T_["xlb"].ap, True, True, [bdm, T_["xlb"]], pab)
                        sig3(T_["ig"].ap, pa, pab + [dv], [T_["ig"]], -1.0, dv.ap[:, l, 10 + c:11 + c])
                        sig3(T_["g2"].ap, T_["g2"].ap, [T_["g2"]], [T_["g2"]], -1.5957691216057308)

                    def l4(c):
                        T_ = LT[c]
                        ts("vector", T_["aa"].ap, T_["rg"].ap, dv.ap[:, l, 6 + c:7 + c], None, ALU.mult, None, [T_["rg"], dv], [T_["aa"]])
                        act(T_["aa"].ap, T_["aa"].ap, AF.Exp, [T_["aa"]], [T_["aa"]])
                        tt("gpsimd", T_["ig"].ap, T_["ig"].ap, T_["xl"].ap, ALU.mult, [T_["ig"], T_["xl"]], [T_["ig"]])
                        tt("gpsimd", T_["g2"].ap, T_["g2"].ap, T_["g1"].ap, ALU.mult, [T_["g2"], T_["g1"]], [T_["g2"]])

                    def l5(c):
                        T_ = LT[c]
                        tt("gpsimd", T_["mu"].ap, T_["aa"].ap, T_["aa"].ap, ALU.mult, [T_["aa"]], [T_["mu"]])
                        act(T_["mu"].ap, T_["mu"].ap, AF.Ln, [T_["mu"]], [T_["mu"]], bias=1.0, scale=-1.0)
                        act(T_["mu"].ap, T_["mu"].ap, AF.Exp, [T_["mu"]], [T_["mu"]], scale=0.5)

                    def l6(c):
                        T_ = LT[c]
                        tt("vector", T_["mu"].ap, T_["mu"].ap, T_["ig"].ap, ALU.mult, [T_["mu"], T_["ig"]], [T_["mu"]])
                        scan(T_["hl"].ap, T_["aa"].ap, T_["mu"].ap, lcar.ap[:, c:c + 1], [T_["aa"], T_["mu"], lcar], [T_["hl"]])
                        cp("vector", lcar.ap[:, c:c + 1], T_["hl"].ap[:, TB - 1:TB], [T_["hl"]], [lcar])
                        tt("vector", T_["hl"].ap, T_["hl"].ap, T_["g2"].ap, ALU.mult, [T_["hl"], T_["g2"]], [T_["hl"]])

                    def lru_gen():
                        for st_fn in (l1, l2, l3, l4, l5, l6):
                            for c in range(2):
                                st_fn(c)
                                yield
                            if st_fn is l2:
                                cp("gpsimd", lraw.ap[:, :, 0:3], lraw.ap[:, :, TB:TB + 3], lraw_s, lraw_s + [lraw])
                        hl_aps = [LT[c]["hl"].ap for c in range(2)]
                        hl_bufs = [LT[c]["hl"] for c in range(2)]
                        rmsnorm(hl_aps, hl_bufs, o + LNG, 2, 256,
                                [ycat.ap[:, 3 + c, :] for c in range(2)], ycat, TB, (2, 256))
                        yield

                    lgen = lru_gen()

                    qT = SC.get([3, TB], BF16, "qT")
                    pT = [SC.get([TB], BF16, f"pT{i}") for i in range(4)]
                    lnd = SC.get([TB], F32, "lnd")
                    yfox = SC.get([3, TB], F32, "yfox")
                    for c in range(3):
                        pso, pbs = proj(OQ + c * 128)
                        cp("scalar", qT.ap[:, c, :], pso, pbs, [qT])
                    for c in range(3):
                        pso, pbs = proj(OK_ + c * 128)
                        cp("scalar", KT.ap[:, c, t0:t0 + TB], pso, pbs, [KT])
                    nkb = (t0 + TB) // 128
                    nb = SC.get([NT128, 6], F32, "nb")
                    tt("vector", nb.ap[:, 0:nkb, :], ncum.ap[:, 0:nkb, :], fref.ap.unsqueeze(1).to_broadcast([128, nkb, 6]), ALU.subtract,
                       [ncum, fref], [nb])
                    pi = 0
                    for c in range(3):
                        py, pyb = PS(5, 0, 256)
                        pdn, pdb = PS(4, 0, 256)
                        its = [(hh, kb) for hh in range(2) for kb in range(nkb)]

                        def stA(idx, it):
                            hh, kb = it
                            h = 2 * c + hh
                            hp = hh * 64
                            rel = kb * 128 - t0
                            q0 = 0 if rel < 0 else rel
                            sbank = 6 if idx % 2 == 0 else 3
                            psS, psSb = PS(sbank, 0, 256)
                            so = ps_t[sbank][:, q0:TB]
                            diag = rel >= 0
                            mm(so, KT.ap[hp:hp + 64, c, kb * 128:(kb + 1) * 128], qT.ap[hp:hp + 64, c, q0:TB], True, not diag,
                               [KT, qT], psSb)
                            if diag:
                                mm(ps_t[sbank][:, q0:q0 + 128], ident_b, mneg_b, False, True, [cb], psSb)
                            pt = pT[idx % 4]
                            if q0 > 0:
                                mset("gpsimd", pt.ap[:, 0:q0], 0.0, [pt])
                            act(pt.ap[:, q0:TB], so, AF.Exp, psSb + [nb], [pt], bias=nb.ap[:, kb, h:h + 1], scale=0.125)

                        def stB(idx, it):
                            hh, kb = it
                            h = 2 * c + hh
                            hp = hh * 64
                            pt = pT[idx % 4]
                            first, last = kb == 0, kb == nkb - 1
                            mm(ps_t[5][hp:hp + 64, 0:TB], V.ap[:, kb, h * 64:(h + 1) * 64], pt.ap[:, 0:TB], first, last, [V, pt], pyb)
                            mm(ps_t[4][hp:hp + 64, 0:TB], ones_b[:, 0:64], pt.ap[:, 0:TB], first, last, [cb, pt], pdb)

                        stA(pi, its[0])
                        for i_, it in enumerate(its):
                            if i_ + 1 < len(its):
                                stA(pi + i_ + 1, its[i_ + 1])
                            stB(pi + i_, it)
                            next(lgen, None)
                        pi += len(its)
                        act(lnd.ap, pdn, AF.Ln, pdb, [lnd])
                        act(lnd.ap, lnd.ap, AF.Exp, [lnd], [lnd], scale=-1.0)
                        tt("vector", yfox.ap[:, c, :], py, lnd.ap, ALU.mult, pyb + [lnd], [yfox])
                    for _ in lgen:
                        pass
                    dump("qT0", qT.ap[:, 0, :], [qT])
                    dump("KT0", KT.ap[:, 0, 0:TB], [KT])
                    dump("nc0", ncum.ap[:, 0, :], [ncum])
                    dump("nc1", ncum.ap[:, 1, :], [ncum])
                    dump("V0", V.ap[:, 0, :], [V])
                    dump("V1", V.ap[:, 1, :], [V])
                    dump("yfox0", yfox.ap[:, 0, :], [yfox])
                    rmsnorm([yfox.ap[:, c, :] for c in range(3)], [yfox], o + FNG, 3, 384,
                            [ycat.ap[:, 5 + c, :] for c in range(3)], ycat, TB, (6, 256))

                    S.enabled = True
                    if blk + 1 < NBLK:
                        t1 = (blk + 1) * TB
                        rmsnorm([hT[:, c, t1:t1 + TB] for c in range(8)], [hTb[blk + 1]], o + G1, 8, D_MODEL,
                                [uT.ap[:, c, :] for c in range(8)], uT, TB, (6, 256))
                    for c in range(8):
                        dump(f"ycat{c}", ycat.ap[:, c, :], [ycat])
                    for dc in range(8):
                        bank, half = dc % 2, (dc // 2) % 2
                        pso, pbs = PS(bank, half * 256, half * 256 + 256)
                        for c in range(8):
                            mm(pso, wout.ap[:, c, dc * 128:(dc + 1) * 128], ycat.ap[:, c, :], c == 0, c == 7, wout.R(c) + [ycat], pbs)
                        tt("vector", hsl[dc], hsl[dc], pso, ALU.add, [hb] + pbs, [hb])

                if KSTOP <= 3:
                    S.dead = True
                FFN_B.reset()
                uTs = FFN_B.get([8, NT], BF16, "uTs")
                wgt = [FFN_B.get([8, 512], BF16, f"wg{i}") for i in range(2)]
                wut = [FFN_B.get([8, 512], BF16, f"wu{i}") for i in range(2)]
                wdt = [FFN_B.get([4, D_MODEL], BF16, f"wd{i}") for i in range(2)]
                actb = FFN_B.get([4, 512], BF16, "actb")
                sgt = [FFN_B.get([512], F32, f"sg{i}") for i in range(2)]
                assert FFN_B.off <= PLE_BASE, (FFN_B.off, PLE_BASE)
                PLE_B.reset()
                wpg = PLE_B.get([8, D_MODEL], BF16, "wpg")
                wpp = PLE_B.get([2, D_MODEL], BF16, "wpp")
                pTt = PLE_B.get([2, TB], BF16, "pTt")
                pTt2 = PLE_B.get([2, TB], BF16, "pTt2")
                pst_ = [PLE_B.get([256], F32, "pst0")]
                gat = [PLE_B.get([TB], F32, f"gat{i}") for i in range(2)]
                wpg.split(8)
                wpp.split(2)
                for c in range(8):
                    dma("gpsimd", wpg.ap[:, c, :], wpg_d[l, c * 128:(c + 1) * 128, :], [], [wpg.subs[c]], cast=True)
                for c in range(2):
                    dma("gpsimd", wpp.ap[:, c, :], wpp_d[l, c * 128:(c + 1) * 128, :], [], [wpp.subs[c]], cast=True)
                S.enabled = "ffn" in DBG
                uTs.split(NT // 512)

                def ffn_norm(tb4_):
                    for blk in (2 * tb4_, 2 * tb4_ + 1):
                        t0 = blk * TB
                        rmsnorm([hT[:, c, t0:t0 + TB] for c in range(8)], [hTb[blk]], o + G2, 8, D_MODEL,
                                [uTs.ap[:, c, t0:t0 + TB] for c in range(8)], Tl(None, uTs.subs[tb4_]), TB, (6, 256), sq_eng="scalar")
                groups = [(g * 512, 512) for g in range(5)] + [(2560, 256)]
                NT512 = NT // 512
                for gi, (f0, fw_) in enumerate(groups):
                    wg_, wu_, wd_ = wgt[gi % 2], wut[gi % 2], wdt[gi % 2]
                    nj = fw_ // 128
                    wg_.split(8)
                    wu_.split(8)
                    wd_.split(4)
                    wg_.b.readers = []
                    wu_.b.readers = []
                    wd_.b.readers = []
                    for c in range(8):
                        dma("gpsimd", wg_.ap[:, c, 0:fw_], wg_d[l, c * 128:(c + 1) * 128, f0:f0 + fw_], [], [wg_.subs[c]], cast=True)
                    for c in range(8):
                        dma("gpsimd", wu_.ap[:, c, 0:fw_], wu_d[l, c * 128:(c + 1) * 128, f0:f0 + fw_], [], [wu_.subs[c]], cast=True)
                    for j in range(nj):
                        dma("gpsimd", wd_.ap[:, j, :], wd_d[l, f0 + j * 128:f0 + (j + 1) * 128, :], [], [wd_.subs[j]], cast=True)
                    for tb4 in range(NT512):
                        tsl = slice(tb4 * 512, (tb4 + 1) * 512)
                        if gi == 0:
                            if tb4 == 0:
                                ffn_norm(0)
                            if tb4 + 1 < NT512:
                                ffn_norm(tb4 + 1)
                        for j in range(nj):
                            pg_, pgb_ = PS(j % 2, 0, 512)
                            pu_, pub_ = PS(2 + j % 2, 0, 512)
                            for c in range(8):
                                mm(pg_, wg_.ap[:, c, j * 128:(j + 1) * 128], uTs.ap[:, c, tsl], c == 0, c == 7, wg_.R(c) + uTs.R(tb4), pgb_)
                            for c in range(8):
                                mm(pu_, wu_.ap[:, c, j * 128:(j + 1) * 128], uTs.ap[:, c, tsl], c == 0, c == 7, wu_.R(c) + uTs.R(tb4), pub_)
                            sg_ = sgt[j % 2]
                            act(sg_.ap, pg_, AF.Silu, pgb_, [sg_])
                            tt("vector", actb.ap[:, j, :], sg_.ap, pu_, ALU.mult, [sg_] + pub_, [actb])
                        for dc in range(8):
                            pd_, pdb_ = PS(4 + dc % 2, 0, 512)
                            for j in range(nj):
                                mm(pd_, wd_.ap[:, j, dc * 128:(dc + 1) * 128], actb.ap[:, j, :], j == 0, j == nj - 1, wd_.R(j) + [actb], pdb_)
                            hbs = [hTb[2 * tb4], hTb[2 * tb4 + 1]]
                            tt("vector", hT[:, dc, tsl], hT[:, dc, tsl], pd_, ALU.add, hbs + pdb_, hbs)

                S.enabled = True
                nxt = None
                if l + 1 < DEPTH:
                    nxt = l + 1
                elif s + 1 < NSEQ:
                    nxt = 0
                if nxt is not None:
                    PREFETCHED[0] = load_mixer_weights(nxt)
                S.enabled = "ple" in DBG
                pTts = [pTt, pTt2]

                def ple_pre(blk):
                    t0 = blk * TB
                    hb = hTb[blk]
                    hsl = [hT[:, c, t0:t0 + TB] for c in range(8)]
                    ub = uT if blk % 2 == 0 else ycat
                    rmsnorm(hsl, [hb], o + G3, 8, D_MODEL, [ub.ap[:, c, :] for c in range(8)], ub, TB, (6, 256), sq_eng="scalar")
                    pTb = pTts[blk % 2]
                    for j in range(TB // 128):
                        ps_ = pst_[0]
                        dma("sync", ps_.ap, p_d[l, s, t0 + j * 128:t0 + (j + 1) * 128, :], [], [ps_])
                        pt_, ptb = PS(4, 0, 256)
                        for c in range(2):
                            tr(ps_t[4][:, c * 128:(c + 1) * 128], ps_.ap[:, c * 128:(c + 1) * 128], ident_f, [ps_, cf], ptb)
                        cp("scalar", pTb.ap[:, :, j * 128:(j + 1) * 128], pt_.rearrange("p (a b) -> p a b", a=2), ptb, [pTb])

                def ple_main(blk):
                    t0 = blk * TB
                    hb = hTb[blk]
                    hsl = [hT[:, c, t0:t0 + TB] for c in range(8)]
                    ub = uT if blk % 2 == 0 else ycat
                    pTb = pTts[blk % 2]
                    for dc in range(8):
                        pg_, pgb_ = PS(dc % 2, 0, 256)
                        pp_, ppb_ = PS(2 + dc % 2, 0, 256)
                        for c in range(8):
                            mm(pg_, wpg.ap[:, c, dc * 128:(dc + 1) * 128], ub.ap[:, c, :], c == 0, c == 7, wpg.R(c) + [ub], pgb_)
                        for c in range(2):
                            mm(pp_, wpp.ap[:, c, dc * 128:(dc + 1) * 128], pTb.ap[:, c, :], c == 0, c == 1, wpp.R(c) + [pTb], ppb_)
                        ga = gat[dc % 2]
                        act(ga.ap, pg_, AF.Sigmoid, pgb_ + [pv], [ga], bias=pv.ap[:, o + BPG + dc:o + BPG + dc + 1])
                        tt("vector", ga.ap, ga.ap, pp_, ALU.mult, [ga] + ppb_, [ga])
                        tt("vector", hsl[dc], hsl[dc], ga.ap, ALU.add, [hb, ga], [hb])

                if "ple" in DBG:
                    ple_pre(0)
                    for blk in range(NBLK):
                        if blk + 1 < NBLK:
                            ple_pre(blk + 1)
                        ple_main(blk)

            S.enabled = True
            if KSTOP <= 4:
                S.dead = True
            SC.reset()
            onT = SC.get([8, TB], F32, "onT")
            ost = [uT_f, ycat_f]
            for blk in range(NBLK):
                t0 = blk * TB
                rmsnorm([hT[:, c, t0:t0 + TB] for c in range(8)], [hTb[blk]], GF, 8, D_MODEL,
                        [onT.ap[:, c, :] for c in range(8)], onT, TB, (6, 256))
                for j in range(TB // 128):
                    if KFIN < 2:
                        break
                    os_ = ost[j % 2]
                    for half in range(2):
                        bank = half
                        pso, pbs = PS(bank, 0, 512)
                        for q in range(4):
                            c = half * 4 + q
                            tr(ps_t[bank][:, q * 128:(q + 1) * 128], onT.ap[:, c, j * 128:(j + 1) * 128], ident_f, [onT, cf], pbs)
                        cp("vector" if half == 0 else "scalar", os_.ap[:, half * 512:(half + 1) * 512], pso, pbs, [os_])
                    if KFIN >= 3:
                        dma("sync", y_d[s, t0 + j * 128:t0 + (j + 1) * 128, :], os_.ap, [os_], [], is_out=True)
        if KDUMP:
            S.enabled = True
            S.dead = False
            dma("sync", dbg_d, dbgt.ap, [dbgt], [], is_out=True)
        S.emit()
    return nc


PREFETCHED = [None]
import os
DBG = set(os.environ.get("KDBG", "ssd,lru,fox,ffn,ple").split(","))
KSTOP = int(os.environ.get("KSTOP", "99"))
KFIN = int(os.environ.get("KFIN", "3"))
KPLE = int(os.environ.get("KPLE", "3"))
KDUMP = int(os.environ.get("KDUMP", "0"))
DUMPS = {}

def _consts():
    ident = np.eye(128, dtype=np.float32)
    tri = np.triu(np.ones((128, 128), np.float32))
    ones = np.ones((128, 128), np.float32)
    mneg = np.where(np.arange(128)[:, None] > np.arange(128)[None, :], np.float32(-30000.0), np.float32(0.0)).astype(np.float32)
    cf = np.concatenate([ident, tri, ones], axis=1)
    cb = np.concatenate([ident, ones, mneg], axis=1)
    sel = np.zeros((6, 6, 128), np.float32)
    for h in range(6):
        sel[h, h, :] = 1.0
    return cf, cb, sel.reshape(6, 768)


def _pcol(v, nch):
    return np.ascontiguousarray(np.asarray(v, np.float32).reshape(nch, 128).T)


def _pack(inp, DEPTH):
    f = lambda k: np.asarray(inp[k], np.float32)
    pvs = []
    w_in_r, bds = [], []
    for l in range(DEPTH):
        cols = np.zeros((128, NPV), np.float32)
        cols[:, G1:G1 + 8] = _pcol(f("norm1_g")[l], 8)
        cols[:, G2:G2 + 8] = _pcol(f("norm2_g")[l], 8)
        cols[:, G3:G3 + 8] = _pcol(f("norm3_g")[l], 8)
        scw = f("ssd_conv_w")[l]
        for c in range(7):
            for k in range(4):
                cols[:, SCW + c * 4 + k] = scw[k, c * 128:(c + 1) * 128]
        cols[:, SCB:SCB + 7] = _pcol(f("ssd_conv_b")[l], 7)
        cols[:, DSK:DSK + 3] = _pcol(np.repeat(f("ssd_d")[l], 64), 3)
        cols[:, SNG:SNG + 3] = _pcol(f("ssd_norm_g")[l], 3)
        lcw = f("lru_conv_w")[l]
        for c in range(2):
            for k in range(4):
                cols[:, LCW + c * 4 + k] = lcw[k, c * 128:(c + 1) * 128]
        cols[:, LCB:LCB + 2] = _pcol(f("lru_conv_b")[l], 2)
        cols[:, LBA:LBA + 2] = _pcol(f("lru_b_a")[l], 2)
        cols[:, LBX:LBX + 2] = _pcol(f("lru_b_x")[l], 2)
        cols[:, LAM:LAM + 2] = _pcol(f("lru_lambda")[l], 2)
        cols[:, LNG:LNG + 2] = _pcol(f("lru_norm_g")[l], 2)
        cols[:, FNG:FNG + 3] = _pcol(f("fox_norm_g")[l], 3)
        cols[:, BPG:BPG + 8] = _pcol(f("b_ple_gate")[l], 8)
        cols[:, DTB:DTB + 6] = np.broadcast_to(f("ssd_dt_bias")[l][None, :], (128, 6))
        cols[:, ALOG:ALOG + 6] = np.broadcast_to(f("ssd_a_log")[l][None, :], (128, 6))
        cols[:, FBF:FBF + 6] = np.broadcast_to(f("fox_b_f")[l][None, :], (128, 6))
        pvs.append(cols)
        w = f("w_in")[l]
        z, xbc, dt, lx, lg, q, k, v, fr = np.split(w, np.cumsum([384, 896, 6, 256, 256, 384, 384, 384])[:], axis=1)
        w_in_r.append(np.concatenate([z, xbc, lx, lg, q, k, v, dt, fr], axis=1))
        bd = np.zeros((128, 4, 128), np.float32)
        wa, wx = f("lru_w_a")[l], f("lru_w_x")[l]
        for c in range(2):
            for i in range(2):
                bd[i * 64:(i + 1) * 64, c, i * 64:(i + 1) * 64] = wa[2 * c + i]
                bd[i * 64:(i + 1) * 64, 2 + c, i * 64:(i + 1) * 64] = wx[2 * c + i]
        bds.append(bd.reshape(128, 512))
    pv = np.concatenate(pvs + [_pcol(f("final_norm_g"), 8)], axis=1)
    cf, cb, sel = _consts()
    shared = {
        "w_in": np.ascontiguousarray(np.stack(w_in_r)), "w_out": f("w_out")[:DEPTH], "bd": np.stack(bds),
        "w_gate": f("w_gate")[:DEPTH], "w_up": f("w_up")[:DEPTH], "w_down": f("w_down")[:DEPTH],
        "w_pg": f("w_ple_gate")[:DEPTH], "w_pp": f("w_ple_proj")[:DEPTH],
        "pv": np.ascontiguousarray(pv), "cf": cf, "cb": cb, "sel": sel,
    }
    return shared


_NC_CACHE = {}


def run(inp, NCORES, DEPTH):
    x = np.asarray(inp["x"], np.float32)
    p = np.asarray(inp["p"], np.float32)
    B, NT, _ = x.shape
    NSEQ = B // NCORES
    key = (NSEQ, NT, DEPTH)
    if key not in _NC_CACHE:
        PREFETCHED[0] = None
        _NC_CACHE[key] = build(NSEQ, NT, DEPTH)
    nc = _NC_CACHE[key]
    shared = _pack(inp, DEPTH)
    in_maps = []
    for c in range(NCORES):
        m = dict(shared)
        m["x"] = np.ascontiguousarray(x[c * NSEQ:(c + 1) * NSEQ])
        m["p"] = np.ascontiguousarray(p[:DEPTH, c * NSEQ:(c + 1) * NSEQ])
        in_maps.append(m)
    res = run_bass_kernel_spmd(nc, in_maps, core_ids=list(range(NCORES)))
    if KDUMP:
        global LAST_DBG
        LAST_DBG = res.results[0]["dbg"]
    return np.concatenate([r["y"] for r in res.results], axis=0)


def kernel(**inputs):
    return run(inputs, 8, 2)
```

```python
import contextlib
import numpy as np
import concourse.bass as bass
import concourse.mybir as mybir
from concourse.bass_utils import run_bass_kernel_spmd

F32 = mybir.dt.float32
BF16 = mybir.dt.bfloat16
AF = mybir.ActivationFunctionType
ALU = mybir.AluOpType

ENGS = ["tensor", "vector", "scalar", "gpsimd", "sync"]
SEM_CHUNK = 3000

D_MODEL = 1024
D_FF = 2816
IN_COLS = 2956
OZ, OXS, OB, OC, OLX, OLG, OQ, OK_, OV, ODT, OFR = 0, 384, 768, 1024, 1280, 1536, 1792, 2176, 2560, 2944, 2950
G1, G2, G3, SCW, SCB, DSK, SNG, LCW, LCB, LBA, LBX, LAM, LNG, FNG, BPG, DTB, ALOG, FBF, NPV = (
    0, 8, 16, 24, 52, 59, 62, 65, 73, 75, 77, 79, 81, 83, 86, 94, 100, 106, 112)
EPS = 1e-6
TB = 256


class Buf:
    __slots__ = ("name", "writer", "readers")

    def __init__(self, name=""):
        self.name = name
        self.writer = None
        self.readers = []


class Sched:
    def __init__(self, nc):
        self.nc = nc
        self.prog = {e: [] for e in ENGS}
        self.clock = {e: {} for e in ENGS}
        self.dma_count = []
        self.dma_last = []
        self.out_stamps = []
        self.pool = {}
        self.pool_i = {}

    def new_dma_sem(self):
        self.dma_count.append(0)
        self.dma_last.append(None)
        return len(self.dma_count) - 1

    def pool_sem(self, q, n=28):
        if q not in self.pool:
            self.pool[q] = [self.new_dma_sem() for _ in range(n)]
            self.pool_i[q] = 0
        k = self.pool[q][self.pool_i[q] % n]
        self.pool_i[q] += 1
        return k

    def _need(self, eng, stamp, waits):
        if stamp is None:
            return
        kind, key, val, snap = stamp
        if kind == "c" and key == eng and eng == "tensor":
            return
        ck = self.clock[eng]
        k = (kind, key)
        if ck.get(k, -1) >= val:
            return
        waits.append((kind, key, val))
        if kind == "c":
            self.prog[key][val]["inc"] = True
        for kk, vv in snap.items():
            if ck.get(kk, -1) < vv:
                ck[kk] = vv
        ck[k] = val

    enabled = True
    dead = False

    def op(self, eng, fn, reads=(), writes=(), dma_sem=None, is_out=False):
        if not self.enabled or self.dead:
            return None
        waits = []
        for b in reads:
            self._need(eng, b.writer, waits)
        for b in writes:
            self._need(eng, b.writer, waits)
            for r in b.readers:
                self._need(eng, r, waits)
        if dma_sem is not None:
            self._need(eng, self.dma_last[dma_sem], waits)
        idx = len(self.prog[eng])
        ent = {"waits": waits, "fn": fn, "inc": False, "dma": None}
        self.prog[eng].append(ent)
        snap = dict(self.clock[eng])
        if dma_sem is None:
            stamp = ("c", eng, idx, snap)
        else:
            self.dma_count[dma_sem] += 1
            val = 16 * self.dma_count[dma_sem]
            ent["dma"] = (dma_sem, val)
            stamp = ("d", dma_sem, val, snap)
            self.dma_last[dma_sem] = stamp
            if is_out:
                self.out_stamps.append(stamp)
        for b in reads:
            b.readers.append(stamp)
        for b in writes:
            b.writer = stamp
            b.readers = []
        return stamp

    def emit(self, final_eng="sync"):
        nc = self.nc
        waits = []
        for st in self.out_stamps:
            self._need(final_eng, st, waits)
        self.prog[final_eng].append({"waits": waits, "fn": None, "inc": False, "dma": None})
        with contextlib.ExitStack() as es:
            csem, rank = {}, {}
            for e in ENGS:
                r = 0
                rank[e] = {}
                for i, ent in enumerate(self.prog[e]):
                    if ent["inc"]:
                        rank[e][i] = r
                        r += 1
                nsem = (r + SEM_CHUNK - 1) // SEM_CHUNK
                csem[e] = [es.enter_context(nc.semaphore(f"c_{e}_{k}")) for k in range(nsem)]
            dsem = [es.enter_context(nc.semaphore(f"d_{k}")) for k in range(len(self.dma_count))]
            block = es.enter_context(nc.Block())

            def mk(e):
                def body(engh):
                    for i, ent in enumerate(self.prog[e]):
                        for (kind, key, val) in ent["waits"]:
                            if kind == "c":
                                r = rank[key][val]
                                engh.wait_ge(csem[key][r // SEM_CHUNK], r % SEM_CHUNK + 1)
                            else:
                                engh.wait_ge(dsem[key], val)
                        if ent["fn"] is None:
                            continue
                        ins = ent["fn"](engh)
                        if ent["dma"] is not None:
                            ins.then_inc(dsem[ent["dma"][0]], 16)
                        elif ent["inc"]:
                            r = rank[e][i]
                            ins.then_inc(csem[e][r // SEM_CHUNK], 1)
                return body

            for e in ENGS:
                if self.prog[e]:
                    getattr(block, e)(mk(e))


class Tl:
    __slots__ = ("ap", "b", "subs")

    def __init__(self, ap, b):
        self.ap = ap
        self.b = b
        self.subs = None

    def split(self, n):
        self.subs = [Buf() for _ in range(n)]
        for sb_ in self.subs:
            sb_.readers = list(self.b.readers)
        return self

    def R(self, i):
        return [self.b, self.subs[i]]


class Arena:
    def __init__(self, tensor_bf16, nbytes):
        self.t = tensor_bf16
        self.n = nbytes
        self.live = []

    def view(self, off, shape, dt, name=""):
        n = 1
        for s in shape:
            n *= s
        size = n * (4 if dt == F32 else 2)
        assert off % 4 == 0 and off + size <= self.n, (name, off, size, self.n)
        ap = self.t[:, off // 2:(off + size) // 2]
        if dt == F32:
            ap = ap.bitcast(F32)
        if len(shape) == 2:
            ap = ap.rearrange("p (a b) -> p a b", a=shape[0])
        elif len(shape) == 3:
            ap = ap.rearrange("p (a b c) -> p a b c", a=shape[0], b=shape[1])
        buf = Buf(name)
        newlive = []
        for (s, e, b) in self.live:
            if s < off + size and off < e:
                if b.writer is not None:
                    buf.readers.append(b.writer)
                buf.readers.extend(b.readers)
                if not (s >= off and e <= off + size):
                    newlive.append((s, e, b))
            else:
                newlive.append((s, e, b))
        newlive.append((off, off + size, buf))
        self.live = newlive
        return Tl(ap, buf)


class Bump:
    def __init__(self, arena, base):
        self.a = arena
        self.base = base
        self.off = base

    def reset(self):
        self.off = self.base

    def get(self, shape, dt, name=""):
        n = 1
        for s in shape:
            n *= s
        size = (n * (4 if dt == F32 else 2) + 31) // 32 * 32
        t = self.a.view(self.off, shape, dt, name)
        self.off += size
        return t


def build(NSEQ, NT, DEPTH):
    NBLK = NT // TB
    NT128 = NT // 128
    nc = bass.Bass("TRN2", target_bir_lowering=False)
    din = lambda name, shape: nc.dram_tensor(name, shape, F32, kind="ExternalInput").ap()
    x_d = din("x", [NSEQ, NT, D_MODEL])
    p_d = din("p", [DEPTH, NSEQ, NT, 256])
    win_d = din("w_in", [DEPTH, D_MODEL, IN_COLS])
    wout_d = din("w_out", [DEPTH, D_MODEL, D_MODEL])
    bd_d = din("bd", [DEPTH, 128, 4 * 128])
    wg_d = din("w_gate", [DEPTH, D_MODEL, D_FF])
    wu_d = din("w_up", [DEPTH, D_MODEL, D_FF])
    wd_d = din("w_down", [DEPTH, D_FF, D_MODEL])
    wpg_d = din("w_pg", [DEPTH, D_MODEL, D_MODEL])
    wpp_d = din("w_pp", [DEPTH, 256, D_MODEL])
    pv_d = din("pv", [128, DEPTH * NPV + 8])
    cf_d = din("cf", [128, 3 * 128])
    cb_d = din("cb", [128, 3 * 128])
    sel_d = din("sel", [6, 6 * 128])
    y_d = nc.dram_tensor("y", [NSEQ, NT, D_MODEL], F32, kind="ExternalOutput").ap()
    if KDUMP:
        dbg_d = nc.dram_tensor("dbg", [128, 8192], F32, kind="ExternalOutput").ap()

    es = contextlib.ExitStack()
    with es:
        sb = lambda name, shape, dt: es.enter_context(nc.sbuf_tensor("s_" + name, shape, dt))
        S = Sched(nc)
        GF = DEPTH * NPV

        def static(name, shape, dt):
            return Tl(sb(name, shape, dt)[:], Buf(name))

        hT = sb("hT", [128, 8, NT], F32)
        hTb = [Buf(f"hT{i}") for i in range(NBLK)]
        cf = static("cf", [128, 3, 128], F32)
        cb = static("cb", [128, 3, 128], BF16)
        pv = static("pv", [128, DEPTH * NPV + 8], F32)
        dv = static("dv", [128, DEPTH, 12], F32)
        assert TB == 256
        uT_f = static("uT", [128, 1024], F32)
        ycat_f = static("ycat", [128, 1024], F32)
        uT = Tl(uT_f.ap.bitcast(BF16).rearrange("p (c t) -> p c t", c=8), uT_f.b)
        ycat = Tl(ycat_f.ap.bitcast(BF16).rearrange("p (c t) -> p c t", c=8), ycat_f.b)
        sq = static("sq", [128, 8, TB], BF16)
        rs0 = static("rs0", [128, TB], F32)
        rs1 = static("rs1", [128, TB], F32)
        ARENA_BYTES = 125 * 1024
        arena_t = sb("arena", [128, ARENA_BYTES // 2], BF16)
        A = Arena(arena_t, ARENA_BYTES)
        ps_t = [es.enter_context(nc.psum_tensor(f"ps{i}", [128, 512], F32)) for i in range(7)]
        psb_t = es.enter_context(nc.psum_tensor("psb", [128, 1024], BF16))
        pbuf = [[Buf(f"ps{i}_{h}") for h in range(2)] for i in range(7)]
        psbb = Buf("psb")

        def PS(bank, c0, c1, p0=0, p1=128):
            bs = [pbuf[bank][0], pbuf[bank][1]]
            return ps_t[bank][p0:p1, c0:c1], bs

        ident_f = cf.ap[:, 0, :]
        tri_f = cf.ap[:, 1, :]
        ones_f = cf.ap[:, 2, :]
        ident_b = cb.ap[:, 0, :]
        ones_b = cb.ap[:, 1, :]
        mneg_b = cb.ap[:, 2, :]

        def bl(xs):
            out = []
            for x_ in xs:
                if isinstance(x_, Tl):
                    out.append(x_.b)
                elif isinstance(x_, (list, tuple)):
                    out.extend(bl(x_))
                elif x_ is not None:
                    out.append(x_)
            return out

        def mm(out, lhsT, rhs, start, stop, r, w):
            S.op("tensor", lambda e: e.matmul(out, lhsT=lhsT, rhs=rhs, start=start, stop=stop), bl(r), bl(w))

        def tr(out, in_, ident, r, w):
            S.op("tensor", lambda e: e.transpose(out, in_, ident), bl(r), bl(w))

        def act(out, in_, func, r, w, bias=None, scale=None):
            kw = {}
            if bias is not None:
                kw["bias"] = bias
            if scale is not None:
                kw["scale"] = scale
            S.op("scalar", lambda e: e.activation(out=out, in_=in_, func=func, **kw), bl(r), bl(w))

        def tt(eng, out, in0, in1, op, r, w):
            S.op(eng, lambda e: e.tensor_tensor(out=out, in0=in0, in1=in1, op=op), bl(r), bl(w))

        def ts(eng, out, in0, s1, s2, op0, op1, r, w):
            if op1 is None:
                S.op(eng, lambda e: e.tensor_scalar(out=out, in0=in0, scalar1=s1, scalar2=None, op0=op0), bl(r), bl(w))
            else:
                S.op(eng, lambda e: e.tensor_scalar(out=out, in0=in0, scalar1=s1, scalar2=s2, op0=op0, op1=op1), bl(r), bl(w))

        def stt(out, in0, scalar, in1, op0, op1, r, w):
            S.op("vector", lambda e: e.scalar_tensor_tensor(out=out, in0=in0, scalar=scalar, in1=in1, op0=op0, op1=op1), bl(r), bl(w))

        def cp(eng, out, in_, r, w):
            if eng == "scalar":
                act(out, in_, AF.Copy, r, w)
            else:
                S.op(eng, lambda e: e.tensor_copy(out=out, in_=in_), bl(r), bl(w))

        def mset(eng, ap, val, w):
            S.op(eng, lambda e: e.memset(ap, val), [], bl(w))

        def scan(out, d0, d1, init, r, w):
            S.op("vector", lambda e: e.tensor_tensor_scan(out=out, data0=d0, data1=d1, initial=init, op0=ALU.mult, op1=ALU.add),
                 bl(r), bl(w))

        def dma(q, out, in_, r, w, is_out=False, cast=False):
            kw = {"max_dma_last_dim": 8192} if cast else {}
            return S.op(q, lambda e: e.dma_start(out=out, in_=in_, **kw), bl(r), bl(w), dma_sem=S.pool_sem(q), is_out=is_out)

        dbg_state = [0]
        if KDUMP:
            dbgt = static("dbgt", [128, 8192], F32)

        def dump(name, ap, r, p0=0, p1=128):
            if not KDUMP or name in DUMPS or S.dead or not S.enabled:
                return
            n = ap.shape[-1]
            c0 = dbg_state[0]
            DUMPS[name] = (c0, n, p0, p1)
            dbg_state[0] += n
            S.op("vector", lambda e: e.tensor_copy(out=dbgt.ap[p0:p1, c0:c0 + n], in_=ap), bl(r), [dbgt.b])

        dma("sync", cf.ap, cf_d.rearrange("p (a b) -> p a b", a=3), [], [cf])
        dma("sync", pv.ap, pv_d, [], [pv])
        dma("gpsimd", cb.ap, cb_d.rearrange("p (a b) -> p a b", a=3), [], [cb], cast=True)
        for l in range(DEPTH):
            o = l * NPV
            act(dv.ap[:, l, 0:6], pv.ap[:, o + ALOG:o + ALOG + 6], AF.Exp, [pv], [dv])
            ts("vector", dv.ap[:, l, 0:6], dv.ap[:, l, 0:6], -1.0, None, ALU.mult, None, [dv], [dv])
            act(dv.ap[:, l, 6:8], pv.ap[:, o + LAM:o + LAM + 2], AF.Exp, [pv], [dv], scale=-1.0)
            act(dv.ap[:, l, 6:8], dv.ap[:, l, 6:8], AF.Ln, [dv], [dv], bias=1.0)
            ts("vector", dv.ap[:, l, 6:8], dv.ap[:, l, 6:8], -8.0, None, ALU.mult, None, [dv], [dv])
            ts("vector", dv.ap[:, l, 8:10], pv.ap[:, o + LBA:o + LBA + 2], -1.0, None, ALU.mult, None, [pv], [dv])
            ts("vector", dv.ap[:, l, 10:12], pv.ap[:, o + LBX:o + LBX + 2], -1.0, None, ALU.mult, None, [pv], [dv])

        if KSTOP <= 1:
            S.dead = True
        def rmsnorm(src_aps, src_bufs, gcol0, nch, width, out_aps, out_tl, ncols, psum_loc, sq_eng="gpsimd"):
            bank, c0 = psum_loc
            pso, psb_ = PS(bank, c0, c0 + ncols)
            for c in range(nch):
                eng = sq_eng if c % 2 == 0 else "vector"
                if eng == "scalar":
                    act(sq.ap[:, c, 0:ncols], src_aps[c], AF.Square, src_bufs, [sq])
                else:
                    tt(eng, sq.ap[:, c, 0:ncols], src_aps[c], src_aps[c], ALU.mult, src_bufs, [sq])
            for c in range(nch):
                mm(pso, ones_b, sq.ap[:, c, 0:ncols], c == 0, c == nch - 1, [cb, sq], psb_)
            act(rs0.ap[:, 0:ncols], pso, AF.Ln, psb_, [rs0], bias=EPS_AP, scale=1.0 / width)
            act(rs1.ap[:, 0:ncols], rs0.ap[:, 0:ncols], AF.Exp, [rs0], [rs1], scale=-0.5)
            for c in range(nch):
                stt(out_aps[c], src_aps[c], pv.ap[:, gcol0 + c:gcol0 + c + 1], rs1.ap[:, 0:ncols], ALU.mult, ALU.mult,
                    src_bufs + [pv, rs1], [out_tl])

        epst = static("epst", [128, 1], F32)
        mset("vector", epst.ap, EPS, [epst])
        EPS_AP = epst.ap[:, 0:1]

        MP = Bump(A, 0)

        def mixer_persist():
            MP.reset()
            d = {}
            d["win"] = MP.get([8, IN_COLS], BF16, "win")
            d["wout"] = MP.get([8, D_MODEL], BF16, "wout")
            d["bd"] = MP.get([4, 128], BF16, "bd")
            d["KT"] = MP.get([3, NT], BF16, "KT")
            d["V"] = MP.get([NT128, 384], BF16, "V")
            d["ncum"] = MP.get([NT128, 6], F32, "ncum")
            d["xraw"] = MP.get([7, TB + 3], F32, "xraw")
            d["lraw"] = MP.get([2, TB + 3], F32, "lraw")
            d["st"] = MP.get([384], F32, "st")
            d["stb"] = MP.get([384], BF16, "stb")
            d["lcar"] = MP.get([2], F32, "lcar")
            d["fcar"] = MP.get([6], F32, "fcar")
            d["fref"] = MP.get([6], F32, "fref")
            return d

        _tmp = mixer_persist()
        SCR_BASE = MP.off
        A.live = []
        SC = Bump(A, SCR_BASE)
        FFN_B = Bump(A, 0)
        PLE_BASE = SCR_BASE
        PLE_B = Bump(A, PLE_BASE)

        def load_mixer_weights(l):
            d = mixer_persist()
            d["win"].split(8)
            d["wout"].split(8)
            for c in range(8):
                dma("gpsimd", d["win"].ap[:, c, :], win_d[l, c * 128:(c + 1) * 128, :], [], [d["win"].subs[c]], cast=True)
            for c in range(8):
                dma("gpsimd", d["wout"].ap[:, c, :], wout_d[l, c * 128:(c + 1) * 128, :], [], [d["wout"].subs[c]], cast=True)
            dma("gpsimd", d["bd"].ap, bd_d[l].rearrange("p (a b) -> p a b", a=4), [], [d["bd"]], cast=True)
            return d

        for s in range(NSEQ):
            SC.reset()
            xst = [uT_f, ycat_f]
            for t in range(NT128):
                st_ = xst[t % 2]
                dma("sync", st_.ap, x_d[s, t * 128:(t + 1) * 128, :], [], [st_])
                for half in range(2):
                    bank = (2 * t + half) % 4
                    pso, pbs = PS(bank, 0, 512)
                    for j in range(4):
                        c = half * 4 + j
                        tr(ps_t[bank][:, j * 128:(j + 1) * 128], st_.ap[:, c * 128:(c + 1) * 128], ident_f, [st_, cf], pbs)
                    eng = "vector" if half == 0 else "scalar"
                    cp(eng, hT[:, half * 4:half * 4 + 4, t * 128:(t + 1) * 128],
                       pso.rearrange("p (a b) -> p a b", a=4), pbs, [hTb[(t * 128) // TB]])

            if KSTOP <= 2:
                S.dead = True
            for l in range(DEPTH):
                o = l * NPV
                W = PREFETCHED[0] if PREFETCHED[0] is not None else load_mixer_weights(l)
                PREFETCHED[0] = None
                win, wout, bdm = W["win"], W["wout"], W["bd"]
                KT, V, ncum, xraw, lraw = W["KT"], W["V"], W["ncum"], W["xraw"], W["lraw"]
                st, stb, lcar, fcar = W["st"], W["stb"], W["lcar"], W["fcar"]
                fref = W["fref"]
                mset("vector", xraw.ap[:, :, 0:3], 0.0, [xraw])
                mset("vector", lraw.ap[:, :, 0:3], 0.0, [lraw])
                mset("gpsimd", st.ap, 0.0, [st])
                mset("gpsimd", stb.ap, 0.0, [stb])
                mset("gpsimd", lcar.ap, 0.0, [lcar])
                mset("gpsimd", fcar.ap, 0.0, [fcar])

                slot = [0]

                def proj(col0, ncolsM=128):
                    i = slot[0]
                    slot[0] += 1
                    bank, half = i % 2, (i // 2) % 2
                    pso, pbs = PS(bank, half * 256, half * 256 + 256)
                    for c in range(8):
                        mm(pso, win.ap[:, c, col0:col0 + ncolsM], uT.ap[:, c, :], c == 0, c == 7, win.R(c) + [uT], pbs)
                    return pso, pbs

                def ssd_pre():
                    SC.reset()
                    xbc = SC.get([7, TB], BF16, "xbc")
                    zs = SC.get([3, TB], F32, "zs")
                    yssd = SC.get([3, TB], F32, "yssd")
                    cacc = [SC.get([TB], F32, f"cacc{i}") for i in range(2)]
                    sm = SC.get([8, 6], F32, "sm")
                    xdt = SC.get([384], BF16, "xdt")
                    xdts = SC.get([384], BF16, "xdts")
                    btok = SC.get([256], BF16, "btok")
                    Dm = [SC.get([128], F32, f"D{i}") for i in range(2)]
                    MT = [SC.get([128], BF16, f"MT{i}") for i in range(2)]
                    ebc = [SC.get([128], F32, f"ebc{i}") for i in range(2)]
                    Cs = [SC.get([128], BF16, f"Cs{i}") for i in range(2)]
                    stt_tmp = SC.get([384], F32, "sttmp")
                    smp = SC.get([24], F32, "smp")

                    for c in range(3):
                        pso, pbs = proj(OZ + c * 128)
                        act(zs.ap[:, c, :], pso, AF.Silu, pbs, [zs])
                    xbc.split(7)
                    xbc.b.readers = []
                    xraw_s = [Buf() for _ in range(7)]
                    for sb_ in xraw_s:
                        sb_.writer = xraw.b.writer
                        sb_.readers = list(xraw.b.readers)

                    def silu_c(c):
                        act(xbc.ap[:, c, :], cacc[c % 2].ap, AF.Silu, [cacc[c % 2]], [xbc.subs[c]])

                    for c in range(7):
                        pso, pbs = proj(OXS + c * 128)
                        cp("scalar", xraw.ap[:, c, 3:3 + TB], pso, pbs, [xraw_s[c]])
                        if c > 0:
                            silu_c(c - 1)
                        ca = cacc[c % 2]
                        wc = lambda k: pv.ap[:, o + SCW + c * 4 + k:o + SCW + c * 4 + k + 1]
                        ts("vector", ca.ap, xraw.ap[:, c, 0:TB], wc(0), pv.ap[:, o + SCB + c:o + SCB + c + 1], ALU.mult, ALU.add,
                           [xraw_s[c], pv], [ca])
                        for k in range(1, 4):
                            stt(ca.ap, xraw.ap[:, c, k:k + TB], wc(k), ca.ap, ALU.mult, ALU.add, [xraw_s[c], pv, ca], [ca])
                    silu_c(6)
                    cp("gpsimd", xraw.ap[:, :, 0:3], xraw.ap[:, :, TB:TB + 3], xraw_s, xraw_s + [xraw])
                    return (xbc, zs, yssd, sm, xdt, xdts, btok, Dm, MT, ebc, Cs, stt_tmp, smp)

                rmsnorm([hT[:, c, 0:TB] for c in range(8)], [hTb[0]], o + G1, 8, D_MODEL, [uT.ap[:, c, :] for c in range(8)], uT, TB, (6, 256))
                PRE = [ssd_pre()]
                for blk in range(NBLK):
                    t0 = blk * TB
                    hb = hTb[blk]
                    hsl = [hT[:, c, t0:t0 + TB] for c in range(8)]

                    (xbc, zs, yssd, sm, xdt, xdts, btok, Dm, MT, ebc, Cs, stt_tmp, smp) = PRE[0]
                    cp("vector", fref.ap, fcar.ap, [fcar], [fref])
                    for j in range(TB // 128):
                        cs = slice(j * 128, (j + 1) * 128)
                        tg = t0 + j * 128
                        pso, pbs = PS(2, 0, 396)
                        for c in range(8):
                            mm(pso, uT.ap[:, c, cs], win.ap[:, c, OV:OV + 396], c == 0, c == 7, win.R(c) + [uT], pbs)
                        kb = tg // 128
                        cp("vector", V.ap[:, kb, :], ps_t[2][:, 0:384], pbs, [V])
                        dt, adt, nacs, tmp6, decs, cdec, sdt, nlf = [sm.ap[:, i, :] for i in range(8)]
                        tt("vector", tmp6, ps_t[2][:, 384:390], pv.ap[:, o + DTB:o + DTB + 6], ALU.add, pbs + [pv], [sm])
                        tt("vector", nlf, ps_t[2][:, 390:396], pv.ap[:, o + FBF:o + FBF + 6], ALU.add, pbs + [pv], [sm])
                        act(tmp6, tmp6, AF.Exp, [sm], [sm])
                        act(nlf, nlf, AF.Exp, [sm], [sm], scale=-1.0)
                        act(dt, tmp6, AF.Ln, [sm], [sm], bias=1.0)
                        act(nlf, nlf, AF.Ln, [sm], [sm], bias=1.0)
                        tt("vector", adt, dt, dv.ap[:, l, 0:6], ALU.mult, [sm, dv], [sm])
                        p4, p4b = PS(4, 384, 408)
                        mm(ps_t[4][:, 384:390], tri_f, adt, True, True, [cf, sm], p4b)
                        mm(ps_t[4][:, 390:396], ones_f, adt, True, True, [cf, sm], p4b)
                        mm(ps_t[4][:, 396:402], tri_f, nlf, True, True, [cf, sm], p4b)
                        mm(ps_t[4][:, 402:408], ones_f, nlf, True, True, [cf, sm], p4b)
                        cp("vector", smp.ap, ps_t[4][:, 384:408], p4b, [smp])
                        ts("vector", nacs, smp.ap[:, 0:6], -1.0, None, ALU.mult, None, [smp], [sm])
                        tt("vector", tmp6, smp.ap[:, 6:12], nacs, ALU.add, [smp, sm], [sm])
                        act(decs, tmp6, AF.Exp, [sm], [sm])
                        act(cdec, smp.ap[:, 6:12], AF.Exp, [smp], [sm])
                        tt("vector", sdt, dt, decs, ALU.mult, [sm], [sm])
                        tt("vector", ncum.ap[:, kb, :], smp.ap[:, 12:18], fcar.ap, ALU.add, [smp, fcar], [ncum])
                        tt("vector", fcar.ap, smp.ap[:, 18:24], fcar.ap, ALU.add, [smp, fcar], [fcar])
                        for c in range(5):
                            tr(psb_t[:, c * 128:(c + 1) * 128], xbc.ap[:, c, cs], ident_b, xbc.R(c) + [cb], [psbb])
                        xs_tok = psb_t[:, 0:384].rearrange("p (h e) -> p h e", h=6)
                        tt("vector", xdt.ap.rearrange("p (h e) -> p h e", h=6), xs_tok, dt.unsqueeze(2).to_broadcast([128, 6, 64]),
                           ALU.mult, [psbb, sm], [xdt])
                        tt("vector", xdts.ap.rearrange("p (h e) -> p h e", h=6), xs_tok, sdt.unsqueeze(2).to_broadcast([128, 6, 64]),
                           ALU.mult, [psbb, sm], [xdts])
                        cp("vector", btok.ap, psb_t[:, 384:640], [psbb], [btok])
                        pg, pgb = PS(3, 0, 256)
                        for g in range(2):
                            mm(ps_t[3][:, g * 128:(g + 1) * 128], xbc.ap[:, 3 + g, cs], xbc.ap[:, 5 + g, cs], True, True, xbc.R(3 + g) + xbc.R(5 + g), pgb)
                        py, pyb = PS(4, 0, 384)

                        def stageA(h):
                            g = h // 3
                            i2 = h % 2
                            eb = 6 if i2 == 0 else 2
                            pE, pEb = PS(eb, 0, 256)
                            adt_b = adt[:, h:h + 1].to_broadcast([128, 128])
                            mm(ps_t[eb][:, 0:128], adt_b, tri_f, True, False, [sm, cf], pEb)
                            mm(ps_t[eb][:, 0:128], ident_b, mneg_b, False, True, [cb], pEb)
                            mm(ps_t[eb][:, 128:256], adt_b, tri_f, True, True, [sm, cf], pEb)
                            act(Dm[i2].ap, ps_t[eb][:, 0:128], AF.Exp, pEb + [sm], [Dm[i2]], bias=nacs[:, h:h + 1])
                            act(ebc[i2].ap, ps_t[eb][:, 128:256], AF.Exp, pEb, [ebc[i2]])
                            tt("vector", MT[i2].ap, ps_t[3][:, g * 128:(g + 1) * 128], Dm[i2].ap, ALU.mult, pgb + [Dm[i2]], [MT[i2]])
                            tt("gpsimd", Cs[i2].ap, xbc.ap[:, 5 + g, cs], ebc[i2].ap, ALU.mult, xbc.R(5 + g) + [ebc[i2]], [Cs[i2]])

                        def stageB(h):
                            i2 = h % 2
                            hp = (h % 2) * 64
                            yo = ps_t[4][hp:hp + 64, (h // 2) * 128:(h // 2) * 128 + 128]
                            mm(yo, xdt.ap[:, h * 64:(h + 1) * 64], MT[i2].ap, True, False, [xdt, MT[i2]], pyb)
                            mm(yo, stb.ap[:, h * 64:(h + 1) * 64], Cs[i2].ap, False, True, [stb, Cs[i2]], pyb)

                        stageA(0)
                        for h in range(6):
                            if h + 1 < 6:
                                stageA(h + 1)
                            stageB(h)
                        for c in range(3):
                            stt(yssd.ap[:, c, cs], xbc.ap[:, c, cs], pv.ap[:, o + DSK + c:o + DSK + c + 1],
                                ps_t[4][:, c * 128:(c + 1) * 128], ALU.mult, ALU.add, xbc.R(c) + [pv] + pyb, [yssd])
                        pst, pstb = PS(5, 0, 384)
                        for g in range(2):
                            mm(ps_t[5][:, g * 192:(g + 1) * 192], btok.ap[:, g * 128:(g + 1) * 128], xdts.ap[:, g * 192:(g + 1) * 192],
                               True, True, [btok, xdts], pstb)
                        tt("vector", stt_tmp.ap.rearrange("p (h e) -> p h e", h=6), st.ap.rearrange("p (h e) -> p h e", h=6),
                           cdec.unsqueeze(2).to_broadcast([128, 6, 64]), ALU.mult, [st, sm], [stt_tmp])
                        tt("vector", st.ap, stt_tmp.ap, pst, ALU.add, [stt_tmp] + pstb, [st])
                        cp("gpsimd", stb.ap, st.ap, [st], [stb])
                    dump("yssd1_pre", yssd.ap[:, 1, :], [yssd])
                    dump("zs1", zs.ap[:, 1, :], [zs])
                    for c in range(3):
                        tt("gpsimd", yssd.ap[:, c, :], yssd.ap[:, c, :], zs.ap[:, c, :], ALU.mult, [yssd, zs], [yssd])
                    rmsnorm([yssd.ap[:, c, :] for c in range(3)], [yssd], o + SNG, 3, 384,
                            [ycat.ap[:, c, :] for c in range(3)], ycat, TB, (6, 256))

                    dump("ycat1_early", ycat.ap[:, 1, :], [ycat])
                    dump("yssd1", yssd.ap[:, 1, :], [yssd])
                    S.enabled = True
                    SC.reset()
                    LT = [{k: SC.get([TB], BF16 if k == "xlb" else F32, f"{k}{c}") for k in ("xl", "xlb", "rg", "ig", "aa", "mu", "hl", "g1", "g2")}
                          for c in range(2)]
                    lraw_s = [Buf() for _ in range(2)]
                    for sb_ in lraw_s:
                        sb_.writer = lraw.b.writer
                        sb_.readers = list(lraw.b.readers)
                    pgl = [None, None]

                    def l1(c):
                        T_ = LT[c]
                        pso, pbs = proj(OLX + c * 128)
                        cp("scalar", lraw.ap[:, c, 3:3 + TB], pso, pbs, [lraw_s[c]])
                        pgl[c] = proj(OLG + c * 128)
                        cp("scalar", T_["g1"].ap, pgl[c][0], pgl[c][1], [T_["g1"]])

                    def l2(c):
                        T_ = LT[c]
                        wc = lambda k: pv.ap[:, o + LCW + c * 4 + k:o + LCW + c * 4 + k + 1]
                        ts("vector", T_["xl"].ap, lraw.ap[:, c, 0:TB], wc(0), pv.ap[:, o + LCB + c:o + LCB + c + 1], ALU.mult, ALU.add,
                           [lraw_s[c], pv], [T_["xl"]])
                        for k in range(1, 4):
                            stt(T_["xl"].ap, lraw.ap[:, c, k:k + TB], wc(k), T_["xl"].ap, ALU.mult, ALU.add, [lraw_s[c], pv, T_["xl"]], [T_["xl"]])
                        cp("gpsimd", T_["xlb"].ap, T_["xl"].ap, [T_["xl"]], [T_["xlb"]])
                        tt("gpsimd", T_["g2"].ap, T_["g1"].ap, T_["g1"].ap, ALU.mult, [T_["g1"]], [T_["g2"]])
                        ts("vector", T_["g2"].ap, T_["g2"].ap, 0.044715, 1.0, ALU.mult, ALU.add, [T_["g2"]], [T_["g2"]])
                        tt("gpsimd", T_["g2"].ap, T_["g2"].ap, T_["g1"].ap, ALU.mult, [T_["g2"], T_["g1"]], [T_["g2"]])

                    def sig3(out_ap, in_ap, r, w, scale, bias=None):
                        act(out_ap, in_ap, AF.Exp, r, w, scale=scale, bias=bias)
                        act(out_ap, out_ap, AF.Ln, w, w, bias=1.0)
                        act(out_ap, out_ap, AF.Exp, w, w, scale=-1.0)

                    def l3(c):
                        T_ = LT[c]
                        pa, pab = PS(2, c * 256, c * 256 + 256)
                        mm(pa, bdm.ap[:, c, :], T_["xlb"].ap, True, True, [bdm, T_["xlb"]], pab)
                        sig3(T_["rg"].ap, pa, pab + [dv], [T_["rg"]], -1.0, dv.ap[:, l, 8 + c:9 + c])
                        mm(pa, bdm.ap[:, 2 + c, :], T_["xlb"].ap, True, True, [bdm, T_["xlb"]], pab)
                        sig3(T_["ig"].ap, pa, pab + [dv], [T_["ig"]], -1.0, dv.ap[:, l, 10 + c:11 + c])
                        sig3(T_["g2"].ap, T_["g2"].ap, [T_["g2"]], [T_["g2"]], -1.5957691216057308)

                    def l4(c):
                        T_ = LT[c]
                        ts("vector", T_["aa"].ap, T_["rg"].ap, dv.ap[:, l, 6 + c:7 + c], None, ALU.mult, None, [T_["rg"], dv], [T_["aa"]])
                        act(T_["aa"].ap, T_["aa"].ap, AF.Exp, [T_["aa"]], [T_["aa"]])
                        tt("gpsimd", T_["ig"].ap, T_["ig"].ap, T_["xl"].ap, ALU.mult, [T_["ig"], T_["xl"]], [T_["ig"]])
                        tt("gpsimd", T_["g2"].ap, T_["g2"].ap, T_["g1"].ap, ALU.mult, [T_["g2"], T_["g1"]], [T_["g2"]])

                    def l5(c):
                        T_ = LT[c]
                        tt("gpsimd", T_["mu"].ap, T_["aa"].ap, T_["aa"].ap, ALU.mult, [T_["aa"]], [T_["mu"]])
                        act(T_["mu"].ap, T_["mu"].ap, AF.Ln, [T_["mu"]], [T_["mu"]], bias=1.0, scale=-1.0)
                        act(T_["mu"].ap, T_["mu"].ap, AF.Exp, [T_["mu"]], [T_["mu"]], scale=0.5)

                    def l6(c):
                        T_ = LT[c]
                        tt("vector", T_["mu"].ap, T_["mu"].ap, T_["ig"].ap, ALU.mult, [T_["mu"], T_["ig"]], [T_["mu"]])
                        scan(T_["hl"].ap, T_["aa"].ap, T_["mu"].ap, lcar.ap[:, c:c + 1], [T_["aa"], T_["mu"], lcar], [T_["hl"]])
                        cp("vector", lcar.ap[:, c:c + 1], T_["hl"].ap[:, TB - 1:TB], [T_["hl"]], [lcar])
                        tt("vector", T_["hl"].ap, T_["hl"].ap, T_["g2"].ap, ALU.mult, [T_["hl"], T_["g2"]], [T_["hl"]])

                    def lru_gen():
                        for st_fn in (l1, l2, l3, l4, l5, l6):
                            for c in range(2):
                                st_fn(c)
                                yield
                            if st_fn is l2:
                                cp("gpsimd", lraw.ap[:, :, 0:3], lraw.ap[:, :, TB:TB + 3], lraw_s, lraw_s + [lraw])
                        hl_aps = [LT[c]["hl"].ap for c in range(2)]
                        hl_bufs = [LT[c]["hl"] for c in range(2)]
                        rmsnorm(hl_aps, hl_bufs, o + LNG, 2, 256,
                                [ycat.ap[:, 3 + c, :] for c in range(2)], ycat, TB, (2, 256))
                        yield

                    lgen = lru_gen()

                    qT = SC.get([3, TB], BF16, "qT")
                    pT = [SC.get([TB], BF16, f"pT{i}") for i in range(4)]
                    lnd = SC.get([TB], F32, "lnd")
                    yfox = SC.get([3, TB], F32, "yfox")
                    for c in range(3):
                        pso, pbs = proj(OQ + c * 128)
                        cp("scalar", qT.ap[:, c, :], pso, pbs, [qT])
                    for c in range(3):
                        pso, pbs = proj(OK_ + c * 128)
                        cp("scalar", KT.ap[:, c, t0:t0 + TB], pso, pbs, [KT])
                    nkb = (t0 + TB) // 128
                    nb = SC.get([NT128, 6], F32, "nb")
                    tt("vector", nb.ap[:, 0:nkb, :], ncum.ap[:, 0:nkb, :], fref.ap.unsqueeze(1).to_broadcast([128, nkb, 6]), ALU.subtract,
                       [ncum, fref], [nb])
                    pi = 0
                    for c in range(3):
                        py, pyb = PS(5, 0, 256)
                        pdn, pdb = PS(4, 0, 256)
                        its = [(hh, kb) for hh in range(2) for kb in range(nkb)]

                        def stA(idx, it):
                            hh, kb = it
                            h = 2 * c + hh
                            hp = hh * 64
                            rel = kb * 128 - t0
                            q0 = 0 if rel < 0 else rel
                            sbank = 6 if idx % 2 == 0 else 3
                            psS, psSb = PS(sbank, 0, 256)
                            so = ps_t[sbank][:, q0:TB]
                            diag = rel >= 0
                            mm(so, KT.ap[hp:hp + 64, c, kb * 128:(kb + 1) * 128], qT.ap[hp:hp + 64, c, q0:TB], True, not diag,
                               [KT, qT], psSb)
                            if diag:
                                mm(ps_t[sbank][:, q0:q0 + 128], ident_b, mneg_b, False, True, [cb], psSb)
                            pt = pT[idx % 4]
                            if q0 > 0:
                                mset("gpsimd", pt.ap[:, 0:q0], 0.0, [pt])
                            act(pt.ap[:, q0:TB], so, AF.Exp, psSb + [nb], [pt], bias=nb.ap[:, kb, h:h + 1], scale=0.125)

                        def stB(idx, it):
                            hh, kb = it
                            h = 2 * c + hh
                            hp = hh * 64
                            pt = pT[idx % 4]
                            first, last = kb == 0, kb == nkb - 1
                            mm(ps_t[5][hp:hp + 64, 0:TB], V.ap[:, kb, h * 64:(h + 1) * 64], pt.ap[:, 0:TB], first, last, [V, pt], pyb)
                            mm(ps_t[4][hp:hp + 64, 0:TB], ones_b[:, 0:64], pt.ap[:, 0:TB], first, last, [cb, pt], pdb)

                        stA(pi, its[0])
                        for i_, it in enumerate(its):
                            if i_ + 1 < len(its):
                                stA(pi + i_ + 1, its[i_ + 1])
                            stB(pi + i_, it)
                            next(lgen, None)
                        pi += len(its)
                        act(lnd.ap, pdn, AF.Ln, pdb, [lnd])
                        act(lnd.ap, lnd.ap, AF.Exp, [lnd], [lnd], scale=-1.0)
                        tt("vector", yfox.ap[:, c, :], py, lnd.ap, ALU.mult, pyb + [lnd], [yfox])
                    for _ in lgen:
                        pass
                    dump("qT0", qT.ap[:, 0, :], [qT])
                    dump("KT0", KT.ap[:, 0, 0:TB], [KT])
                    dump("nc0", ncum.ap[:, 0, :], [ncum])
                    dump("nc1", ncum.ap[:, 1, :], [ncum])
                    dump("V0", V.ap[:, 0, :], [V])
                    dump("V1", V.ap[:, 1, :], [V])
                    dump("yfox0", yfox.ap[:, 0, :], [yfox])
                    rmsnorm([yfox.ap[:, c, :] for c in range(3)], [yfox], o + FNG, 3, 384,
                            [ycat.ap[:, 5 + c, :] for c in range(3)], ycat, TB, (6, 256))

                    S.enabled = True
                    if blk + 1 < NBLK:
                        t1 = (blk + 1) * TB
                        rmsnorm([hT[:, c, t1:t1 + TB] for c in range(8)], [hTb[blk + 1]], o + G1, 8, D_MODEL,
                                [uT.ap[:, c, :] for c in range(8)], uT, TB, (6, 256))
                        PRE[0] = ssd_pre()
                    for dc in range(8):
                        pso, pbs = PS(2 + dc % 2, 0, 256)
                        for c in range(8):
                            mm(pso, wout.ap[:, c, dc * 128:(dc + 1) * 128], ycat.ap[:, c, :], c == 0, c == 7, wout.R(c) + [ycat], pbs)
                        tt("vector", hsl[dc], hsl[dc], pso, ALU.add, [hb] + pbs, [hb])

                if KSTOP <= 3:
                    S.dead = True
                FFN_B.reset()
                uTs = FFN_B.get([8, NT], BF16, "uTs")
                wgt = [FFN_B.get([8, 512], BF16, f"wg{i}") for i in range(2)]
                wut = [FFN_B.get([8, 512], BF16, f"wu{i}") for i in range(2)]
                wdt = [FFN_B.get([4, D_MODEL], BF16, f"wd{i}") for i in range(2)]
                actb = FFN_B.get([4, 512], BF16, "actb")
                sgt = [FFN_B.get([512], F32, f"sg{i}") for i in range(2)]
                assert FFN_B.off <= PLE_BASE, (FFN_B.off, PLE_BASE)
                PLE_B.reset()
                wpg = PLE_B.get([8, D_MODEL], BF16, "wpg")
                wpp = PLE_B.get([2, D_MODEL], BF16, "wpp")
                pTt = PLE_B.get([2, TB], BF16, "pTt")
                pTt2 = PLE_B.get([2, TB], BF16, "pTt2")
                pst_ = [PLE_B.get([256], F32, "pst0")]
                gat = [PLE_B.get([TB], F32, f"gat{i}") for i in range(2)]
                wpg.split(8)
                wpp.split(2)
                for c in range(8):
                    dma("gpsimd", wpg.ap[:, c, :], wpg_d[l, c * 128:(c + 1) * 128, :], [], [wpg.subs[c]], cast=True)
                for c in range(2):
                    dma("gpsimd", wpp.ap[:, c, :], wpp_d[l, c * 128:(c + 1) * 128, :], [], [wpp.subs[c]], cast=True)
                S.enabled = "ffn" in DBG
                uTs.split(NT // 512)

                def ffn_norm(tb4_):
                    for blk in (2 * tb4_, 2 * tb4_ + 1):
                        t0 = blk * TB
                        rmsnorm([hT[:, c, t0:t0 + TB] for c in range(8)], [hTb[blk]], o + G2, 8, D_MODEL,
                                [uTs.ap[:, c, t0:t0 + TB] for c in range(8)], Tl(None, uTs.subs[tb4_]), TB, (6, 256), sq_eng="scalar")
                groups = [(g * 512, 512) for g in range(5)] + [(2560, 256)]
                NT512 = NT // 512
                for gi, (f0, fw_) in enumerate(groups):
                    wg_, wu_, wd_ = wgt[gi % 2], wut[gi % 2], wdt[gi % 2]
                    nj = fw_ // 128
                    wg_.split(8)
                    wu_.split(8)
                    wd_.split(4)
                    wg_.b.readers = []
                    wu_.b.readers = []
                    wd_.b.readers = []
                    for c in range(8):
                        dma("gpsimd", wg_.ap[:, c, 0:fw_], wg_d[l, c * 128:(c + 1) * 128, f0:f0 + fw_], [], [wg_.subs[c]], cast=True)
                    for c in range(8):
                        dma("gpsimd", wu_.ap[:, c, 0:fw_], wu_d[l, c * 128:(c + 1) * 128, f0:f0 + fw_], [], [wu_.subs[c]], cast=True)
                    for j in range(nj):
                        dma("gpsimd", wd_.ap[:, j, :], wd_d[l, f0 + j * 128:f0 + (j + 1) * 128, :], [], [wd_.subs[j]], cast=True)
                    for tb4 in range(NT512):
                        tsl = slice(tb4 * 512, (tb4 + 1) * 512)
                        if gi == 0:
                            if tb4 == 0:
                                ffn_norm(0)
                            if tb4 + 1 < NT512:
                                ffn_norm(tb4 + 1)
                        for j in range(nj):
                            pg_, pgb_ = PS(j % 2, 0, 512)
                            pu_, pub_ = PS(2 + j % 2, 0, 512)
                            for c in range(8):
                                mm(pg_, wg_.ap[:, c, j * 128:(j + 1) * 128], uTs.ap[:, c, tsl], c == 0, c == 7, wg_.R(c) + uTs.R(tb4), pgb_)
                            for c in range(8):
                                mm(pu_, wu_.ap[:, c, j * 128:(j + 1) * 128], uTs.ap[:, c, tsl], c == 0, c == 7, wu_.R(c) + uTs.R(tb4), pub_)
                            sg_ = sgt[j % 2]
                            act(sg_.ap, pg_, AF.Silu, pgb_, [sg_])
                            tt("vector", actb.ap[:, j, :], sg_.ap, pu_, ALU.mult, [sg_] + pub_, [actb])
                        for dc in range(8):
                            pd_, pdb_ = PS(4 + dc % 2, 0, 512)
                            for j in range(nj):
                                mm(pd_, wd_.ap[:, j, dc * 128:(dc + 1) * 128], actb.ap[:, j, :], j == 0, j == nj - 1, wd_.R(j) + [actb], pdb_)
                            hbs = [hTb[2 * tb4], hTb[2 * tb4 + 1]]
                            tt("vector", hT[:, dc, tsl], hT[:, dc, tsl], pd_, ALU.add, hbs + pdb_, hbs)

                S.enabled = True
                nxt = None
                if l + 1 < DEPTH:
                    nxt = l + 1
                elif s + 1 < NSEQ:
                    nxt = 0
                if nxt is not None:
                    PREFETCHED[0] = load_mixer_weights(nxt)
                S.enabled = "ple" in DBG
                pTts = [pTt, pTt2]

                def ple_pre(blk):
                    t0 = blk * TB
                    hb = hTb[blk]
                    hsl = [hT[:, c, t0:t0 + TB] for c in range(8)]
                    ub = uT if blk % 2 == 0 else ycat
                    rmsnorm(hsl, [hb], o + G3, 8, D_MODEL, [ub.ap[:, c, :] for c in range(8)], ub, TB, (6, 256), sq_eng="scalar")
                    pTb = pTts[blk % 2]
                    for j in range(TB // 128):
                        ps_ = pst_[0]
                        dma("sync", ps_.ap, p_d[l, s, t0 + j * 128:t0 + (j + 1) * 128, :], [], [ps_])
                        pt_, ptb = PS(4, 0, 256)
                        for c in range(2):
                            tr(ps_t[4][:, c * 128:(c + 1) * 128], ps_.ap[:, c * 128:(c + 1) * 128], ident_f, [ps_, cf], ptb)
                        cp("scalar", pTb.ap[:, :, j * 128:(j + 1) * 128], pt_.rearrange("p (a b) -> p a b", a=2), ptb, [pTb])

                def ple_main(blk):
                    t0 = blk * TB
                    hb = hTb[blk]
                    hsl = [hT[:, c, t0:t0 + TB] for c in range(8)]
                    ub = uT if blk % 2 == 0 else ycat
                    pTb = pTts[blk % 2]
                    for dc in range(8):
                        pg_, pgb_ = PS(dc % 2, 0, 256)
                        pp_, ppb_ = PS(2 + dc % 2, 0, 256)
                        for c in range(8):
                            mm(pg_, wpg.ap[:, c, dc * 128:(dc + 1) * 128], ub.ap[:, c, :], c == 0, c == 7, wpg.R(c) + [ub], pgb_)
                        for c in range(2):
                            mm(pp_, wpp.ap[:, c, dc * 128:(dc + 1) * 128], pTb.ap[:, c, :], c == 0, c == 1, wpp.R(c) + [pTb], ppb_)
                        ga = gat[dc % 2]
                        act(ga.ap, pg_, AF.Sigmoid, pgb_ + [pv], [ga], bias=pv.ap[:, o + BPG + dc:o + BPG + dc + 1])
                        tt("vector", ga.ap, ga.ap, pp_, ALU.mult, [ga] + ppb_, [ga])
                        tt("vector", hsl[dc], hsl[dc], ga.ap, ALU.add, [hb, ga], [hb])

                if "ple" in DBG:
                    ple_pre(0)
                    for blk in range(NBLK):
                        if blk + 1 < NBLK:
                            ple_pre(blk + 1)
                        ple_main(blk)

            S.enabled = True
            if KSTOP <= 4:
                S.dead = True
            SC.reset()
            onT = SC.get([8, TB], F32, "onT")
            ost = [uT_f, ycat_f]
            for blk in range(NBLK):
                t0 = blk * TB
                rmsnorm([hT[:, c, t0:t0 + TB] for c in range(8)], [hTb[blk]], GF, 8, D_MODEL,
                        [onT.ap[:, c, :] for c in range(8)], onT, TB, (6, 256))
                for j in range(TB // 128):
                    if KFIN < 2:
                        break
                    os_ = ost[j % 2]
                    for half in range(2):
                        bank = half
                        pso, pbs = PS(bank, 0, 512)
                        for q in range(4):
                            c = half * 4 + q
                            tr(ps_t[bank][:, q * 128:(q + 1) * 128], onT.ap[:, c, j * 128:(j + 1) * 128], ident_f, [onT, cf], pbs)
                        cp("vector" if half == 0 else "scalar", os_.ap[:, half * 512:(half + 1) * 512], pso, pbs, [os_])
                    if KFIN >= 3:
                        dma("sync", y_d[s, t0 + j * 128:t0 + (j + 1) * 128, :], os_.ap, [os_], [], is_out=True)
        if KDUMP:
            S.enabled = True
            S.dead = False
            dma("sync", dbg_d, dbgt.ap, [dbgt], [], is_out=True)
        S.emit()
    return nc


PREFETCHED = [None]
import os
DBG = set(os.environ.get("KDBG", "ssd,lru,fox,ffn,ple").split(","))
KSTOP = int(os.environ.get("KSTOP", "99"))
KFIN = int(os.environ.get("KFIN", "3"))
KPLE = int(os.environ.get("KPLE", "3"))
KDUMP = int(os.environ.get("KDUMP", "0"))
DUMPS = {}

def _consts():
    ident = np.eye(128, dtype=np.float32)
    tri = np.triu(np.ones((128, 128), np.float32))
    ones = np.ones((128, 128), np.float32)
    mneg = np.where(np.arange(128)[:, None] > np.arange(128)[None, :], np.float32(-30000.0), np.float32(0.0)).astype(np.float32)
    cf = np.concatenate([ident, tri, ones], axis=1)
    cb = np.concatenate([ident, ones, mneg], axis=1)
    sel = np.zeros((6, 6, 128), np.float32)
    for h in range(6):
        sel[h, h, :] = 1.0
    return cf, cb, sel.reshape(6, 768)


def _pcol(v, nch):
    return np.ascontiguousarray(np.asarray(v, np.float32).reshape(nch, 128).T)


def _pack(inp, DEPTH):
    f = lambda k: np.asarray(inp[k], np.float32)
    pvs = []
    w_in_r, bds = [], []
    for l in range(DEPTH):
        cols = np.zeros((128, NPV), np.float32)
        cols[:, G1:G1 + 8] = _pcol(f("norm1_g")[l], 8)
        cols[:, G2:G2 + 8] = _pcol(f("norm2_g")[l], 8)
        cols[:, G3:G3 + 8] = _pcol(f("norm3_g")[l], 8)
        scw = f("ssd_conv_w")[l]
        for c in range(7):
            for k in range(4):
                cols[:, SCW + c * 4 + k] = scw[k, c * 128:(c + 1) * 128]
        cols[:, SCB:SCB + 7] = _pcol(f("ssd_conv_b")[l], 7)
        cols[:, DSK:DSK + 3] = _pcol(np.repeat(f("ssd_d")[l], 64), 3)
        cols[:, SNG:SNG + 3] = _pcol(f("ssd_norm_g")[l], 3)
        lcw = f("lru_conv_w")[l]
        for c in range(2):
            for k in range(4):
                cols[:, LCW + c * 4 + k] = lcw[k, c * 128:(c + 1) * 128]
        cols[:, LCB:LCB + 2] = _pcol(f("lru_conv_b")[l], 2)
        cols[:, LBA:LBA + 2] = _pcol(f("lru_b_a")[l], 2)
        cols[:, LBX:LBX + 2] = _pcol(f("lru_b_x")[l], 2)
        cols[:, LAM:LAM + 2] = _pcol(f("lru_lambda")[l], 2)
        cols[:, LNG:LNG + 2] = _pcol(f("lru_norm_g")[l], 2)
        cols[:, FNG:FNG + 3] = _pcol(f("fox_norm_g")[l], 3)
        cols[:, BPG:BPG + 8] = _pcol(f("b_ple_gate")[l], 8)
        cols[:, DTB:DTB + 6] = np.broadcast_to(f("ssd_dt_bias")[l][None, :], (128, 6))
        cols[:, ALOG:ALOG + 6] = np.broadcast_to(f("ssd_a_log")[l][None, :], (128, 6))
        cols[:, FBF:FBF + 6] = np.broadcast_to(f("fox_b_f")[l][None, :], (128, 6))
        pvs.append(cols)
        w = f("w_in")[l]
        z, xbc, dt, lx, lg, q, k, v, fr = np.split(w, np.cumsum([384, 896, 6, 256, 256, 384, 384, 384])[:], axis=1)
        w_in_r.append(np.concatenate([z, xbc, lx, lg, q, k, v, dt, fr], axis=1))
        bd = np.zeros((128, 4, 128), np.float32)
        wa, wx = f("lru_w_a")[l], f("lru_w_x")[l]
        for c in range(2):
            for i in range(2):
                bd[i * 64:(i + 1) * 64, c, i * 64:(i + 1) * 64] = wa[2 * c + i]
                bd[i * 64:(i + 1) * 64, 2 + c, i * 64:(i + 1) * 64] = wx[2 * c + i]
        bds.append(bd.reshape(128, 512))
    pv = np.concatenate(pvs + [_pcol(f("final_norm_g"), 8)], axis=1)
    cf, cb, sel = _consts()
    shared = {
        "w_in": np.ascontiguousarray(np.stack(w_in_r)), "w_out": f("w_out")[:DEPTH], "bd": np.stack(bds),
        "w_gate": f("w_gate")[:DEPTH], "w_up": f("w_up")[:DEPTH], "w_down": f("w_down")[:DEPTH],
        "w_pg": f("w_ple_gate")[:DEPTH], "w_pp": f("w_ple_proj")[:DEPTH],
        "pv": np.ascontiguousarray(pv), "cf": cf, "cb": cb, "sel": sel,
    }
    return shared


_NC_CACHE = {}


def run(inp, NCORES, DEPTH):
    x = np.asarray(inp["x"], np.float32)
    p = np.asarray(inp["p"], np.float32)
    B, NT, _ = x.shape
    NSEQ = B // NCORES
    key = (NSEQ, NT, DEPTH)
    if key not in _NC_CACHE:
        PREFETCHED[0] = None
        _NC_CACHE[key] = build(NSEQ, NT, DEPTH)
    nc = _NC_CACHE[key]
    shared = _pack(inp, DEPTH)
    in_maps = []
    for c in range(NCORES):
        m = dict(shared)
        m["x"] = np.ascontiguousarray(x[c * NSEQ:(c + 1) * NSEQ])
        m["p"] = np.ascontiguousarray(p[:DEPTH, c * NSEQ:(c + 1) * NSEQ])
        in_maps.append(m)
    res = run_bass_kernel_spmd(nc, in_maps, core_ids=list(range(NCORES)))
    if KDUMP:
        global LAST_DBG
        LAST_DBG = res.results[0]["dbg"]
    return np.concatenate([r["y"] for r in res.results], axis=0)


def kernel(**inputs):
    return run(inputs, 8, 2)
```

```python
import contextlib
import numpy as np
import concourse.bass as bass
import concourse.mybir as mybir
from concourse.bass_utils import run_bass_kernel_spmd

F32 = mybir.dt.float32
BF16 = mybir.dt.bfloat16
AF = mybir.ActivationFunctionType
ALU = mybir.AluOpType

ENGS = ["tensor", "vector", "scalar", "gpsimd", "sync"]
SEM_CHUNK = 3000

D_MODEL = 1024
D_FF = 2816
IN_COLS = 2956
OZ, OXS, OB, OC, OLX, OLG, OQ, OK_, OV, ODT, OFR = 0, 384, 768, 1024, 1280, 1536, 1792, 2176, 2560, 2944, 2950
G1, G2, G3, SCW, SCB, DSK, SNG, LCW, LCB, LBA, LBX, LAM, LNG, FNG, BPG, DTB, ALOG, FBF, NPV = (
    0, 8, 16, 24, 52, 59, 62, 65, 73, 75, 77, 79, 81, 83, 86, 94, 100, 106, 112)
EPS = 1e-6
TB = 256


class Buf:
    __slots__ = ("name", "writer", "readers")

    def __init__(self, name=""):
        self.name = name
        self.writer = None
        self.readers = []


class Sched:
    def __init__(self, nc):
        self.nc = nc
        self.prog = {e: [] for e in ENGS}
        self.clock = {e: {} for e in ENGS}
        self.dma_count = []
        self.dma_last = []
        self.out_stamps = []
        self.pool = {}
        self.pool_i = {}

    def new_dma_sem(self):
        self.dma_count.append(0)
        self.dma_last.append(None)
        return len(self.dma_count) - 1

    def pool_sem(self, q, n=28):
        if q not in self.pool:
            self.pool[q] = [self.new_dma_sem() for _ in range(n)]
            self.pool_i[q] = 0
        k = self.pool[q][self.pool_i[q] % n]
        self.pool_i[q] += 1
        return k

    def _need(self, eng, stamp, waits):
        if stamp is None:
            return
        kind, key, val, snap = stamp
        if kind == "c" and key == eng and eng == "tensor":
            return
        ck = self.clock[eng]
        k = (kind, key)
        if ck.get(k, -1) >= val:
            return
        waits.append((kind, key, val))
        if kind == "c":
            self.prog[key][val]["inc"] = True
        for kk, vv in snap.items():
            if ck.get(kk, -1) < vv:
                ck[kk] = vv
        ck[k] = val

    enabled = True
    dead = False

    def op(self, eng, fn, reads=(), writes=(), dma_sem=None, is_out=False):
        if not self.enabled or self.dead:
            return None
        waits = []
        for b in reads:
            self._need(eng, b.writer, waits)
        for b in writes:
            self._need(eng, b.writer, waits)
            for r in b.readers:
                self._need(eng, r, waits)
        if dma_sem is not None:
            self._need(eng, self.dma_last[dma_sem], waits)
        idx = len(self.prog[eng])
        ent = {"waits": waits, "fn": fn, "inc": False, "dma": None}
        self.prog[eng].append(ent)
        snap = dict(self.clock[eng])
        if dma_sem is None:
            stamp = ("c", eng, idx, snap)
        else:
            self.dma_count[dma_sem] += 1
            val = 16 * self.dma_count[dma_sem]
            ent["dma"] = (dma_sem, val)
            stamp = ("d", dma_sem, val, snap)
            self.dma_last[dma_sem] = stamp
            if is_out:
                self.out_stamps.append(stamp)
        for b in reads:
            b.readers.append(stamp)
        for b in writes:
            b.writer = stamp
            b.readers = []
        return stamp

    def emit(self, final_eng="sync"):
        nc = self.nc
        waits = []
        for st in self.out_stamps:
            self._need(final_eng, st, waits)
        self.prog[final_eng].append({"waits": waits, "fn": None, "inc": False, "dma": None})
        with contextlib.ExitStack() as es:
            csem, rank = {}, {}
            for e in ENGS:
                r = 0
                rank[e] = {}
                for i, ent in enumerate(self.prog[e]):
                    if ent["inc"]:
                        rank[e][i] = r
                        r += 1
                nsem = (r + SEM_CHUNK - 1) // SEM_CHUNK
                csem[e] = [es.enter_context(nc.semaphore(f"c_{e}_{k}")) for k in range(nsem)]
            dsem = [es.enter_context(nc.semaphore(f"d_{k}")) for k in range(len(self.dma_count))]
            block = es.enter_context(nc.Block())

            def mk(e):
                def body(engh):
                    for i, ent in enumerate(self.prog[e]):
                        for (kind, key, val) in ent["waits"]:
                            if kind == "c":
                                r = rank[key][val]
                                engh.wait_ge(csem[key][r // SEM_CHUNK], r % SEM_CHUNK + 1)
                            else:
                                engh.wait_ge(dsem[key], val)
                        if ent["fn"] is None:
                            continue
                        ins = ent["fn"](engh)
                        if ent["dma"] is not None:
                            ins.then_inc(dsem[ent["dma"][0]], 16)
                        elif ent["inc"]:
                            r = rank[e][i]
                            ins.then_inc(csem[e][r // SEM_CHUNK], 1)
                return body

            for e in ENGS:
                if self.prog[e]:
                    getattr(block, e)(mk(e))


class Tl:
    __slots__ = ("ap", "b", "subs")

    def __init__(self, ap, b):
        self.ap = ap
        self.b = b
        self.subs = None

    def split(self, n):
        self.subs = [Buf() for _ in range(n)]
        for sb_ in self.subs:
            sb_.readers = list(self.b.readers)
        return self

    def R(self, i):
        return [self.b, self.subs[i]]


class Arena:
    def __init__(self, tensor_bf16, nbytes):
        self.t = tensor_bf16
        self.n = nbytes
        self.live = []

    def view(self, off, shape, dt, name=""):
        n = 1
        for s in shape:
            n *= s
        size = n * (4 if dt == F32 else 2)
        assert off % 4 == 0 and off + size <= self.n, (name, off, size, self.n)
        ap = self.t[:, off // 2:(off + size) // 2]
        if dt == F32:
            ap = ap.bitcast(F32)
        if len(shape) == 2:
            ap = ap.rearrange("p (a b) -> p a b", a=shape[0])
        elif len(shape) == 3:
            ap = ap.rearrange("p (a b c) -> p a b c", a=shape[0], b=shape[1])
        buf = Buf(name)
        newlive = []
        for (s, e, b) in self.live:
            if s < off + size and off < e:
                if b.writer is not None:
                    buf.readers.append(b.writer)
                buf.readers.extend(b.readers)
                if not (s >= off and e <= off + size):
                    newlive.append((s, e, b))
            else:
                newlive.append((s, e, b))
        newlive.append((off, off + size, buf))
        self.live = newlive
        return Tl(ap, buf)


class Bump:
    def __init__(self, arena, base):
        self.a = arena
        self.base = base
        self.off = base

    def reset(self):
        self.off = self.base

    def get(self, shape, dt, name=""):
        n = 1
        for s in shape:
            n *= s
        size = (n * (4 if dt == F32 else 2) + 31) // 32 * 32
        t = self.a.view(self.off, shape, dt, name)
        self.off += size
        return t


def build(NSEQ, NT, DEPTH):
    NBLK = NT // TB
    NT128 = NT // 128
    nc = bass.Bass("TRN2", target_bir_lowering=False)
    din = lambda name, shape: nc.dram_tensor(name, shape, F32, kind="ExternalInput").ap()
    x_d = din("x", [NSEQ, NT, D_MODEL])
    p_d = din("p", [DEPTH, NSEQ, NT, 256])
    win_d = din("w_in", [DEPTH, D_MODEL, IN_COLS])
    wout_d = din("w_out", [DEPTH, D_MODEL, D_MODEL])
    bd_d = din("bd", [DEPTH, 128, 4 * 128])
    wg_d = din("w_gate", [DEPTH, D_MODEL, D_FF])
    wu_d = din("w_up", [DEPTH, D_MODEL, D_FF])
    wd_d = din("w_down", [DEPTH, D_FF, D_MODEL])
    wpg_d = din("w_pg", [DEPTH, D_MODEL, D_MODEL])
    wpp_d = din("w_pp", [DEPTH, 256, D_MODEL])
    pv_d = din("pv", [128, DEPTH * NPV + 8])
    cf_d = din("cf", [128, 3 * 128])
    cb_d = din("cb", [128, 3 * 128])
    sel_d = din("sel", [6, 6 * 128])
    y_d = nc.dram_tensor("y", [NSEQ, NT, D_MODEL], F32, kind="ExternalOutput").ap()
    if KDUMP:
        dbg_d = nc.dram_tensor("dbg", [128, 8192], F32, kind="ExternalOutput").ap()

    es = contextlib.ExitStack()
    with es:
        sb = lambda name, shape, dt: es.enter_context(nc.sbuf_tensor("s_" + name, shape, dt))
        S = Sched(nc)
        GF = DEPTH * NPV

        def static(name, shape, dt):
            return Tl(sb(name, shape, dt)[:], Buf(name))

        hT = sb("hT", [128, 8, NT], F32)
        hTb = [Buf(f"hT{i}") for i in range(NBLK)]
        cf = static("cf", [128, 3, 128], F32)
        cb = static("cb", [128, 3, 128], BF16)
        pv = static("pv", [128, DEPTH * NPV + 8], F32)
        dv = static("dv", [128, DEPTH, 12], F32)
        assert TB == 256
        uT_f = static("uT", [128, 1024], F32)
        ycat_f = static("ycat", [128, 1024], F32)
        uT = Tl(uT_f.ap.bitcast(BF16).rearrange("p (c t) -> p c t", c=8), uT_f.b)
        ycat = Tl(ycat_f.ap.bitcast(BF16).rearrange("p (c t) -> p c t", c=8), ycat_f.b)
        sq = static("sq", [128, 8, TB], BF16)
        rs0 = static("rs0", [128, TB], F32)
        rs1 = static("rs1", [128, TB], F32)
        ARENA_BYTES = 125 * 1024
        arena_t = sb("arena", [128, ARENA_BYTES // 2], BF16)
        A = Arena(arena_t, ARENA_BYTES)
        ps_t = [es.enter_context(nc.psum_tensor(f"ps{i}", [128, 512], F32)) for i in range(7)]
        psb_t = es.enter_context(nc.psum_tensor("psb", [128, 1024], BF16))
        pbuf = [[Buf(f"ps{i}_{h}") for h in range(2)] for i in range(7)]
        psbb = Buf("psb")

        def PS(bank, c0, c1, p0=0, p1=128):
            bs = [pbuf[bank][0], pbuf[bank][1]]
            return ps_t[bank][p0:p1, c0:c1], bs

        ident_f = cf.ap[:, 0, :]
        tri_f = cf.ap[:, 1, :]
        ones_f = cf.ap[:, 2, :]
        ident_b = cb.ap[:, 0, :]
        ones_b = cb.ap[:, 1, :]
        mneg_b = cb.ap[:, 2, :]

        def bl(xs):
            out = []
            for x_ in xs:
                if isinstance(x_, Tl):
                    out.append(x_.b)
                elif isinstance(x_, (list, tuple)):
                    out.extend(bl(x_))
                elif x_ is not None:
                    out.append(x_)
            return out

        def mm(out, lhsT, rhs, start, stop, r, w):
            S.op("tensor", lambda e: e.matmul(out, lhsT=lhsT, rhs=rhs, start=start, stop=stop), bl(r), bl(w))

        def tr(out, in_, ident, r, w):
            S.op("tensor", lambda e: e.transpose(out, in_, ident), bl(r), bl(w))

        def act(out, in_, func, r, w, bias=None, scale=None):
            kw = {}
            if bias is not None:
                kw["bias"] = bias
            if scale is not None:
                kw["scale"] = scale
            S.op("scalar", lambda e: e.activation(out=out, in_=in_, func=func, **kw), bl(r), bl(w))

        def tt(eng, out, in0, in1, op, r, w):
            S.op(eng, lambda e: e.tensor_tensor(out=out, in0=in0, in1=in1, op=op), bl(r), bl(w))

        def ts(eng, out, in0, s1, s2, op0, op1, r, w):
            if op1 is None:
                S.op(eng, lambda e: e.tensor_scalar(out=out, in0=in0, scalar1=s1, scalar2=None, op0=op0), bl(r), bl(w))
            else:
                S.op(eng, lambda e: e.tensor_scalar(out=out, in0=in0, scalar1=s1, scalar2=s2, op0=op0, op1=op1), bl(r), bl(w))

        def stt(out, in0, scalar, in1, op0, op1, r, w):
            S.op("vector", lambda e: e.scalar_tensor_tensor(out=out, in0=in0, scalar=scalar, in1=in1, op0=op0, op1=op1), bl(r), bl(w))

        def cp(eng, out, in_, r, w):
            if eng == "scalar":
                act(out, in_, AF.Copy, r, w)
            else:
                S.op(eng, lambda e: e.tensor_copy(out=out, in_=in_), bl(r), bl(w))

        def mset(eng, ap, val, w):
            S.op(eng, lambda e: e.memset(ap, val), [], bl(w))

        def scan(out, d0, d1, init, r, w):
            S.op("vector", lambda e: e.tensor_tensor_scan(out=out, data0=d0, data1=d1, initial=init, op0=ALU.mult, op1=ALU.add),
                 bl(r), bl(w))

        def dma(q, out, in_, r, w, is_out=False, cast=False):
            kw = {"max_dma_last_dim": 8192} if cast else {}
            return S.op(q, lambda e: e.dma_start(out=out, in_=in_, **kw), bl(r), bl(w), dma_sem=S.pool_sem(q), is_out=is_out)

        dbg_state = [0]
        if KDUMP:
            dbgt = static("dbgt", [128, 8192], F32)

        def dump(name, ap, r, p0=0, p1=128):
            if not KDUMP or name in DUMPS or S.dead or not S.enabled:
                return
            n = ap.shape[-1]
            c0 = dbg_state[0]
            DUMPS[name] = (c0, n, p0, p1)
            dbg_state[0] += n
            S.op("vector", lambda e: e.tensor_copy(out=dbgt.ap[p0:p1, c0:c0 + n], in_=ap), bl(r), [dbgt.b])

        dma("sync", cf.ap, cf_d.rearrange("p (a b) -> p a b", a=3), [], [cf])
        dma("sync", pv.ap, pv_d, [], [pv])
        dma("gpsimd", cb.ap, cb_d.rearrange("p (a b) -> p a b", a=3), [], [cb], cast=True)
        for l in range(DEPTH):
            o = l * NPV
            act(dv.ap[:, l, 0:6], pv.ap[:, o + ALOG:o + ALOG + 6], AF.Exp, [pv], [dv])
            ts("vector", dv.ap[:, l, 0:6], dv.ap[:, l, 0:6], -1.0, None, ALU.mult, None, [dv], [dv])
            act(dv.ap[:, l, 6:8], pv.ap[:, o + LAM:o + LAM + 2], AF.Exp, [pv], [dv], scale=-1.0)
            act(dv.ap[:, l, 6:8], dv.ap[:, l, 6:8], AF.Ln, [dv], [dv], bias=1.0)
            ts("vector", dv.ap[:, l, 6:8], dv.ap[:, l, 6:8], -8.0, None, ALU.mult, None, [dv], [dv])
            ts("vector", dv.ap[:, l, 8:10], pv.ap[:, o + LBA:o + LBA + 2], -1.0, None, ALU.mult, None, [pv], [dv])
            ts("vector", dv.ap[:, l, 10:12], pv.ap[:, o + LBX:o + LBX + 2], -1.0, None, ALU.mult, None, [pv], [dv])

        if KSTOP <= 1:
            S.dead = True
        def rmsnorm(src_aps, src_bufs, gcol0, nch, width, out_aps, out_tl, ncols, psum_loc, sq_eng="gpsimd"):
            bank, c0 = psum_loc
            pso, psb_ = PS(bank, c0, c0 + ncols)
            for c in range(nch):
                if sq_eng == "gpsimd":
                    eng = ("gpsimd", "vector", "scalar")[c % 3]
                else:
                    eng = sq_eng if c % 2 == 0 else "vector"
                if eng == "scalar":
                    act(sq.ap[:, c, 0:ncols], src_aps[c], AF.Square, src_bufs, [sq])
                else:
                    tt(eng, sq.ap[:, c, 0:ncols], src_aps[c], src_aps[c], ALU.mult, src_bufs, [sq])
            for c in range(nch):
                mm(pso, ones_b, sq.ap[:, c, 0:ncols], c == 0, c == nch - 1, [cb, sq], psb_)
            act(rs0.ap[:, 0:ncols], pso, AF.Ln, psb_, [rs0], bias=EPS_AP, scale=1.0 / width)
            act(rs1.ap[:, 0:ncols], rs0.ap[:, 0:ncols], AF.Exp, [rs0], [rs1], scale=-0.5)
            for c in range(nch):
                stt(out_aps[c], src_aps[c], pv.ap[:, gcol0 + c:gcol0 + c + 1], rs1.ap[:, 0:ncols], ALU.mult, ALU.mult,
                    src_bufs + [pv, rs1], [out_tl])

        epst = static("epst", [128, 1], F32)
        mset("vector", epst.ap, EPS, [epst])
        EPS_AP = epst.ap[:, 0:1]

        MP = Bump(A, 0)

        def mixer_persist():
            MP.reset()
            d = {}
            d["win"] = MP.get([8, IN_COLS], BF16, "win")
            d["wout"] = MP.get([8, D_MODEL], BF16, "wout")
            d["bd"] = MP.get([4, 128], BF16, "bd")
            d["KT"] = MP.get([3, NT], BF16, "KT")
            d["V"] = MP.get([NT128, 384], BF16, "V")
            d["ncum"] = MP.get([NT128, 6], F32, "ncum")
            d["xraw"] = MP.get([7, TB + 3], F32, "xraw")
            d["lraw"] = MP.get([2, TB + 3], F32, "lraw")
            d["st"] = MP.get([384], F32, "st")
            d["stb"] = MP.get([384], BF16, "stb")
            d["lcar"] = MP.get([2], F32, "lcar")
            d["fcar"] = MP.get([6], F32, "fcar")
            d["fref"] = MP.get([6], F32, "fref")
            return d

        _tmp = mixer_persist()
        SCR_BASE = MP.off
        A.live = []
        SC = Bump(A, SCR_BASE)
        FFN_B = Bump(A, 0)
        PLE_BASE = SCR_BASE
        PLE_B = Bump(A, PLE_BASE)

        def load_mixer_weights(l):
            d = mixer_persist()
            d["win"].split(8)
            d["wout"].split(8)
            for c in range(8):
                dma("gpsimd", d["win"].ap[:, c, :], win_d[l, c * 128:(c + 1) * 128, :], [], [d["win"].subs[c]], cast=True)
            for c in range(8):
                dma("gpsimd", d["wout"].ap[:, c, :], wout_d[l, c * 128:(c + 1) * 128, :], [], [d["wout"].subs[c]], cast=True)
            dma("gpsimd", d["bd"].ap, bd_d[l].rearrange("p (a b) -> p a b", a=4), [], [d["bd"]], cast=True)
            return d

        for s in range(NSEQ):
            SC.reset()
            xst = [uT_f, ycat_f]
            for t in range(NT128):
                st_ = xst[t % 2]
                dma("sync", st_.ap, x_d[s, t * 128:(t + 1) * 128, :], [], [st_])
                for half in range(2):
                    bank = (2 * t + half) % 4
                    pso, pbs = PS(bank, 0, 512)
                    for j in range(4):
                        c = half * 4 + j
                        tr(ps_t[bank][:, j * 128:(j + 1) * 128], st_.ap[:, c * 128:(c + 1) * 128], ident_f, [st_, cf], pbs)
                    eng = "vector" if half == 0 else "scalar"
                    cp(eng, hT[:, half * 4:half * 4 + 4, t * 128:(t + 1) * 128],
                       pso.rearrange("p (a b) -> p a b", a=4), pbs, [hTb[(t * 128) // TB]])

            if KSTOP <= 2:
                S.dead = True
            for l in range(DEPTH):
                o = l * NPV
                W = PREFETCHED[0] if PREFETCHED[0] is not None else load_mixer_weights(l)
                PREFETCHED[0] = None
                win, wout, bdm = W["win"], W["wout"], W["bd"]
                KT, V, ncum, xraw, lraw = W["KT"], W["V"], W["ncum"], W["xraw"], W["lraw"]
                st, stb, lcar, fcar = W["st"], W["stb"], W["lcar"], W["fcar"]
                fref = W["fref"]
                mset("vector", xraw.ap[:, :, 0:3], 0.0, [xraw])
                mset("vector", lraw.ap[:, :, 0:3], 0.0, [lraw])
                mset("gpsimd", st.ap, 0.0, [st])
                mset("gpsimd", stb.ap, 0.0, [stb])
                mset("gpsimd", lcar.ap, 0.0, [lcar])
                mset("gpsimd", fcar.ap, 0.0, [fcar])

                for blk in range(NBLK):
                    t0 = blk * TB
                    hb = hTb[blk]
                    hsl = [hT[:, c, t0:t0 + TB] for c in range(8)]
                    if blk == 0:
                        rmsnorm(hsl, [hb], o + G1, 8, D_MODEL, [uT.ap[:, c, :] for c in range(8)], uT, TB, (6, 256))

                    slot = [0]

                    def proj(col0, ncolsM=128):
                        i = slot[0]
                        slot[0] += 1
                        bank, half = i % 2, (i // 2) % 2
                        pso, pbs = PS(bank, half * 256, half * 256 + 256)
                        for c in range(8):
                            mm(pso, win.ap[:, c, col0:col0 + ncolsM], uT.ap[:, c, :], c == 0, c == 7, win.R(c) + [uT], pbs)
                        return pso, pbs

                    SC.reset()
                    if "ssd" not in DBG:
                        for c in range(3):
                            mset("vector", ycat.ap[:, c, :], 0.0, [ycat])
                    S.enabled = "ssd" in DBG
                    xbc = SC.get([7, TB], BF16, "xbc")
                    zs = SC.get([3, TB], F32, "zs")
                    yssd = SC.get([3, TB], F32, "yssd")
                    cacc = [SC.get([TB], F32, f"cacc{i}") for i in range(2)]
                    sm = SC.get([8, 6], F32, "sm")
                    xdt = SC.get([384], BF16, "xdt")
                    xdts = SC.get([384], BF16, "xdts")
                    btok = SC.get([256], BF16, "btok")
                    Dm = [SC.get([128], F32, f"D{i}") for i in range(2)]
                    MT = [SC.get([128], BF16, f"MT{i}") for i in range(2)]
                    ebc = [SC.get([128], F32, f"ebc{i}") for i in range(2)]
                    Cs = [SC.get([128], BF16, f"Cs{i}") for i in range(2)]
                    stt_tmp = SC.get([384], F32, "sttmp")
                    smp = SC.get([24], F32, "smp")

                    for c in range(3):
                        pso, pbs = proj(OZ + c * 128)
                        act(zs.ap[:, c, :], pso, AF.Silu, pbs, [zs])
                    xbc.split(7)
                    xbc.b.readers = []
                    xraw_s = [Buf() for _ in range(7)]
                    for sb_ in xraw_s:
                        sb_.writer = xraw.b.writer
                        sb_.readers = list(xraw.b.readers)

                    def silu_c(c):
                        act(xbc.ap[:, c, :], cacc[c % 2].ap, AF.Silu, [cacc[c % 2]], [xbc.subs[c]])

                    for c in range(7):
                        pso, pbs = proj(OXS + c * 128)
                        cp("scalar", xraw.ap[:, c, 3:3 + TB], pso, pbs, [xraw_s[c]])
                        if c > 0:
                            silu_c(c - 1)
                        ca = cacc[c % 2]
                        wc = lambda k: pv.ap[:, o + SCW + c * 4 + k:o + SCW + c * 4 + k + 1]
                        ts("vector", ca.ap, xraw.ap[:, c, 0:TB], wc(0), pv.ap[:, o + SCB + c:o + SCB + c + 1], ALU.mult, ALU.add,
                           [xraw_s[c], pv], [ca])
                        for k in range(1, 4):
                            stt(ca.ap, xraw.ap[:, c, k:k + TB], wc(k), ca.ap, ALU.mult, ALU.add, [xraw_s[c], pv, ca], [ca])
                    silu_c(6)
                    cp("gpsimd", xraw.ap[:, :, 0:3], xraw.ap[:, :, TB:TB + 3], xraw_s, xraw_s + [xraw])
                    cp("vector", fref.ap, fcar.ap, [fcar], [fref])
                    for j in range(TB // 128):
                        cs = slice(j * 128, (j + 1) * 128)
                        tg = t0 + j * 128
                        pso, pbs = PS(2, 0, 396)
                        for c in range(8):
                            mm(pso, uT.ap[:, c, cs], win.ap[:, c, OV:OV + 396], c == 0, c == 7, win.R(c) + [uT], pbs)
                        kb = tg // 128
                        cp("vector", V.ap[:, kb, :], ps_t[2][:, 0:384], pbs, [V])
                        dt, adt, nacs, tmp6, decs, cdec, sdt, nlf = [sm.ap[:, i, :] for i in range(8)]
                        tt("vector", tmp6, ps_t[2][:, 384:390], pv.ap[:, o + DTB:o + DTB + 6], ALU.add, pbs + [pv], [sm])
                        tt("vector", nlf, ps_t[2][:, 390:396], pv.ap[:, o + FBF:o + FBF + 6], ALU.add, pbs + [pv], [sm])
                        act(tmp6, tmp6, AF.Exp, [sm], [sm])
                        act(nlf, nlf, AF.Exp, [sm], [sm], scale=-1.0)
                        act(dt, tmp6, AF.Ln, [sm], [sm], bias=1.0)
                        act(nlf, nlf, AF.Ln, [sm], [sm], bias=1.0)
                        tt("vector", adt, dt, dv.ap[:, l, 0:6], ALU.mult, [sm, dv], [sm])
                        p4, p4b = PS(4, 384, 408)
                        mm(ps_t[4][:, 384:390], tri_f, adt, True, True, [cf, sm], p4b)
                        mm(ps_t[4][:, 390:396], ones_f, adt, True, True, [cf, sm], p4b)
                        mm(ps_t[4][:, 396:402], tri_f, nlf, True, True, [cf, sm], p4b)
                        mm(ps_t[4][:, 402:408], ones_f, nlf, True, True, [cf, sm], p4b)
                        cp("vector", smp.ap, ps_t[4][:, 384:408], p4b, [smp])
                        ts("vector", nacs, smp.ap[:, 0:6], -1.0, None, ALU.mult, None, [smp], [sm])
                        tt("vector", tmp6, smp.ap[:, 6:12], nacs, ALU.add, [smp, sm], [sm])
                        act(decs, tmp6, AF.Exp, [sm], [sm])
                        act(cdec, smp.ap[:, 6:12], AF.Exp, [smp], [sm])
                        tt("vector", sdt, dt, decs, ALU.mult, [sm], [sm])
                        tt("vector", ncum.ap[:, kb, :], smp.ap[:, 12:18], fcar.ap, ALU.add, [smp, fcar], [ncum])
                        tt("vector", fcar.ap, smp.ap[:, 18:24], fcar.ap, ALU.add, [smp, fcar], [fcar])
                        for c in range(5):
                            tr(psb_t[:, c * 128:(c + 1) * 128], xbc.ap[:, c, cs], ident_b, xbc.R(c) + [cb], [psbb])
                        xs_tok = psb_t[:, 0:384].rearrange("p (h e) -> p h e", h=6)
                        tt("vector", xdt.ap.rearrange("p (h e) -> p h e", h=6), xs_tok, dt.unsqueeze(2).to_broadcast([128, 6, 64]),
                           ALU.mult, [psbb, sm], [xdt])
                        tt("vector", xdts.ap.rearrange("p (h e) -> p h e", h=6), xs_tok, sdt.unsqueeze(2).to_broadcast([128, 6, 64]),
                           ALU.mult, [psbb, sm], [xdts])
                        cp("vector", btok.ap, psb_t[:, 384:640], [psbb], [btok])
                        pg, pgb = PS(3, 0, 256)
                        for g in range(2):
                            mm(ps_t[3][:, g * 128:(g + 1) * 128], xbc.ap[:, 3 + g, cs], xbc.ap[:, 5 + g, cs], True, True, xbc.R(3 + g) + xbc.R(5 + g), pgb)
                        py, pyb = PS(4, 0, 384)

                        def stageA(h):
                            g = h // 3
                            i2 = h % 2
                            eb = 6 if i2 == 0 else 2
                            pE, pEb = PS(eb, 0, 256)
                            adt_b = adt[:, h:h + 1].to_broadcast([128, 128])
                            mm(ps_t[eb][:, 0:128], adt_b, tri_f, True, False, [sm, cf], pEb)
                            mm(ps_t[eb][:, 0:128], ident_b, mneg_b, False, True, [cb], pEb)
                            mm(ps_t[eb][:, 128:256], adt_b, tri_f, True, True, [sm, cf], pEb)
                            act(Dm[i2].ap, ps_t[eb][:, 0:128], AF.Exp, pEb + [sm], [Dm[i2]], bias=nacs[:, h:h + 1])
                            act(ebc[i2].ap, ps_t[eb][:, 128:256], AF.Exp, pEb, [ebc[i2]])
                            tt("vector", MT[i2].ap, ps_t[3][:, g * 128:(g + 1) * 128], Dm[i2].ap, ALU.mult, pgb + [Dm[i2]], [MT[i2]])
                            tt("gpsimd", Cs[i2].ap, xbc.ap[:, 5 + g, cs], ebc[i2].ap, ALU.mult, xbc.R(5 + g) + [ebc[i2]], [Cs[i2]])

                        def stageB(h):
                            i2 = h % 2
                            hp = (h % 2) * 64
                            yo = ps_t[4][hp:hp + 64, (h // 2) * 128:(h // 2) * 128 + 128]
                            mm(yo, xdt.ap[:, h * 64:(h + 1) * 64], MT[i2].ap, True, False, [xdt, MT[i2]], pyb)
                            mm(yo, stb.ap[:, h * 64:(h + 1) * 64], Cs[i2].ap, False, True, [stb, Cs[i2]], pyb)

                        stageA(0)
                        for h in range(6):
                            if h + 1 < 6:
                                stageA(h + 1)
                            stageB(h)
                        for c in range(3):
                            stt(yssd.ap[:, c, cs], xbc.ap[:, c, cs], pv.ap[:, o + DSK + c:o + DSK + c + 1],
                                ps_t[4][:, c * 128:(c + 1) * 128], ALU.mult, ALU.add, xbc.R(c) + [pv] + pyb, [yssd])
                        pst, pstb = PS(5, 0, 384)
                        for g in range(2):
                            mm(ps_t[5][:, g * 192:(g + 1) * 192], btok.ap[:, g * 128:(g + 1) * 128], xdts.ap[:, g * 192:(g + 1) * 192],
                               True, True, [btok, xdts], pstb)
                        tt("vector", stt_tmp.ap.rearrange("p (h e) -> p h e", h=6), st.ap.rearrange("p (h e) -> p h e", h=6),
                           cdec.unsqueeze(2).to_broadcast([128, 6, 64]), ALU.mult, [st, sm], [stt_tmp])
                        tt("vector", st.ap, stt_tmp.ap, pst, ALU.add, [stt_tmp] + pstb, [st])
                        cp("gpsimd", stb.ap, st.ap, [st], [stb])
                    dump("yssd1_pre", yssd.ap[:, 1, :], [yssd])
                    dump("zs1", zs.ap[:, 1, :], [zs])
                    for c in range(3):
                        tt("gpsimd", yssd.ap[:, c, :], yssd.ap[:, c, :], zs.ap[:, c, :], ALU.mult, [yssd, zs], [yssd])
                    rmsnorm([yssd.ap[:, c, :] for c in range(3)], [yssd], o + SNG, 3, 384,
                            [ycat.ap[:, c, :] for c in range(3)], ycat, TB, (6, 256))

                    dump("ycat1_early", ycat.ap[:, 1, :], [ycat])
                    dump("yssd1", yssd.ap[:, 1, :], [yssd])
                    S.enabled = True
                    SC.reset()
                    LT = [{k: SC.get([TB], BF16 if k == "xlb" else F32, f"{k}{c}") for k in ("xl", "xlb", "rg", "ig", "aa", "mu", "hl", "g1", "g2")}
                          for c in range(2)]
                    lraw_s = [Buf() for _ in range(2)]
                    for sb_ in lraw_s:
                        sb_.writer = lraw.b.writer
                        sb_.readers = list(lraw.b.readers)
                    pgl = [None, None]

                    def l1(c):
                        T_ = LT[c]
                        pso, pbs = proj(OLX + c * 128)
                        cp("scalar", lraw.ap[:, c, 3:3 + TB], pso, pbs, [lraw_s[c]])
                        pgl[c] = proj(OLG + c * 128)
                        cp("scalar", T_["g1"].ap, pgl[c][0], pgl[c][1], [T_["g1"]])

                    def l2(c):
                        T_ = LT[c]
                        wc = lambda k: pv.ap[:, o + LCW + c * 4 + k:o + LCW + c * 4 + k + 1]
                        ts("vector", T_["xl"].ap, lraw.ap[:, c, 0:TB], wc(0), pv.ap[:, o + LCB + c:o + LCB + c + 1], ALU.mult, ALU.add,
                           [lraw_s[c], pv], [T_["xl"]])
                        for k in range(1, 4):
                            stt(T_["xl"].ap, lraw.ap[:, c, k:k + TB], wc(k), T_["xl"].ap, ALU.mult, ALU.add, [lraw_s[c], pv, T_["xl"]], [T_["xl"]])
                        cp("gpsimd", T_["xlb"].ap, T_["xl"].ap, [T_["xl"]], [T_["xlb"]])
                        tt("gpsimd", T_["g2"].ap, T_["g1"].ap, T_["g1"].ap, ALU.mult, [T_["g1"]], [T_["g2"]])
                        ts("vector", T_["g2"].ap, T_["g2"].ap, 0.044715, 1.0, ALU.mult, ALU.add, [T_["g2"]], [T_["g2"]])
                        tt("gpsimd", T_["g2"].ap, T_["g2"].ap, T_["g1"].ap, ALU.mult, [T_["g2"], T_["g1"]], [T_["g2"]])

                    def sig3(out_ap, in_ap, r, w, scale, bias=None):
                        act(out_ap, in_ap, AF.Exp, r, w, scale=scale, bias=bias)
                        act(out_ap, out_ap, AF.Ln, w, w, bias=1.0)
                        act(out_ap, out_ap, AF.Exp, w, w, scale=-1.0)

                    def l3(c):
                        T_ = LT[c]
                        pa, pab = PS(2, c * 256, c * 256 + 256)
                        mm(pa, bdm.ap[:, c, :], T_["xlb"].ap, True, True, [bdm, T_["xlb"]], pab)
                        sig3(T_["rg"].ap, pa, pab + [dv], [T_["rg"]], -1.0, dv.ap[:, l, 8 + c:9 + c])
                        mm(pa, bdm.ap[:, 2 + c, :], T_["xlb"].ap, True, True, [bdm, T_["xlb"]], pab)
                        sig3(T_["ig"].ap, pa, pab + [dv], [T_["ig"]], -1.0, dv.ap[:, l, 10 + c:11 + c])
                        sig3(T_["g2"].ap, T_["g2"].ap, [T_["g2"]], [T_["g2"]], -1.5957691216057308)

                    def l4(c):
                        T_ = LT[c]
                        ts("vector", T_["aa"].ap, T_["rg"].ap, dv.ap[:, l, 6 + c:7 + c], None, ALU.mult, None, [T_["rg"], dv], [T_["aa"]])
                        act(T_["aa"].ap, T_["aa"].ap, AF.Exp, [T_["aa"]], [T_["aa"]])
                        tt("gpsimd", T_["ig"].ap, T_["ig"].ap, T_["xl"].ap, ALU.mult, [T_["ig"], T_["xl"]], [T_["ig"]])
                        tt("gpsimd", T_["g2"].ap, T_["g2"].ap, T_["g1"].ap, ALU.mult, [T_["g2"], T_["g1"]], [T_["g2"]])

                    def l5(c):
                        T_ = LT[c]
                        tt("gpsimd", T_["mu"].ap, T_["aa"].ap, T_["aa"].ap, ALU.mult, [T_["aa"]], [T_["mu"]])
                        act(T_["mu"].ap, T_["mu"].ap, AF.Ln, [T_["mu"]], [T_["mu"]], bias=1.0, scale=-1.0)
                        act(T_["mu"].ap, T_["mu"].ap, AF.Exp, [T_["mu"]], [T_["mu"]], scale=0.5)

                    def l6(c):
                        T_ = LT[c]
                        tt("vector", T_["mu"].ap, T_["mu"].ap, T_["ig"].ap, ALU.mult, [T_["mu"], T_["ig"]], [T_["mu"]])
                        scan(T_["hl"].ap, T_["aa"].ap, T_["mu"].ap, lcar.ap[:, c:c + 1], [T_["aa"], T_["mu"], lcar], [T_["hl"]])
                        cp("vector", lcar.ap[:, c:c + 1], T_["hl"].ap[:, TB - 1:TB], [T_["hl"]], [lcar])
                        tt("vector", T_["hl"].ap, T_["hl"].ap, T_["g2"].ap, ALU.mult, [T_["hl"], T_["g2"]], [T_["hl"]])

                    def lru_gen():
                        for st_fn in (l1, l2, l3, l4, l5, l6):
                            for c in range(2):
                                st_fn(c)
                                yield
                            if st_fn is l2:
                                cp("gpsimd", lraw.ap[:, :, 0:3], lraw.ap[:, :, TB:TB + 3], lraw_s, lraw_s + [lraw])
                        hl_aps = [LT[c]["hl"].ap for c in range(2)]
                        hl_bufs = [LT[c]["hl"] for c in range(2)]
                        rmsnorm(hl_aps, hl_bufs, o + LNG, 2, 256,
                                [ycat.ap[:, 3 + c, :] for c in range(2)], ycat, TB, (2, 256))
                        yield

                    lgen = lru_gen()

                    qT = SC.get([3, TB], BF16, "qT")
                    pT = [SC.get([TB], BF16, f"pT{i}") for i in range(4)]
                    lnd = SC.get([TB], F32, "lnd")
                    yfox = SC.get([3, TB], F32, "yfox")
                    for c in range(3):
                        pso, pbs = proj(OQ + c * 128)
                        cp("scalar", qT.ap[:, c, :], pso, pbs, [qT])
                    for c in range(3):
                        pso, pbs = proj(OK_ + c * 128)
                        cp("scalar", KT.ap[:, c, t0:t0 + TB], pso, pbs, [KT])
                    nkb = (t0 + TB) // 128
                    nb = SC.get([NT128, 6], F32, "nb")
                    tt("vector", nb.ap[:, 0:nkb, :], ncum.ap[:, 0:nkb, :], fref.ap.unsqueeze(1).to_broadcast([128, nkb, 6]), ALU.subtract,
                       [ncum, fref], [nb])
                    pi = 0
                    for c in range(3):
                        py, pyb = PS(5, 0, 256)
                        pdn, pdb = PS(4, 0, 256)
                        its = [(hh, kb) for hh in range(2) for kb in range(nkb)]

                        def stA(idx, it):
                            hh, kb = it
                            h = 2 * c + hh
                            hp = hh * 64
                            rel = kb * 128 - t0
                            q0 = 0 if rel < 0 else rel
                            sbank = 6 if idx % 2 == 0 else 3
                            psS, psSb = PS(sbank, 0, 256)
                            so = ps_t[sbank][:, q0:TB]
                            diag = rel >= 0
                            mm(so, KT.ap[hp:hp + 64, c, kb * 128:(kb + 1) * 128], qT.ap[hp:hp + 64, c, q0:TB], True, not diag,
                               [KT, qT], psSb)
                            if diag:
                                mm(ps_t[sbank][:, q0:q0 + 128], ident_b, mneg_b, False, True, [cb], psSb)
                            pt = pT[idx % 4]
                            if q0 > 0:
                                mset("gpsimd", pt.ap[:, 0:q0], 0.0, [pt])
                            act(pt.ap[:, q0:TB], so, AF.Exp, psSb + [nb], [pt], bias=nb.ap[:, kb, h:h + 1], scale=0.125)

                        def stB(idx, it):
                            hh, kb = it
                            h = 2 * c + hh
                            hp = hh * 64
                            pt = pT[idx % 4]
                            first, last = kb == 0, kb == nkb - 1
                            mm(ps_t[5][hp:hp + 64, 0:TB], V.ap[:, kb, h * 64:(h + 1) * 64], pt.ap[:, 0:TB], first, last, [V, pt], pyb)
                            mm(ps_t[4][hp:hp + 64, 0:TB], ones_b[:, 0:64], pt.ap[:, 0:TB], first, last, [cb, pt], pdb)

                        stA(pi, its[0])
                        for i_, it in enumerate(its):
                            if i_ + 1 < len(its):
                                stA(pi + i_ + 1, its[i_ + 1])
                            stB(pi + i_, it)
                            next(lgen, None)
                        pi += len(its)
                        act(lnd.ap, pdn, AF.Ln, pdb, [lnd])
                        act(lnd.ap, lnd.ap, AF.Exp, [lnd], [lnd], scale=-1.0)
                        tt("vector", yfox.ap[:, c, :], py, lnd.ap, ALU.mult, pyb + [lnd], [yfox])
                    for _ in lgen:
                        pass
                    dump("qT0", qT.ap[:, 0, :], [qT])
                    dump("KT0", KT.ap[:, 0, 0:TB], [KT])
                    dump("nc0", ncum.ap[:, 0, :], [ncum])
                    dump("nc1", ncum.ap[:, 1, :], [ncum])
                    dump("V0", V.ap[:, 0, :], [V])
                    dump("V1", V.ap[:, 1, :], [V])
                    dump("yfox0", yfox.ap[:, 0, :], [yfox])
                    rmsnorm([yfox.ap[:, c, :] for c in range(3)], [yfox], o + FNG, 3, 384,
                            [ycat.ap[:, 5 + c, :] for c in range(3)], ycat, TB, (6, 256))

                    S.enabled = True
                    if blk + 1 < NBLK:
                        t1 = (blk + 1) * TB
                        rmsnorm([hT[:, c, t1:t1 + TB] for c in range(8)], [hTb[blk + 1]], o + G1, 8, D_MODEL,
                                [uT.ap[:, c, :] for c in range(8)], uT, TB, (6, 256))
                    for c in range(8):
                        dump(f"ycat{c}", ycat.ap[:, c, :], [ycat])
                    for dc in range(8):
                        bank, half = dc % 2, (dc // 2) % 2
                        pso, pbs = PS(bank, half * 256, half * 256 + 256)
                        for c in range(8):
                            mm(pso, wout.ap[:, c, dc * 128:(dc + 1) * 128], ycat.ap[:, c, :], c == 0, c == 7, wout.R(c) + [ycat], pbs)
                        tt("vector", hsl[dc], hsl[dc], pso, ALU.add, [hb] + pbs, [hb])

                if KSTOP <= 3:
                    S.dead = True
                FFN_B.reset()
                uTs = FFN_B.get([8, NT], BF16, "uTs")
                wgt = [FFN_B.get([8, 512], BF16, f"wg{i}") for i in range(2)]
                wut = [FFN_B.get([8, 512], BF16, f"wu{i}") for i in range(2)]
                wdt = [FFN_B.get([4, D_MODEL], BF16, f"wd{i}") for i in range(2)]
                actb = FFN_B.get([4, 512], BF16, "actb")
                sgt = [FFN_B.get([512], F32, f"sg{i}") for i in range(2)]
                assert FFN_B.off <= PLE_BASE, (FFN_B.off, PLE_BASE)
                PLE_B.reset()
                wpg = PLE_B.get([8, D_MODEL], BF16, "wpg")
                wpp = PLE_B.get([2, D_MODEL], BF16, "wpp")
                pTt = PLE_B.get([2, TB], BF16, "pTt")
                pTt2 = PLE_B.get([2, TB], BF16, "pTt2")
                pst_ = [PLE_B.get([256], F32, "pst0")]
                gat = [PLE_B.get([TB], F32, f"gat{i}") for i in range(2)]
                wpg.split(8)
                wpp.split(2)
                for c in range(8):
                    dma("gpsimd", wpg.ap[:, c, :], wpg_d[l, c * 128:(c + 1) * 128, :], [], [wpg.subs[c]], cast=True)
                for c in range(2):
                    dma("gpsimd", wpp.ap[:, c, :], wpp_d[l, c * 128:(c + 1) * 128, :], [], [wpp.subs[c]], cast=True)
                S.enabled = "ffn" in DBG
                uTs.split(NT // 512)

                def ffn_norm(tb4_):
                    for blk in (2 * tb4_, 2 * tb4_ + 1):
                        t0 = blk * TB
                        rmsnorm([hT[:, c, t0:t0 + TB] for c in range(8)], [hTb[blk]], o + G2, 8, D_MODEL,
                                [uTs.ap[:, c, t0:t0 + TB] for c in range(8)], Tl(None, uTs.subs[tb4_]), TB, (6, 256), sq_eng="scalar")
                groups = [(g * 512, 512) for g in range(5)] + [(2560, 256)]
                NT512 = NT // 512
                for gi, (f0, fw_) in enumerate(groups):
                    wg_, wu_, wd_ = wgt[gi % 2], wut[gi % 2], wdt[gi % 2]
                    nj = fw_ // 128
                    wg_.split(8)
                    wu_.split(8)
                    wd_.split(4)
                    wg_.b.readers = []
                    wu_.b.readers = []
                    wd_.b.readers = []
                    for c in range(8):
                        dma("gpsimd", wg_.ap[:, c, 0:fw_], wg_d[l, c * 128:(c + 1) * 128, f0:f0 + fw_], [], [wg_.subs[c]], cast=True)
                    for c in range(8):
                        dma("gpsimd", wu_.ap[:, c, 0:fw_], wu_d[l, c * 128:(c + 1) * 128, f0:f0 + fw_], [], [wu_.subs[c]], cast=True)
                    for j in range(nj):
                        dma("gpsimd", wd_.ap[:, j, :], wd_d[l, f0 + j * 128:f0 + (j + 1) * 128, :], [], [wd_.subs[j]], cast=True)
                    for tb4 in range(NT512):
                        tsl = slice(tb4 * 512, (tb4 + 1) * 512)
                        if gi == 0:
                            if tb4 == 0:
                                ffn_norm(0)
                            if tb4 + 1 < NT512:
                                ffn_norm(tb4 + 1)
                        for j in range(nj):
                            pg_, pgb_ = PS(j % 2, 0, 512)
                            pu_, pub_ = PS(2 + j % 2, 0, 512)
                            for c in range(8):
                                mm(pg_, wg_.ap[:, c, j * 128:(j + 1) * 128], uTs.ap[:, c, tsl], c == 0, c == 7, wg_.R(c) + uTs.R(tb4), pgb_)
                            for c in range(8):
                                mm(pu_, wu_.ap[:, c, j * 128:(j + 1) * 128], uTs.ap[:, c, tsl], c == 0, c == 7, wu_.R(c) + uTs.R(tb4), pub_)
                            sg_ = sgt[j % 2]
                            act(sg_.ap, pg_, AF.Silu, pgb_, [sg_])
                            tt("vector", actb.ap[:, j, :], sg_.ap, pu_, ALU.mult, [sg_] + pub_, [actb])
                        for dc in range(8):
                            pd_, pdb_ = PS(4 + dc % 2, 0, 512)
                            for j in range(nj):
                                mm(pd_, wd_.ap[:, j, dc * 128:(dc + 1) * 128], actb.ap[:, j, :], j == 0, j == nj - 1, wd_.R(j) + [actb], pdb_)
                            hbs = [hTb[2 * tb4], hTb[2 * tb4 + 1]]
                            tt("vector", hT[:, dc, tsl], hT[:, dc, tsl], pd_, ALU.add, hbs + pdb_, hbs)

                S.enabled = True
                nxt = None
                if l + 1 < DEPTH:
                    nxt = l + 1
                elif s + 1 < NSEQ:
                    nxt = 0
                if nxt is not None:
                    PREFETCHED[0] = load_mixer_weights(nxt)
                S.enabled = "ple" in DBG
                pTts = [pTt, pTt2]

                def ple_pre(blk):
                    t0 = blk * TB
                    hb = hTb[blk]
                    hsl = [hT[:, c, t0:t0 + TB] for c in range(8)]
                    ub = uT if blk % 2 == 0 else ycat
                    rmsnorm(hsl, [hb], o + G3, 8, D_MODEL, [ub.ap[:, c, :] for c in range(8)], ub, TB, (6, 256), sq_eng="scalar")
                    pTb = pTts[blk % 2]
                    for j in range(TB // 128):
                        ps_ = pst_[0]
                        dma("sync", ps_.ap, p_d[l, s, t0 + j * 128:t0 + (j + 1) * 128, :], [], [ps_])
                        pt_, ptb = PS(4, 0, 256)
                        for c in range(2):
                            tr(ps_t[4][:, c * 128:(c + 1) * 128], ps_.ap[:, c * 128:(c + 1) * 128], ident_f, [ps_, cf], ptb)
                        cp("scalar", pTb.ap[:, :, j * 128:(j + 1) * 128], pt_.rearrange("p (a b) -> p a b", a=2), ptb, [pTb])

                def ple_main(blk):
                    t0 = blk * TB
                    hb = hTb[blk]
                    hsl = [hT[:, c, t0:t0 + TB] for c in range(8)]
                    ub = uT if blk % 2 == 0 else ycat
                    pTb = pTts[blk % 2]
                    for dc in range(8):
                        pg_, pgb_ = PS(dc % 2, 0, 256)
                        pp_, ppb_ = PS(2 + dc % 2, 0, 256)
                        for c in range(8):
                            mm(pg_, wpg.ap[:, c, dc * 128:(dc + 1) * 128], ub.ap[:, c, :], c == 0, c == 7, wpg.R(c) + [ub], pgb_)
                        for c in range(2):
                            mm(pp_, wpp.ap[:, c, dc * 128:(dc + 1) * 128], pTb.ap[:, c, :], c == 0, c == 1, wpp.R(c) + [pTb], ppb_)
                        ga = gat[dc % 2]
                        act(ga.ap, pg_, AF.Sigmoid, pgb_ + [pv], [ga], bias=pv.ap[:, o + BPG + dc:o + BPG + dc + 1])
                        tt("vector", ga.ap, ga.ap, pp_, ALU.mult, [ga] + ppb_, [ga])
                        tt("vector", hsl[dc], hsl[dc], ga.ap, ALU.add, [hb, ga], [hb])

                if "ple" in DBG:
                    ple_pre(0)
                    for blk in range(NBLK):
                        if blk + 1 < NBLK:
                            ple_pre(blk + 1)
                        ple_main(blk)

            S.enabled = True
            if KSTOP <= 4:
                S.dead = True
            SC.reset()
            onT = SC.get([8, TB], F32, "onT")
            ost = [uT_f, ycat_f]
            for blk in range(NBLK):
                t0 = blk * TB
                rmsnorm([hT[:, c, t0:t0 + TB] for c in range(8)], [hTb[blk]], GF, 8, D_MODEL,
                        [onT.ap[:, c, :] for c in range(8)], onT, TB, (6, 256))
                for j in range(TB // 128):
                    if KFIN < 2:
                        break
                    os_ = ost[j % 2]
                    for half in range(2):
                        bank = half
                        pso, pbs = PS(bank, 0, 512)
                        for q in range(4):
                            c = half * 4 + q
                            tr(ps_t[bank][:, q * 128:(q + 1) * 128], onT.ap[:, c, j * 128:(j + 1) * 128], ident_f, [onT, cf], pbs)
                        cp("vector" if half == 0 else "scalar", os_.ap[:, half * 512:(half + 1) * 512], pso, pbs, [os_])
                    if KFIN >= 3:
                        dma("sync", y_d[s, t0 + j * 128:t0 + (j + 1) * 128, :], os_.ap, [os_], [], is_out=True)
        if KDUMP:
            S.enabled = True
            S.dead = False
            dma("sync", dbg_d, dbgt.ap, [dbgt], [], is_out=True)
        S.emit()
    return nc


PREFETCHED = [None]
import os
DBG = set(os.environ.get("KDBG", "ssd,lru,fox,ffn,ple").split(","))
KSTOP = int(os.environ.get("KSTOP", "99"))
KFIN = int(os.environ.get("KFIN", "3"))
KPLE = int(os.environ.get("KPLE", "3"))
KDUMP = int(os.environ.get("KDUMP", "0"))
DUMPS = {}

def _consts():
    ident = np.eye(128, dtype=np.float32)
    tri = np.triu(np.ones((128, 128), np.float32))
    ones = np.ones((128, 128), np.float32)
    mneg = np.where(np.arange(128)[:, None] > np.arange(128)[None, :], np.float32(-30000.0), np.float32(0.0)).astype(np.float32)
    cf = np.concatenate([ident, tri, ones], axis=1)
    cb = np.concatenate([ident, ones, mneg], axis=1)
    sel = np.zeros((6, 6, 128), np.float32)
    for h in range(6):
        sel[h, h, :] = 1.0
    return cf, cb, sel.reshape(6, 768)


def _pcol(v, nch):
    return np.ascontiguousarray(np.asarray(v, np.float32).reshape(nch, 128).T)


def _pack(inp, DEPTH):
    f = lambda k: np.asarray(inp[k], np.float32)
    pvs = []
    w_in_r, bds = [], []
    for l in range(DEPTH):
        cols = np.zeros((128, NPV), np.float32)
        cols[:, G1:G1 + 8] = _pcol(f("norm1_g")[l], 8)
        cols[:, G2:G2 + 8] = _pcol(f("norm2_g")[l], 8)
        cols[:, G3:G3 + 8] = _pcol(f("norm3_g")[l], 8)
        scw = f("ssd_conv_w")[l]
        for c in range(7):
            for k in range(4):
                cols[:, SCW + c * 4 + k] = scw[k, c * 128:(c + 1) * 128]
        cols[:, SCB:SCB + 7] = _pcol(f("ssd_conv_b")[l], 7)
        cols[:, DSK:DSK + 3] = _pcol(np.repeat(f("ssd_d")[l], 64), 3)
        cols[:, SNG:SNG + 3] = _pcol(f("ssd_norm_g")[l], 3)
        lcw = f("lru_conv_w")[l]
        for c in range(2):
            for k in range(4):
                cols[:, LCW + c * 4 + k] = lcw[k, c * 128:(c + 1) * 128]
        cols[:, LCB:LCB + 2] = _pcol(f("lru_conv_b")[l], 2)
        cols[:, LBA:LBA + 2] = _pcol(f("lru_b_a")[l], 2)
        cols[:, LBX:LBX + 2] = _pcol(f("lru_b_x")[l], 2)
        cols[:, LAM:LAM + 2] = _pcol(f("lru_lambda")[l], 2)
        cols[:, LNG:LNG + 2] = _pcol(f("lru_norm_g")[l], 2)
        cols[:, FNG:FNG + 3] = _pcol(f("fox_norm_g")[l], 3)
        cols[:, BPG:BPG + 8] = _pcol(f("b_ple_gate")[l], 8)
        cols[:, DTB:DTB + 6] = np.broadcast_to(f("ssd_dt_bias")[l][None, :], (128, 6))
        cols[:, ALOG:ALOG + 6] = np.broadcast_to(f("ssd_a_log")[l][None, :], (128, 6))
        cols[:, FBF:FBF + 6] = np.broadcast_to(f("fox_b_f")[l][None, :], (128, 6))
        pvs.append(cols)
        w = f("w_in")[l]
        z, xbc, dt, lx, lg, q, k, v, fr = np.split(w, np.cumsum([384, 896, 6, 256, 256, 384, 384, 384])[:], axis=1)
        w_in_r.append(np.concatenate([z, xbc, lx, lg, q, k, v, dt, fr], axis=1))
        bd = np.zeros((128, 4, 128), np.float32)
        wa, wx = f("lru_w_a")[l], f("lru_w_x")[l]
        for c in range(2):
            for i in range(2):
                bd[i * 64:(i + 1) * 64, c, i * 64:(i + 1) * 64] = wa[2 * c + i]
                bd[i * 64:(i + 1) * 64, 2 + c, i * 64:(i + 1) * 64] = wx[2 * c + i]
        bds.append(bd.reshape(128, 512))
    pv = np.concatenate(pvs + [_pcol(f("final_norm_g"), 8)], axis=1)
    cf, cb, sel = _consts()
    shared = {
        "w_in": np.ascontiguousarray(np.stack(w_in_r)), "w_out": f("w_out")[:DEPTH], "bd": np.stack(bds),
        "w_gate": f("w_gate")[:DEPTH], "w_up": f("w_up")[:DEPTH], "w_down": f("w_down")[:DEPTH],
        "w_pg": f("w_ple_gate")[:DEPTH], "w_pp": f("w_ple_proj")[:DEPTH],
        "pv": np.ascontiguousarray(pv), "cf": cf, "cb": cb, "sel": sel,
    }
    return shared


_NC_CACHE = {}


def run(inp, NCORES, DEPTH):
    x = np.asarray(inp["x"], np.float32)
    p = np.asarray(inp["p"], np.float32)
    B, NT, _ = x.shape
    NSEQ = B // NCORES
    key = (NSEQ, NT, DEPTH)
    if key not in _NC_CACHE:
        PREFETCHED[0] = None
        _NC_CACHE[key] = build(NSEQ, NT, DEPTH)
    nc = _NC_CACHE[key]
    shared = _pack(inp, DEPTH)
    in_maps = []
    for c in range(NCORES):
        m = dict(shared)
        m["x"] = np.ascontiguousarray(x[c * NSEQ:(c + 1) * NSEQ])
        m["p"] = np.ascontiguousarray(p[:DEPTH, c * NSEQ:(c + 1) * NSEQ])
        in_maps.append(m)
    res = run_bass_kernel_spmd(nc, in_maps, core_ids=list(range(NCORES)))
    if KDUMP:
        global LAST_DBG
        LAST_DBG = res.results[0]["dbg"]
    return np.concatenate([r["y"] for r in res.results], axis=0)


def kernel(**inputs):
    return run(inputs, 8, 2)
```
